# Optimizing a Trainium2 kernel written in Bass

```python
import math
import jax, jax.numpy as jnp
from jax import lax
import numpy as np

D_MODEL = 1024
BATCH = 16
SEQ = 2048
DEPTH = 4

N_A_LAYERS = DEPTH // 2
N_B_LAYERS = DEPTH - N_A_LAYERS
RET_HEADS = 4
RET_QK_DIM = D_MODEL // RET_HEADS
RET_V_DIM = 2 * RET_QK_DIM
RET_CHUNK = 128
RET_THETA = 10000.0
RET_IN_WIDTH = 2 * RET_HEADS * RET_QK_DIM + 2 * RET_HEADS * RET_V_DIM
DIFF_HEADS = 8
DIFF_HEAD_DIM = D_MODEL // (2 * DIFF_HEADS)
DIFF_V_DIM = 2 * DIFF_HEAD_DIM
ROPE_THETA = 500000.0
ROPE_DIM = DIFF_HEAD_DIM // 4
Q_BLOCK = 128
PEER_HEADS = 8
PEER_N_KEYS = 128
PEER_N_EXPERTS = PEER_N_KEYS * PEER_N_KEYS
PEER_KEY_DIM = 256
PEER_TOPK = 16
PEER_CHUNK = 128
DN_ALPHA = (2 * DEPTH) ** 0.25
DN_BETA = (8 * DEPTH) ** -0.25
LN_EPS = 1e-5

kernel_name = "yoco_retnet_diffattn_peer"


def layer_norm(x, g, b):
    xf = x.astype(jnp.float32)
    mu = jnp.mean(xf, axis=-1, keepdims=True)
    var = jnp.mean(jnp.square(xf - mu), axis=-1, keepdims=True)
    y = (xf - mu) * lax.rsqrt(var + LN_EPS)
    return (y * g.astype(jnp.float32) + b.astype(jnp.float32)).astype(x.dtype)


def rope(x, angles):
    r2 = angles.shape[-1]
    r = 2 * r2
    shape = (1, angles.shape[0]) + (1,) * (x.ndim - 3) + (r2,)
    cos = jnp.cos(angles).reshape(shape)
    sin = jnp.sin(angles).reshape(shape)
    xf = x.astype(jnp.float32)
    x1 = xf[..., :r2]
    x2 = xf[..., r2:r]
    out = jnp.concatenate([x1 * cos - x2 * sin, x1 * sin + x2 * cos, xf[..., r:]], axis=-1)
    return out.astype(x.dtype)


def retention(x, w_in, w_out, angles):
    b, s, _ = x.shape
    nc = s // RET_CHUNK
    hq = RET_HEADS * RET_QK_DIM
    hv = RET_HEADS * RET_V_DIM
    proj = x @ w_in
    q = proj[..., :hq].reshape(b, s, RET_HEADS, RET_QK_DIM)
    k = proj[..., hq:2 * hq].reshape(b, s, RET_HEADS, RET_QK_DIM)
    v = proj[..., 2 * hq:2 * hq + hv].reshape(b, s, RET_HEADS, RET_V_DIM)
    g = proj[..., 2 * hq + hv:]
    q = rope(q, angles)
    k = rope(k, angles) * (RET_QK_DIM ** -0.5)

    def to_chunks(t):
        return t.astype(jnp.float32).reshape(b, nc, RET_CHUNK, RET_HEADS, -1).transpose(1, 0, 3, 2, 4)

    qc, kc, vc = to_chunks(q), to_chunks(k), to_chunks(v)
    log_g = jnp.log(1.0 - jnp.exp2(-5.0 - jnp.arange(RET_HEADS, dtype=jnp.float32)))
    ar = jnp.arange(RET_CHUNK, dtype=jnp.float32)
    rel = ar[:, None] - ar[None, :]
    decay_mask = jnp.where(rel[None] >= 0, jnp.exp(jnp.maximum(rel, 0.0)[None] * log_g[:, None, None]), 0.0)
    q_decay = jnp.exp((ar + 1.0)[None] * log_g[:, None])[None, :, :, None]
    k_decay = jnp.exp((RET_CHUNK - 1.0 - ar)[None] * log_g[:, None])[None, :, :, None]
    chunk_decay = jnp.exp(RET_CHUNK * log_g)[None, :, None, None]

    def step(state, inp):
        qi, ki, vi = inp
        sc = jnp.einsum('bhid,bhjd->bhij', qi, ki) * decay_mask
        intra = jnp.einsum('bhij,bhje->bhie', sc, vi)
        inter = jnp.einsum('bhid,bhde->bhie', qi, state) * q_decay
        state = state * chunk_decay + jnp.einsum('bhjd,bhje->bhde', ki * k_decay, vi)
        return state, intra + inter

    state0 = jnp.zeros((b, RET_HEADS, RET_QK_DIM, RET_V_DIM), jnp.float32)
    _, y = lax.scan(step, state0, (qc, kc, vc))
    y = y.transpose(1, 0, 3, 2, 4).reshape(b, s, RET_HEADS, RET_V_DIM)
    mu = jnp.mean(y, axis=-1, keepdims=True)
    var = jnp.mean(jnp.square(y - mu), axis=-1, keepdims=True)
    y = ((y - mu) * lax.rsqrt(var + LN_EPS)).reshape(b, s, hv)
    return (jax.nn.silu(g) * y.astype(x.dtype)) @ w_out


def diff_attention(x, w_q, lam_p, subln_g, w_out, k_sh, v_sh, layer_idx, angles):
    b, s, _ = x.shape
    q = (x @ w_q).reshape(b, s, DIFF_HEADS, 2, DIFF_HEAD_DIM)
    q = rope(q, angles) * (DIFF_HEAD_DIM ** -0.5)
    lam_init = 0.8 - 0.6 * math.exp(-0.3 * layer_idx)
    lp = lam_p.astype(jnp.float32)
    lam = jnp.exp(jnp.sum(lp[0] * lp[1])) - jnp.exp(jnp.sum(lp[2] * lp[3])) + lam_init
    outs = []
    for i in range(s // Q_BLOCK):
        s0 = i * Q_BLOCK
        end = s0 + Q_BLOCK
        qb = q[:, s0:end]
        kb = k_sh[:, :end]
        vb = v_sh[:, :end]
        sc = jnp.einsum('bqhmd,bkhmd->bhmqk', qb, kb).astype(jnp.float32)
        mask = jnp.arange(end)[None, :] <= (s0 + jnp.arange(Q_BLOCK))[:, None]
        p = jax.nn.softmax(jnp.where(mask, sc, -jnp.inf), axis=-1)
        a = p[:, :, 0] - lam * p[:, :, 1]
        outs.append(jnp.einsum('bhqk,bkhe->bqhe', a.astype(vb.dtype), vb))
    o = jnp.concatenate(outs, axis=1).astype(jnp.float32)
    o = o * lax.rsqrt(jnp.mean(jnp.square(o), axis=-1, keepdims=True) + LN_EPS)
    o = o * subln_g.astype(jnp.float32) * (1.0 - lam_init)
    return o.reshape(b, s, DIFF_HEADS * DIFF_V_DIM).astype(x.dtype) @ w_out


def peer_ffn(x, w_q, subkeys, u, v):
    b, s, d = x.shape
    xs = x.reshape(b * s // PEER_CHUNK, PEER_CHUNK, d)
    half = PEER_KEY_DIM // 2
    ncand = PEER_TOPK * PEER_TOPK

    def block(xc):
        q = (xc @ w_q).reshape(PEER_CHUNK, PEER_HEADS, 2, half)
        sc = jnp.einsum('thpd,hpnd->thpn', q, subkeys).astype(jnp.float32)
        s1, i1 = lax.top_k(sc[:, :, 0], PEER_TOPK)
        s2, i2 = lax.top_k(sc[:, :, 1], PEER_TOPK)
        cand = (s1[..., :, None] + s2[..., None, :]).reshape(PEER_CHUNK, PEER_HEADS, ncand)
        cidx = (i1[..., :, None] * PEER_N_KEYS + i2[..., None, :]).reshape(PEER_CHUNK, PEER_HEADS, ncand)
        top, pos = lax.top_k(cand, PEER_TOPK)
        idx = jnp.take_along_axis(cidx, pos, axis=-1).reshape(PEER_CHUNK, PEER_HEADS * PEER_TOPK)
        gate = jax.nn.softmax(top, axis=-1).reshape(PEER_CHUNK, PEER_HEADS * PEER_TOPK)
        u_e = jnp.take(u, idx, axis=0)
        v_e = jnp.take(v, idx, axis=0)
        act = jax.nn.gelu(jnp.einsum('ted,td->te', u_e, xc).astype(jnp.float32))
        return jnp.einsum('te,ted->td', (gate * act).astype(v_e.dtype), v_e)

    return lax.map(block, xs).reshape(b, s, d).astype(x.dtype)


def setup_inputs(seed: int = 0) -> dict:
    key = jax.random.key(seed)
    ks = jax.random.split(key, 14)
    f32 = jnp.float32
    d = D_MODEL
    nrm = lambda k, shape, scale: jax.random.normal(k, shape, f32) * scale
    return {
        "x": nrm(ks[0], (BATCH, SEQ, d), 1.0),
        "ret_w_in": nrm(ks[1], (N_A_LAYERS, d, RET_IN_WIDTH), d ** -0.5),
        "ret_w_out": nrm(ks[2], (N_A_LAYERS, RET_HEADS * RET_V_DIM, d), DN_BETA * (RET_HEADS * RET_V_DIM) ** -0.5),
        "kv_w": nrm(ks[3], (d, DIFF_HEADS * 2 * DIFF_HEAD_DIM + DIFF_HEADS * DIFF_V_DIM), d ** -0.5),
        "diff_w_q": nrm(ks[4], (N_B_LAYERS, d, DIFF_HEADS * 2 * DIFF_HEAD_DIM), d ** -0.5),
        "diff_lambda": nrm(ks[5], (N_B_LAYERS, 4, DIFF_HEAD_DIM), 0.1),
        "diff_subln_g": 1.0 + nrm(ks[6], (N_B_LAYERS, DIFF_V_DIM), 0.02),
        "diff_w_out": nrm(ks[7], (N_B_LAYERS, DIFF_HEADS * DIFF_V_DIM, d), DN_BETA * (DIFF_HEADS * DIFF_V_DIM) ** -0.5),
        "peer_w_q": nrm(ks[8], (DEPTH, d, PEER_HEADS * PEER_KEY_DIM), d ** -0.5),
        "peer_subkeys": nrm(ks[9], (DEPTH, PEER_HEADS, 2, PEER_N_KEYS, PEER_KEY_DIM // 2), (PEER_KEY_DIM // 2) ** -0.5),
        "peer_u": nrm(ks[10], (DEPTH, PEER_N_EXPERTS, d), d ** -0.5),
        "peer_v": nrm(ks[11], (DEPTH, PEER_N_EXPERTS, d), DN_BETA * PEER_HEADS ** -0.5),
        "ln_g": 1.0 + nrm(ks[12], (DEPTH, 2, d), 0.02),
        "ln_b": nrm(ks[13], (DEPTH, 2, d), 0.02),
    }


def reference(x, ret_w_in, ret_w_out, kv_w, diff_w_q, diff_lambda, diff_subln_g, diff_w_out,
              peer_w_q, peer_subkeys, peer_u, peer_v, ln_g, ln_b):
    b, s, _ = x.shape
    pos = jnp.arange(s, dtype=jnp.float32)
    ret_freqs = 1.0 / (RET_THETA ** jnp.linspace(0.0, 1.0, RET_QK_DIM // 2, dtype=jnp.float32))
    ret_angles = pos[:, None] * ret_freqs[None, :]
    diff_freqs = ROPE_THETA ** (-jnp.arange(0, ROPE_DIM, 2, dtype=jnp.float32) / ROPE_DIM)
    diff_angles = pos[:, None] * diff_freqs[None, :]
    kw = DIFF_HEADS * 2 * DIFF_HEAD_DIM
    k_sh = None
    v_sh = None
    for l in range(DEPTH):
        if l < N_A_LAYERS:
            mix = retention(x, ret_w_in[l], ret_w_out[l], ret_angles)
        else:
            j = l - N_A_LAYERS
            mix = diff_attention(x, diff_w_q[j], diff_lambda[j], diff_subln_g[j], diff_w_out[j],
                                 k_sh, v_sh, l, diff_angles)
        x = layer_norm(DN_ALPHA * x + mix, ln_g[l, 0], ln_b[l, 0])
        x = layer_norm(DN_ALPHA * x + peer_ffn(x, peer_w_q[l], peer_subkeys[l], peer_u[l], peer_v[l]),
                       ln_g[l, 1], ln_b[l, 1])
        if l == N_A_LAYERS - 1:
            kv = x @ kv_w
            k_sh = rope(kv[..., :kw].reshape(b, s, DIFF_HEADS, 2, DIFF_HEAD_DIM), diff_angles)
            v_sh = kv[..., kw:].reshape(b, s, DIFF_HEADS, DIFF_V_DIM)
    return x
```

```python
import math
from contextlib import ExitStack

import numpy as np
import ml_dtypes
import concourse.bass as bass
import concourse.mybir as mybir
from concourse.bass_utils import run_bass_kernel_spmd

F32 = mybir.dt.float32
BF16 = mybir.dt.bfloat16
I32 = mybir.dt.int32
U32 = mybir.dt.uint32
AF = mybir.ActivationFunctionType
ALU = mybir.AluOpType
AX = mybir.AxisListType

D = 1024
SEQ = 2048
DEPTH = 4
ALPHA = (2 * DEPTH) ** 0.25
EPS = 1e-5
NEXP = 16384
N_CORES = 8


class Ctr:
    def __init__(self, sem, step):
        self.sem = sem
        self.step = step
        self.count = 0


class Buf:
    def __init__(self, name=""):
        self.name = name
        self.w = None
        self.r = {}


class T:
    def __init__(self, t, name):
        self.t = t
        self.b = Buf(name)

    def __getitem__(self, k):
        return self.t[k]


def _b(x):
    return x.b if hasattr(x, "b") else x


class Sched:
    def __init__(self, nc, es, n_dma_sems=(12, 4, 12)):
        self.nc = nc
        self.engs = {"pe": nc.tensor, "dve": nc.vector, "act": nc.scalar, "pool": nc.gpsimd, "sp": nc.sync}
        self.ctr = {}
        for k in ("pe", "dve", "act", "pool"):
            self.ctr[k] = Ctr(es.enter_context(nc.semaphore("c_" + k)), 1)
        self.dq = {}
        for k, n in zip(("sp", "act", "pool"), n_dma_sems):
            self.dq[k] = [Ctr(es.enter_context(nc.semaphore(f"d_{k}{i}")), 16) for i in range(n)]
        self.dq_i = {"sp": 0, "act": 0, "pool": 0}
        self.bar_sem = es.enter_context(nc.semaphore("bar"))
        self.bar_n = 0
        self.waited = {k: {} for k in ("pe", "dve", "act", "pool", "sp")}
        self.n_ins = 0

    def _wait(self, ek, deps):
        best = {}
        for c, v in deps:
            if v > best.get(c, 0):
                best[c] = v
        for c, v in best.items():
            if self.waited[ek].get(c, 0) >= v:
                continue
            self.engs[ek].wait_ge(c.sem, v)
            self.waited[ek][c] = v
            self.n_ins += 1

    def _deps(self, r, w, skip=None):
        deps = []
        for b in r:
            if b.w is not None:
                deps.append(b.w)
        for b in w:
            if b.w is not None:
                deps.append(b.w)
            for c, v in b.r.items():
                deps.append((c, v))
        if skip is not None:
            deps = [(c, v) for c, v in deps if c is not skip]
        return deps

    def _mark(self, r, w, c, v):
        for b in w:
            b.w = (c, v)
            b.r = {}
        for b in r:
            if b.r.get(c, 0) < v:
                b.r[c] = v

    def op(self, ek, fn, r=(), w=(), same_ok=False):
        r = [_b(x) for x in r]
        w = [_b(x) for x in w]
        c = self.ctr[ek]
        self._wait(ek, self._deps(r, w, skip=c if same_ok else None))
        ins = fn(self.engs[ek])
        c.count += 1
        ins.then_inc(c.sem, 1)
        self.n_ins += 1
        self._mark(r, w, c, c.count)
        return ins

    def dma(self, qk, fn, r=(), w=()):
        r = [_b(x) for x in r]
        w = [_b(x) for x in w]
        lst = self.dq[qk]
        i = self.dq_i[qk]
        self.dq_i[qk] = (i + 1) % len(lst)
        c = lst[i]
        deps = self._deps(r, w)
        if c.count > 0:
            deps.append((c, c.count))
        self._wait(qk, deps)
        ins = fn(self.engs[qk])
        c.count += 16
        ins.then_inc(c.sem, 16)
        self.n_ins += 1
        self._mark(r, w, c, c.count)
        return ins

    def barrier(self):
        deps = []
        for k in ("pe", "dve", "act", "pool"):
            c = self.ctr[k]
            if c.count:
                deps.append((c, c.count))
        for k in self.dq:
            for c in self.dq[k]:
                if c.count:
                    deps.append((c, c.count))
        self._wait("sp", deps)
        self.bar_n += 1
        self.engs["sp"].sem_inc(self.bar_sem, 1)
        for k in ("pe", "dve", "act", "pool"):
            self.engs[k].wait_ge(self.bar_sem, self.bar_n)
            for c, v in deps:
                self.waited[k][c] = v
        for c, v in deps:
            self.waited["sp"][c] = v


class Rot:
    def __init__(self, tiles):
        self.tiles = tiles
        self.i = 0

    def next(self):
        t = self.tiles[self.i % len(self.tiles)]
        self.i += 1
        return t


def make_consts():
    c = {}
    pos = np.arange(SEQ, dtype=np.float32)
    ret_freqs = (1.0 / (np.float32(10000.0) ** np.linspace(0.0, 1.0, 128, dtype=np.float32))).astype(np.float32)
    ang = (pos[:, None] * ret_freqs[None, :]).astype(np.float32)
    c["ret_cos"] = np.ascontiguousarray(np.cos(ang).T).astype(np.float32)
    c["ret_sin"] = np.ascontiguousarray(np.sin(ang).T).astype(np.float32)
    log_g = np.log(1.0 - np.exp2(-5.0 - np.arange(4, dtype=np.float32))).astype(np.float32)
    ar = np.arange(128, dtype=np.float32)
    rel = ar[:, None] - ar[None, :]
    dm = np.where(rel[None] >= 0, np.exp(np.maximum(rel, 0.0)[None] * log_g[:, None, None]), 0.0)
    c["maskT"] = np.ascontiguousarray(dm.transpose(2, 0, 1)).astype(np.float32)
    qd = np.exp((ar + 1.0)[None] * log_g[:, None]).astype(np.float32)
    c["qdec"] = np.ascontiguousarray(np.broadcast_to(qd[None], (128, 4, 128))).astype(np.float32)
    kd = np.exp((128 - 1.0 - ar)[None] * log_g[:, None]).astype(np.float32)
    c["kdec"] = np.ascontiguousarray(kd.T).astype(np.float32)
    c["cdec"] = [float(x) for x in np.exp(128 * log_g)]
    dfreq = (np.float32(500000.0) ** (-np.arange(0, 16, 2, dtype=np.float32) / 16)).astype(np.float32)
    dang = (pos[:, None] * dfreq[None, :]).astype(np.float32)
    c["dcos"] = np.ascontiguousarray(np.cos(dang).reshape(16, 128, 8).transpose(1, 0, 2)).astype(np.float32)
    c["dsin"] = np.ascontiguousarray(np.sin(dang).reshape(16, 128, 8).transpose(1, 0, 2)).astype(np.float32)
    c["tri"] = (ar[None, :] >= ar[:, None]).astype(np.float32).astype(ml_dtypes.bfloat16)
    c["ident"] = np.eye(128, dtype=np.float32).astype(ml_dtypes.bfloat16)
    io = np.arange(16, dtype=np.float32)
    c["iota16"] = np.ascontiguousarray(np.broadcast_to(np.stack([io, io * 16.0])[None], (128, 2, 16))).astype(np.float32)
    return c


CONST_SPECS = {
    "ret_cos": ([128, SEQ], F32), "ret_sin": ([128, SEQ], F32), "maskT": ([128, 4, 128], F32),
    "qdec": ([128, 4, 128], F32), "kdec": ([128, 4], F32), "dcos": ([128, 16, 8], F32),
    "dsin": ([128, 16, 8], F32), "tri": ([128, 128], BF16), "ident": ([128, 128], BF16),
    "iota16": ([128, 2, 16], F32),
}


def build_program(nseq=2, n_steps=99, dbg=False):
    Tk = nseq * SEQ
    NT = Tk // 128
    NTB = Tk // 512
    nc = bass.Bass("TRN2", target_bir_lowering=False)
    cdec = make_consts()["cdec"]

    def din(name, shape, dt=F32):
        return nc.dram_tensor(name, shape, dt, kind="ExternalInput").ap()

    def dscr(name, shape, dt):
        return nc.dram_tensor(name, shape, dt, kind="ExternalOutput" if dbg else "Internal").ap()

    x_in = din("x", [Tk, D])
    ret_w_in = din("ret_w_in", [2, D, 6144])
    ret_w_out = din("ret_w_out", [2, 2048, D])
    kv_w = din("kv_w", [D, 2048])
    diff_w_q = din("diff_w_q", [2, D, D])
    diff_lambda = din("diff_lambda", [2, 256])
    diff_subln_g = din("diff_subln_g", [2, 128])
    diff_w_out = din("diff_w_out", [2, D, D])
    peer_w_q = din("peer_w_q", [4, D, 2048])
    peer_skT = din("peer_skT", [4, 128, 16, 128])
    peer_u = din("peer_u", [4, NEXP, D])
    peer_v = din("peer_v", [4, NEXP, D])
    peer_u_flat = peer_u.rearrange("l n d -> (l n) d")
    peer_v_flat = peer_v.rearrange("l n d -> (l n) d")
    ln_g = din("ln_g", [4, 2, D])
    ln_b = din("ln_b", [4, 2, D])
    cst = {k: din("c_" + k, shp, dt) for k, (shp, dt) in CONST_SPECS.items()}
    out = nc.dram_tensor("out", [Tk, D], F32, kind="ExternalOutput").ap()

    X = dscr("X", [Tk, D], F32)
    XT = dscr("XT", [D, Tk], BF16)
    QT = dscr("QT", [D, Tk], BF16)
    QDT = dscr("QDT", [D, Tk], BF16)
    KT = dscr("KT", [D, Tk], BF16)
    KD = dscr("KD", [Tk, D], BF16)
    V = dscr("V", [Tk, 2048], BF16)
    SG = dscr("SG", [Tk, 2048], BF16)
    Z = dscr("Z", [Tk, 2048], BF16)
    QP = dscr("QP", [2048, Tk], BF16)
    KTS = dscr("KTS", [D, Tk], BF16)
    VS = dscr("VS", [Tk, D], BF16)
    QTD = dscr("QTD", [D, Tk], BF16)
    OATT = dscr("OATT", [Tk, D], BF16)
    UVB = nc.dram_tensor("UVB", [4 * NEXP, 2048], BF16, kind="Internal").ap()

    def fm(ap):
        return ap.rearrange("(c p) t -> p c t", p=128)

    def tm(ap):
        return ap.rearrange("(n p) f -> p n f", p=128)

    with ExitStack() as es:
        S = Sched(nc, es)

        uid = {"n": 0}

        def sb(st, name, shape, dt):
            uid["n"] += 1
            name = f"{name}_{uid['n']}"
            return T(st.enter_context(nc.sbuf_tensor(name, shape, dt)), name)

        def ps(st, name, shape, dt):
            uid["n"] += 1
            name = f"{name}_{uid['n']}"
            return T(st.enter_context(nc.psum_tensor(name, shape, dt)), name)

        def rot(st, name, shape, dt, n, psum=False):
            mk = ps if psum else sb
            return Rot([mk(st, f"{name}{i}", shape, dt) for i in range(n)])

        ident = sb(es, "ident", [128, 128], BF16)
        S.dma("sp", lambda e: e.dma_start(out=ident[:], in_=cst["ident"]), w=[ident])

        flip = {"i": 0}

        def copy_any(out_ap, in_ap, r, w, engines=("act", "dve")):
            ek = engines[flip["i"] % len(engines)]
            flip["i"] += 1
            if ek == "act":
                S.op("act", lambda e: e.copy(out=out_ap, in_=in_ap), r=r, w=w)
            else:
                S.op(ek, lambda e: e.tensor_copy(out=out_ap, in_=in_ap), r=r, w=w)

        def load_xT_all(st):
            xT = sb(st, "xT_all", [128, 8, Tk], BF16)
            for c in range(8):
                S.dma("sp", lambda e, c=c: e.dma_start(out=xT[:, c, :], in_=fm(XT)[:, c, :]), w=[xT])
            return xT

        class WLoader:
            def __init__(self, st, kc, wmax, nbuf=2, name="wl"):
                self.kc = kc
                self.wf = rot(st, name + "f", [128, kc, wmax], F32, nbuf)
                self.wb = rot(st, name + "b", [128, kc, wmax], BF16, nbuf)

            def load(self, w_ap, w):
                wf = self.wf.next()
                wb = self.wb.next()
                src = w_ap.rearrange("(c p) n -> p c n", p=128)
                half = self.kc // 2
                S.dma("sp", lambda e: e.dma_start(out=wf[:, 0:half, 0:w], in_=src[:, 0:half, :]), w=[wf])
                S.dma("sp", lambda e: e.dma_start(out=wf[:, half:, 0:w], in_=src[:, half:, :]), w=[wf])
                S.op("pool", lambda e: e.tensor_copy(out=wb[:, :, 0:w], in_=wf[:, :, 0:w]), r=[wf], w=[wb])
                return wb

        class LNEpilogue:
            def __init__(self, st, l, which, ew="pool", npt=2, nbuf=2):
                self.ew = ew
                self.gb = sb(st, "ln_gbc", [128, D], F32)
                self.bb = sb(st, "ln_bbc", [128, D], F32)
                S.dma("sp", lambda e: e.dma_start(out=self.gb[:], in_=ln_g[l, which].partition_broadcast(128)), w=[self.gb])
                S.dma("sp", lambda e: e.dma_start(out=self.bb[:], in_=ln_b[l, which].partition_broadcast(128)), w=[self.bb])
                self.st = rot(st, "ln_st", [128, 2, 6], F32, 2)
                self.mv = rot(st, "ln_mv", [128, 4], F32, 2)
                self.xn = rot(st, "ln_xn", [128, D], F32, nbuf)
                self.xo = rot(st, "ln_xo", [128, D], F32, nbuf)
                self.xb = rot(st, "ln_xb", [128, D], BF16, nbuf)
                self.xT = rot(st, "ln_xT", [128, 8, 128], BF16, nbuf)
                self.pT = rot(st, "ln_pT", [128, 8, 128], BF16, npt, psum=True)

            def run(self, y, tt, x_dst):
                st_, mv, xn, xo, xb, xT, pT = (self.st.next(), self.mv.next(), self.xn.next(), self.xo.next(),
                                               self.xb.next(), self.xT.next(), self.pT.next())
                S.op("dve", lambda e: e.bn_stats(out=st_[:, 0, :], in_=y[:, 0:512]), r=[y], w=[st_])
                S.op("dve", lambda e: e.bn_stats(out=st_[:, 1, :], in_=y[:, 512:1024]), r=[y], w=[st_])
                S.op("dve", lambda e: e.bn_aggr(out=mv[:, 0:2], in_=st_[:].rearrange("p a b -> p (a b)")), r=[st_], w=[mv])
                S.op("dve", lambda e: e.tensor_scalar(out=mv[:, 2:3], in0=mv[:, 1:2], scalar1=1.0, scalar2=EPS,
                                                      op0=ALU.mult, op1=ALU.add), r=[mv], w=[mv])
                S.op("act", lambda e: e.activation(out=mv[:, 2:3], in_=mv[:, 2:3], func=AF.Sqrt), r=[mv], w=[mv])
                S.op("dve", lambda e: e.reciprocal(out=mv[:, 3:4], in_=mv[:, 2:3]), r=[mv], w=[mv])
                S.op("dve", lambda e: e.tensor_scalar(out=xn[:], in0=y[:], scalar1=mv[:, 0:1], scalar2=mv[:, 3:4],
                                                      op0=ALU.subtract, op1=ALU.mult), r=[y, mv], w=[xn])
                S.op(self.ew, lambda e: e.tensor_tensor(out=xn[:], in0=xn[:], in1=self.gb[:], op=ALU.mult), r=[xn, self.gb], w=[xn])
                S.op(self.ew, lambda e: e.tensor_tensor(out=xo[:], in0=xn[:], in1=self.bb[:], op=ALU.add), r=[xn, self.bb], w=[xo])
                sq = "pool" if self.ew == "pool" else "sp"
                S.dma(sq, lambda e: e.dma_start(out=x_dst[tt * 128:(tt + 1) * 128, :], in_=xo[:]), r=[xo])
                S.op("act", lambda e: e.copy(out=xb[:], in_=xo[:]), r=[xo], w=[xb])
                for c in range(8):
                    S.op("pe", lambda e, c=c: e.transpose(pT[:, c, :], xb[:, c * 128:(c + 1) * 128], ident[:]),
                         r=[xb, ident], w=[pT], same_ok=True)
                S.op("act", lambda e: e.copy(out=xT[:], in_=pT[:]), r=[pT], w=[xT])
                S.dma(sq, lambda e: e.dma_start(out=fm(XT)[:, :, tt * 128:(tt + 1) * 128], in_=xT[:]), r=[xT])

        def phase_init():
            with ExitStack() as st:
                xf = rot(st, "i_xf", [128, D], F32, 2)
                xb = rot(st, "i_xb", [128, D], BF16, 2)
                xT = rot(st, "i_xT", [128, 8, 128], BF16, 2)
                pT = rot(st, "i_pT", [128, 8, 128], BF16, 2, psum=True)
                for tt in range(NT):
                    a, b_, c_, p = xf.next(), xb.next(), xT.next(), pT.next()
                    S.dma("sp", lambda e: e.dma_start(out=a[:], in_=x_in[tt * 128:(tt + 1) * 128, :]), w=[a])
                    S.op("dve", lambda e: e.tensor_copy(out=b_[:], in_=a[:]), r=[a], w=[b_])
                    for c in range(8):
                        S.op("pe", lambda e, c=c: e.transpose(p[:, c, :], b_[:, c * 128:(c + 1) * 128], ident[:]),
                             r=[b_, ident], w=[p], same_ok=True)
                    S.op("act", lambda e: e.copy(out=c_[:], in_=p[:]), r=[p], w=[c_])
                    S.dma("sp", lambda e: e.dma_start(out=fm(XT)[:, :, tt * 128:(tt + 1) * 128], in_=c_[:]), r=[c_])
                S.barrier()

        def phase_ret_proj(l):
            with ExitStack() as st:
                xT = load_xT_all(st)
                cosT = sb(st, "r_cos", [128, SEQ], F32)
                sinT = sb(st, "r_sin", [128, SEQ], F32)
                qdec = sb(st, "r_qdec", [128, 4, 128], F32)
                kdec = sb(st, "r_kdec", [128, 4], F32)
                S.dma("sp", lambda e: e.dma_start(out=cosT[:], in_=cst["ret_cos"]), w=[cosT])
                S.dma("sp", lambda e: e.dma_start(out=sinT[:], in_=cst["ret_sin"]), w=[sinT])
                S.dma("sp", lambda e: e.dma_start(out=qdec[:], in_=cst["qdec"]), w=[qdec])
                S.dma("sp", lambda e: e.dma_start(out=kdec[:], in_=cst["kdec"]), w=[kdec])
                wl = WLoader(st, 8, 512, 2, "rw")
                p1r = rot(st, "r_p1", [128, 512], F32, 2, psum=True)
                p2r = rot(st, "r_p2", [128, 512], F32, 2, psum=True)
                pkr = rot(st, "r_pk", [128, 8, 128], BF16, 2, psum=True)
                a1r = rot(st, "r_a1", [128, 512], F32, 2)
                a2r = rot(st, "r_a2", [128, 512], F32, 2)
                tr = [rot(st, f"r_t{i}", [128, 512], F32, 2) for i in range(4)]
                o1r = rot(st, "r_o1", [128, 512], F32, 2)
                o2r = rot(st, "r_o2", [128, 512], F32, 2)
                obr = rot(st, "r_ob", [128, 2, 512], BF16, 3)
                odr = rot(st, "r_od", [128, 2, 512], BF16, 3)
                kdr = rot(st, "r_kd", [128, 4, 256], BF16, 2)
                for kind in range(2):
                    for h in range(4):
                        c0 = kind * 1024 + h * 256
                        wb = wl.load(ret_w_in[l, :, c0:c0 + 256], 256)
                        for tb in range(NTB):
                            p1, p2 = p1r.next(), p2r.next()
                            tsl = slice(tb * 512, (tb + 1) * 512)
                            for kc in range(8):
                                S.op("pe", lambda e, kc=kc: e.matmul(p1[:], lhsT=wb[:, kc, 0:128], rhs=xT[:, kc, tsl],
                                                                      start=(kc == 0), stop=(kc == 7)),
                                     r=[wb, xT], w=[p1], same_ok=True)
                            for kc in range(8):
                                S.op("pe", lambda e, kc=kc: e.matmul(p2[:], lhsT=wb[:, kc, 128:256], rhs=xT[:, kc, tsl],
                                                                      start=(kc == 0), stop=(kc == 7)),
                                     r=[wb, xT], w=[p2], same_ok=True)
                            a1, a2 = a1r.next(), a2r.next()
                            sc = 1.0 if kind == 0 else 1.0 / 16.0
                            S.op("act", lambda e: e.mul(out=a1[:], in_=p1[:], mul=sc), r=[p1], w=[a1])
                            S.op("act", lambda e: e.mul(out=a2[:], in_=p2[:], mul=sc), r=[p2], w=[a2])
                            p0 = (tb * 512) % SEQ
                            cs, sn = cosT[:, p0:p0 + 512], sinT[:, p0:p0 + 512]
                            t1, t2, t3, t4 = [r_.next() for r_ in tr]
                            S.op("dve", lambda e: e.tensor_tensor(out=t1[:], in0=a1[:], in1=cs, op=ALU.mult), r=[a1, cosT], w=[t1])
                            S.op("pool", lambda e: e.tensor_tensor(out=t2[:], in0=a2[:], in1=sn, op=ALU.mult), r=[a2, sinT], w=[t2])
                            S.op("dve", lambda e: e.tensor_tensor(out=t3[:], in0=a1[:], in1=sn, op=ALU.mult), r=[a1, sinT], w=[t3])
                            S.op("pool", lambda e: e.tensor_tensor(out=t4[:], in0=a2[:], in1=cs, op=ALU.mult), r=[a2, cosT], w=[t4])
                            ob = obr.next()
                            if kind == 0:
                                o1, o2, od = o1r.next(), o2r.next(), odr.next()
                                S.op("dve", lambda e: e.tensor_tensor(out=o1[:], in0=t1[:], in1=t2[:], op=ALU.subtract), r=[t1, t2], w=[o1])
                                S.op("pool", lambda e: e.tensor_tensor(out=o2[:], in0=t3[:], in1=t4[:], op=ALU.add), r=[t3, t4], w=[o2])
                                S.op("act", lambda e: e.copy(out=ob[:, 0, :], in_=o1[:]), r=[o1], w=[ob])
                                S.op("act", lambda e: e.copy(out=ob[:, 1, :], in_=o2[:]), r=[o2], w=[ob])
                                qd_b = qdec[:, h, :].unsqueeze(1).to_broadcast([128, 4, 128])
                                S.op("dve", lambda e: e.tensor_tensor(out=od[:, 0, :].rearrange("p (a b) -> p a b", a=4),
                                                                      in0=o1[:].rearrange("p (a b) -> p a b", a=4), in1=qd_b, op=ALU.mult),
                                     r=[o1, qdec], w=[od])
                                S.op("pool", lambda e: e.tensor_tensor(out=od[:, 1, :].rearrange("p (a b) -> p a b", a=4),
                                                                       in0=o2[:].rearrange("p (a b) -> p a b", a=4), in1=qd_b, op=ALU.mult),
                                     r=[o2, qdec], w=[od])
                                S.dma("sp", lambda e: e.dma_start(out=fm(QT)[:, 2 * h:2 * h + 2, tsl], in_=ob[:]), r=[ob])
                                S.dma("sp", lambda e: e.dma_start(out=fm(QDT)[:, 2 * h:2 * h + 2, tsl], in_=od[:]), r=[od])
                            else:
                                S.op("dve", lambda e: e.tensor_tensor(out=ob[:, 0, :], in0=t1[:], in1=t2[:], op=ALU.subtract), r=[t1, t2], w=[ob])
                                S.op("pool", lambda e: e.tensor_tensor(out=ob[:, 1, :], in0=t3[:], in1=t4[:], op=ALU.add), r=[t3, t4], w=[ob])
                                S.dma("sp", lambda e: e.dma_start(out=fm(KT)[:, 2 * h:2 * h + 2, tsl], in_=ob[:]), r=[ob])
                                pk, kd = pkr.next(), kdr.next()
                                for i in range(4):
                                    for j in range(2):
                                        S.op("pe", lambda e, i=i, j=j: e.transpose(pk[:, i * 2 + j, :], ob[:, j, i * 128:(i + 1) * 128], ident[:]),
                                             r=[ob, ident], w=[pk], same_ok=True)
                                S.op("act", lambda e: e.activation(out=kd[:].rearrange("p a b -> p (a b)"),
                                                                   in_=pk[:].rearrange("p a b -> p (a b)"), func=AF.Copy,
                                                                   scale=kdec[:, h:h + 1]), r=[pk, kdec], w=[kd])
                                S.dma("sp", lambda e: e.dma_start(out=tm(KD)[:, tb * 4:(tb + 1) * 4, h * 256:(h + 1) * 256], in_=kd[:]), r=[kd])
                pvr = rot(st, "r_pv", [128, 512], F32, 2, psum=True)
                ovr = rot(st, "r_ov", [128, 512], BF16, 3)
                for kind in range(2):
                    for cg in range(4):
                        c0 = 2048 + kind * 2048 + cg * 512
                        wb = wl.load(ret_w_in[l, :, c0:c0 + 512], 512)
                        dst = V if kind == 0 else SG
                        for tt in range(NT):
                            pv, ov = pvr.next(), ovr.next()
                            for kc in range(8):
                                S.op("pe", lambda e, kc=kc: e.matmul(pv[:], lhsT=xT[:, kc, tt * 128:(tt + 1) * 128], rhs=wb[:, kc, :],
                                                                      start=(kc == 0), stop=(kc == 7)),
                                     r=[wb, xT], w=[pv], same_ok=True)
                            if kind == 0:
                                copy_any(ov[:], pv[:], [pv], [ov])
                            else:
                                S.op("act", lambda e: e.activation(out=ov[:], in_=pv[:], func=AF.Silu), r=[pv], w=[ov])
                            S.dma("sp", lambda e: e.dma_start(out=dst[tt * 128:(tt + 1) * 128, cg * 512:(cg + 1) * 512], in_=ov[:]), r=[ov])
                S.barrier()

        def phase_ret_core():
            with ExitStack() as st:
                maskT = sb(st, "c_mask", [128, 4, 128], F32)
                S.dma("sp", lambda e: e.dma_start(out=maskT[:], in_=cst["maskT"]), w=[maskT])
                st_fs = [sb(st, f"c_stf{i}", [128, 2, 512], F32) for i in range(2)]
                st_bs = [sb(st, f"c_stb{i}", [128, 2, 512], BF16) for i in range(2)]
                qtr = rot(st, "c_qt", [128, 2, 128], BF16, 3)
                qdr = rot(st, "c_qd", [128, 2, 128], BF16, 3)
                ktr = rot(st, "c_kt", [128, 2, 128], BF16, 3)
                kdr = rot(st, "c_kd", [128, 256], BF16, 3)
                vr = rot(st, "c_v", [128, 512], BF16, 3)
                sgr = rot(st, "c_sg", [128, 512], BF16, 3)
                scmr = rot(st, "c_scm", [128, 128], BF16, 2)
                ynr = rot(st, "c_yn", [128, 512], F32, 2)
                zr = rot(st, "c_z", [128, 512], BF16, 2)
                str_ = rot(st, "c_st", [128, 6], F32, 2)
                mvr = rot(st, "c_mv", [128, 4], F32, 2)
                scp = rot(st, "c_scp", [128, 128], F32, 2, psum=True)
                ypr = rot(st, "c_yp", [128, 512], F32, 2, psum=True)
                upr = rot(st, "c_up", [128, 2, 512], F32, 2, psum=True)

                def loads(s, h, c):
                    tt = s * 16 + c
                    tsl = slice(tt * 128, (tt + 1) * 128)
                    qt, qd, kt, kd, v, sg = qtr.next(), qdr.next(), ktr.next(), kdr.next(), vr.next(), sgr.next()
                    S.dma("sp", lambda e: e.dma_start(out=qt[:], in_=fm(QT)[:, 2 * h:2 * h + 2, tsl]), w=[qt])
                    S.dma("sp", lambda e: e.dma_start(out=qd[:], in_=fm(QDT)[:, 2 * h:2 * h + 2, tsl]), w=[qd])
                    S.dma("sp", lambda e: e.dma_start(out=kt[:], in_=fm(KT)[:, 2 * h:2 * h + 2, tsl]), w=[kt])
                    S.dma("sp", lambda e: e.dma_start(out=kd[:], in_=KD[tsl, h * 256:(h + 1) * 256]), w=[kd])
                    S.dma("sp", lambda e: e.dma_start(out=v[:], in_=V[tsl, h * 512:(h + 1) * 512]), w=[v])
                    S.dma("sp", lambda e: e.dma_start(out=sg[:], in_=SG[tsl, h * 512:(h + 1) * 512]), w=[sg])
                    return qt, qd, kt, kd, v, sg

                items = [(s, h, c) for s in range(nseq) for hp in (0, 2) for c in range(16) for h in (hp, hp + 1)]
                nxt = loads(*items[0])
                for idx, (s, h, c) in enumerate(items):
                    st_f, st_b = st_fs[h % 2], st_bs[h % 2]
                    qt, qd, kt, kd, v, sg = nxt
                    if idx + 1 < len(items):
                        nxt = loads(*items[idx + 1])
                    tt = s * 16 + c
                    if c == 0:
                        S.op("pool", lambda e: e.memset(st_f[:], 0.0), w=[st_f])
                        S.op("pool", lambda e: e.memset(st_b[:], 0.0), w=[st_b])
                    sp_, yp, up = scp.next(), ypr.next(), upr.next()
                    for dc in range(2):
                        S.op("pe", lambda e, dc=dc: e.matmul(sp_[:], lhsT=kt[:, dc, :], rhs=qt[:, dc, :], start=(dc == 0), stop=(dc == 1)),
                             r=[kt, qt], w=[sp_], same_ok=True)
                    scm = scmr.next()
                    S.op("dve", lambda e: e.tensor_tensor(out=scm[:], in0=sp_[:], in1=maskT[:, h, :], op=ALU.mult), r=[sp_, maskT], w=[scm])
                    S.op("pe", lambda e: e.matmul(yp[:], lhsT=scm[:], rhs=v[:], start=True, stop=False), r=[scm, v], w=[yp], same_ok=True)
                    for dc in range(2):
                        S.op("pe", lambda e, dc=dc: e.matmul(yp[:], lhsT=qd[:, dc, :], rhs=st_b[:, dc, :], start=False, stop=(dc == 1)),
                             r=[qd, st_b], w=[yp], same_ok=True)
                    for dc in range(2):
                        S.op("pe", lambda e, dc=dc: e.matmul(up[:, dc, :], lhsT=kd[:, dc * 128:(dc + 1) * 128], rhs=v[:], start=True, stop=True),
                             r=[kd, v], w=[up], same_ok=True)
                    if c < 15:
                        S.op("dve", lambda e: e.scalar_tensor_tensor(out=st_f[:].rearrange("p a b -> p (a b)"),
                                                                      in0=st_f[:].rearrange("p a b -> p (a b)"), scalar=cdec[h],
                                                                      in1=up[:].rearrange("p a b -> p (a b)"), op0=ALU.mult, op1=ALU.add),
                             r=[st_f, up], w=[st_f])
                        S.op("act", lambda e: e.copy(out=st_b[:], in_=st_f[:]), r=[st_f], w=[st_b])
                    st_, mv, yn, z = str_.next(), mvr.next(), ynr.next(), zr.next()
                    S.op("dve", lambda e: e.bn_stats(out=st_[:], in_=yp[:]), r=[yp], w=[st_])
                    S.op("dve", lambda e: e.bn_aggr(out=mv[:, 0:2], in_=st_[:]), r=[st_], w=[mv])
                    S.op("dve", lambda e: e.tensor_scalar(out=mv[:, 2:3], in0=mv[:, 1:2], scalar1=1.0, scalar2=EPS,
                                                          op0=ALU.mult, op1=ALU.add), r=[mv], w=[mv])
                    S.op("act", lambda e: e.activation(out=mv[:, 2:3], in_=mv[:, 2:3], func=AF.Sqrt), r=[mv], w=[mv])
                    S.op("dve", lambda e: e.reciprocal(out=mv[:, 3:4], in_=mv[:, 2:3]), r=[mv], w=[mv])
                    S.op("dve", lambda e: e.tensor_scalar(out=yn[:], in0=yp[:], scalar1=mv[:, 0:1], scalar2=mv[:, 3:4],
                                                          op0=ALU.subtract, op1=ALU.mult), r=[yp, mv], w=[yn])
                    S.op("pool", lambda e: e.tensor_tensor(out=z[:], in0=yn[:], in1=sg[:], op=ALU.mult), r=[yn, sg], w=[z])
                    S.dma("pool", lambda e: e.dma_start(out=Z[tt * 128:(tt + 1) * 128, h * 512:(h + 1) * 512], in_=z[:]), r=[z])
                S.barrier()

        def phase_outproj_ln(Zsrc, F, w_ap, x_src, x_dst, l, which):
            FC = F // 128
            with ExitStack() as st:
                wsb = sb(st, "o_w", [128, FC, D], BF16)
                with ExitStack() as st2:
                    wl = WLoader(st2, 8, 512, 2, "ow")
                    for fg in range(FC // 8):
                        for hf in range(2):
                            wb = wl.load(w_ap[fg * 1024:(fg + 1) * 1024, hf * 512:(hf + 1) * 512], 512)
                            S.op("dve", lambda e: e.tensor_copy(out=wsb[:, fg * 8:(fg + 1) * 8, hf * 512:(hf + 1) * 512], in_=wb[:]),
                                 r=[wb], w=[wsb])
                    S.barrier()
                ln = LNEpilogue(st, l, which)
                zr = rot(st, "o_z", [128, F], BF16, 2)
                xr = rot(st, "o_x", [128, D], F32, 2)
                zTr = rot(st, "o_zT", [128, FC, 128], BF16, 2)
                yr = rot(st, "o_y", [128, D], F32, 2)
                pzr = rot(st, "o_pz", [128, FC, 128], BF16, 1, psum=True)
                pmr = rot(st, "o_pm", [128, 2, 512], F32, 2, psum=True)

                def loads(tt):
                    z, x = zr.next(), xr.next()
                    S.dma("sp", lambda e: e.dma_start(out=z[:], in_=Zsrc[tt * 128:(tt + 1) * 128, :]), w=[z])
                    S.dma("sp", lambda e: e.dma_start(out=x[:], in_=x_src[tt * 128:(tt + 1) * 128, :]), w=[x])
                    return z, x
                nxt = loads(0)
                for tt in range(NT):
                    z, x = nxt
                    if tt + 1 < NT:
                        nxt = loads(tt + 1)
                    pz, zT, pm, y = pzr.next(), zTr.next(), pmr.next(), yr.next()
                    for fc in range(FC):
                        S.op("pe", lambda e, fc=fc: e.transpose(pz[:, fc, :], z[:, fc * 128:(fc + 1) * 128], ident[:]),
                             r=[z, ident], w=[pz], same_ok=True)
                    for g8 in range(FC // 8):
                        copy_any(zT[:, g8 * 8:(g8 + 1) * 8, :], pz[:, g8 * 8:(g8 + 1) * 8, :], [pz], [zT])
                    for hf in range(2):
                        for fc in range(FC):
                            S.op("pe", lambda e, fc=fc, hf=hf: e.matmul(pm[:, hf, :], lhsT=zT[:, fc, :], rhs=wsb[:, fc, hf * 512:(hf + 1) * 512],
                                                                         start=(fc == 0), stop=(fc == FC - 1)),
                                 r=[zT, wsb], w=[pm], same_ok=True)
                    for hf in range(2):
                        S.op("dve", lambda e, hf=hf: e.scalar_tensor_tensor(out=y[:, hf * 512:(hf + 1) * 512], in0=x[:, hf * 512:(hf + 1) * 512],
                                                                             scalar=ALPHA, in1=pm[:, hf, :], op0=ALU.mult, op1=ALU.add),
                             r=[x, pm], w=[y])
                    ln.run(y, tt, x_dst)
                S.barrier()

        def phase_uv_prep():
            R = 4
            NI = 4 * NEXP // (128 * R)
            uview = peer_u_flat.rearrange("(n p r) d -> n p r d", p=128, r=R)
            vview = peer_v_flat.rearrange("(n p r) d -> n p r d", p=128, r=R)
            oview = UVB.rearrange("(n p r) d -> n p r d", p=128, r=R)
            with ExitStack() as st:
                ufr = rot(st, "uv_uf", [128, R, D], F32, 3)
                vfr = rot(st, "uv_vf", [128, R, D], F32, 3)
                obr = rot(st, "uv_ob", [128, R, 2048], BF16, 3)

                def loads(n):
                    uf, vf = ufr.next(), vfr.next()
                    S.dma("sp", lambda e: e.dma_start(out=uf[:], in_=uview[n]), w=[uf])
                    S.dma("sp", lambda e: e.dma_start(out=vf[:], in_=vview[n]), w=[vf])
                    return uf, vf
                nxt = loads(0)
                for n in range(NI):
                    uf, vf = nxt
                    if n + 1 < NI:
                        nxt = loads(n + 1)
                    ob = obr.next()
                    S.op("dve", lambda e: e.tensor_copy(out=ob[:, :, 0:D], in_=uf[:]), r=[uf], w=[ob])
                    if n % 3 == 2:
                        S.op("pool", lambda e: e.tensor_copy(out=ob[:, :, D:2 * D], in_=vf[:]), r=[vf], w=[ob])
                    else:
                        S.op("act", lambda e: e.copy(out=ob[:, :, D:2 * D], in_=vf[:]), r=[vf], w=[ob])
                    S.dma("sp", lambda e: e.dma_start(out=oview[n], in_=ob[:]), r=[ob])
                S.barrier()

        def phase_peer_q(l):
            with ExitStack() as st:
                xT = load_xT_all(st)
                wl = WLoader(st, 8, 512, 2, "pw")
                ppr = rot(st, "p_pp", [128, 512], F32, 3, psum=True)
                obr = rot(st, "p_ob", [128, 512], BF16, 3)
                for g in range(4):
                    wb = wl.load(peer_w_q[l, :, g * 512:(g + 1) * 512], 512)
                    for j in range(4):
                        for tb in range(NTB):
                            pp, ob = ppr.next(), obr.next()
                            tsl = slice(tb * 512, (tb + 1) * 512)
                            for kc in range(8):
                                S.op("pe", lambda e, kc=kc: e.matmul(pp[:], lhsT=wb[:, kc, j * 128:(j + 1) * 128], rhs=xT[:, kc, tsl],
                                                                      start=(kc == 0), stop=(kc == 7)),
                                     r=[wb, xT], w=[pp], same_ok=True)
                            copy_any(ob[:], pp[:], [pp], [ob])
                            S.dma("sp", lambda e: e.dma_start(out=fm(QP)[:, g * 4 + j, tsl], in_=ob[:]), r=[ob])
                S.barrier()

        def phase_peer_main(l, x_src, x_dst):
            with ExitStack() as st:
                skb = sb(st, "m_skb", [128, 16, 128], BF16)
                with ExitStack() as st2:
                    skf = sb(st2, "m_skf", [128, 16, 128], F32)
                    S.dma("sp", lambda e: e.dma_start(out=skf[:], in_=peer_skT[l]), w=[skf])
                    S.op("dve", lambda e: e.tensor_copy(out=skb[:], in_=skf[:]), r=[skf], w=[skb])
                    S.barrier()
                io16 = sb(st, "m_io16", [128, 2, 16], F32)
                S.dma("sp", lambda e: e.dma_start(out=io16[:], in_=cst["iota16"]), w=[io16])
                ln = LNEpilogue(st, l, 1, ew="dve", npt=1, nbuf=1)
                xr = rot(st, "m_x", [128, D], F32, 3)
                qpr = rot(st, "m_qp", [128, 16, 128], BF16, 2)
                scps = ps(st, "m_scp", [128, 2, 512], F32)
                sc = sb(st, "m_sc", [128, 16, 128], F32)
                scr = rot(st, "m_scr", [128, 128], F32, 4)
                s16h = [Buf(f"s16h{i}") for i in range(16)]
                i16h = [Buf(f"i16h{i}") for i in range(16)]
                s16 = sb(st, "m_s16", [128, 16, 16], F32)
                i16u = sb(st, "m_i16u", [128, 16, 16], U32)
                i16f = sb(st, "m_i16f", [128, 16, 16], F32)
                candr = rot(st, "m_cand", [128, 16, 16], F32, 2)
                scr2 = rot(st, "m_scr2", [128, 256], F32, 2)
                tv = sb(st, "m_tv", [128, 8, 16], F32)
                posu = sb(st, "m_posu", [128, 8, 16], U32)
                posf = sb(st, "m_posf", [128, 8, 16], F32)
                posb = sb(st, "m_posb", [128, 8, 16], U32)
                bfm = sb(st, "m_bfm", [128, 8, 16], F32)
                a16 = sb(st, "m_a16", [128, 8, 16], F32)
                eq4 = rot(st, "m_eq4", [128, 8, 16, 16], F32, 1)
                sel1 = sb(st, "m_sel1", [128, 8, 16], F32)
                sel2 = sb(st, "m_sel2", [128, 8, 16], F32)
                idxf = sb(st, "m_idxf", [128, 128], F32)
                idxi = rot(st, "m_idxi", [128, 128], I32, 2)
                ex = sb(st, "m_ex", [128, 8, 16], F32)
                ssum = sb(st, "m_ssum", [128, 8], F32)
                gater = rot(st, "m_gate", [128, 8, 16], F32, 2)
                dots = sb(st, "m_dots", [128, 128], F32)
                wgt = sb(st, "m_wgt", [128, 128], F32)
                g1 = rot(st, "m_g1", [128, 16], F32, 2)
                g2 = rot(st, "m_g2", [128, 16], F32, 2)
                gbr = rot(st, "m_gb", [128, 2048], BF16, 24)
                junkb = rot(st, "m_junkb", [128, D], BF16, 4)
                junka = rot(st, "m_junka", [128, D], BF16, 2)
                dotsb = [Buf("dots_e"), Buf("dots_o")]
                xbr = rot(st, "m_xb", [128, D], BF16, 2)
                dgr = rot(st, "m_dg", [128, 8, 128], BF16, 3)
                accr = rot(st, "m_accp", [128, 2, 512], F32, 2, psum=True)
                g3 = rot(st, "m_g3", [128, 16], F32, 2)
                yr = rot(st, "m_y", [128, D], F32, 1)

                def select(tt, holder):
                    x, qp = xr.next(), qpr.next()
                    S.dma("sp", lambda e: e.dma_start(out=x[:], in_=x_src[tt * 128:(tt + 1) * 128, :]), w=[x])
                    S.dma("sp", lambda e: e.dma_start(out=qp[:], in_=fm(QP)[:, :, tt * 128:(tt + 1) * 128]), w=[qp])
                    for rnd in range(2):
                        for hq in range(8):
                            hp = rnd * 8 + hq
                            S.op("pe", lambda e, hp=hp, hq=hq: e.matmul(scps[:, hq // 4, (hq % 4) * 128:(hq % 4 + 1) * 128], lhsT=qp[:, hp, :], rhs=skb[:, hp, :],
                                                                         start=True, stop=True), r=[qp, skb], w=[scps], same_ok=True)
                        for bk in range(2):
                            S.op("act", lambda e, bk=bk, rnd=rnd: e.copy(out=sc[:, rnd * 8 + bk * 4:rnd * 8 + (bk + 1) * 4, :].rearrange("p a b -> p (a b)"),
                                                                          in_=scps[:, bk, :]), r=[scps], w=[sc])
                    xb = xbr.next()
                    S.op("act", lambda e: e.copy(out=xb[:], in_=x[:]), r=[x], w=[xb])
                    yield
                    for hp0 in range(0, 16, 2):
                        pr = [(hp0, scr.next()), (hp0 + 1, scr.next())]
                        for (hp, sr) in pr:
                            S.op("dve", lambda e: e.max(out=s16[:, hp, 0:8], in_=sc[:, hp, :]), r=[sc], w=[s16h[hp]])
                        for (hp, sr) in pr:
                            S.op("dve", lambda e: e.max_index(out=i16u[:, hp, 0:8], in_max=s16[:, hp, 0:8], in_values=sc[:, hp, :]), r=[sc, s16h[hp]], w=[i16h[hp]])
                        for (hp, sr) in pr:
                            S.op("dve", lambda e: e.match_replace(out=sr[:], in_to_replace=s16[:, hp, 0:8], in_values=sc[:, hp, :], imm_value=-1e30),
                                 r=[sc, s16h[hp]], w=[sr])
                        for (hp, sr) in pr:
                            S.op("dve", lambda e: e.max(out=s16[:, hp, 8:16], in_=sr[:]), r=[sr], w=[s16h[hp]])
                        for (hp, sr) in pr:
                            S.op("dve", lambda e: e.max_index(out=i16u[:, hp, 8:16], in_max=s16[:, hp, 8:16], in_values=sr[:]), r=[sr, s16h[hp]], w=[i16h[hp]])
                        if hp0 % 4 == 2:
                            yield
                    S.op("dve", lambda e: e.tensor_copy(out=i16f[:], in_=i16u[:]), r=i16h, w=[i16f])
                    i4 = i16f[:].rearrange("p (h two) k -> p h two k", two=2)
                    i1v, i2v = i4[:, :, 0, :], i4[:, :, 1, :]
                    S.op("dve", lambda e: e.tensor_scalar(out=i1v, in0=i1v, scalar1=128.0, scalar2=None, op0=ALU.mult), r=[i16f], w=[i16f])
                    for h in range(8):
                        cand, s2 = candr.next(), scr2.next()
                        a_b = lambda t_, hp_: t_[:, hp_, :].unsqueeze(2).to_broadcast([128, 16, 16])
                        b_b = lambda t_, hp_: t_[:, hp_, :].unsqueeze(1).to_broadcast([128, 16, 16])
                        S.op("dve", lambda e: e.tensor_tensor(out=cand[:], in0=a_b(s16, 2 * h), in1=b_b(s16, 2 * h + 1), op=ALU.add), r=[s16h[2 * h], s16h[2 * h + 1]], w=[cand])
                        cf = cand[:].rearrange("p a b -> p (a b)")
                        S.op("dve", lambda e: e.max(out=tv[:, h, 0:8], in_=cf), r=[cand], w=[tv])
                        S.op("dve", lambda e: e.max_index(out=posu[:, h, 0:8], in_max=tv[:, h, 0:8], in_values=cf), r=[cand, tv], w=[posu])
                        S.op("dve", lambda e: e.match_replace(out=s2[:], in_to_replace=tv[:, h, 0:8], in_values=cf, imm_value=-1e30), r=[cand, tv], w=[s2])
                        S.op("dve", lambda e: e.max(out=tv[:, h, 8:16], in_=s2[:]), r=[s2], w=[tv])
                        S.op("dve", lambda e: e.max_index(out=posu[:, h, 8:16], in_max=tv[:, h, 8:16], in_values=s2[:]), r=[s2, tv], w=[posu])
                        if h % 2 == 1:
                            yield
                    S.op("dve", lambda e: e.tensor_copy(out=posf[:], in_=posu[:]), r=[posu], w=[posf])
                    S.op("dve", lambda e: e.tensor_single_scalar(out=posb[:], in_=posu[:], scalar=15, op=ALU.bitwise_and), r=[posu], w=[posb])
                    S.op("dve", lambda e: e.tensor_copy(out=bfm[:], in_=posb[:]), r=[posb], w=[bfm])
                    S.op("dve", lambda e: e.tensor_tensor(out=a16[:], in0=posf[:], in1=bfm[:], op=ALU.subtract), r=[posf, bfm], w=[a16])
                    shp = [128, 8, 16, 16]
                    for (keyt, iot, valv, selo) in ((a16, 1, i1v, sel1), (bfm, 0, i2v, sel2)):
                        e4 = eq4.next()
                        S.op("dve", lambda e: e.tensor_tensor(out=e4[:], in0=keyt[:].unsqueeze(3).to_broadcast(shp),
                                                              in1=io16[:, iot, :].unsqueeze(1).unsqueeze(1).to_broadcast(shp), op=ALU.is_equal),
                             r=[keyt, io16], w=[e4])
                        S.op("dve", lambda e: e.tensor_tensor(out=e4[:], in0=e4[:], in1=valv.unsqueeze(2).to_broadcast(shp), op=ALU.mult),
                             r=[e4, i16f], w=[e4])
                        S.op("dve", lambda e: e.reduce_sum(out=selo[:].rearrange("p a b -> p (a b)"), in_=e4[:].rearrange("p a b c -> p (a b) c"), axis=AX.X),
                             r=[e4], w=[selo])
                        yield
                    S.op("dve", lambda e: e.tensor_tensor(out=idxf[:], in0=sel1[:].rearrange("p a b -> p (a b)"), in1=sel2[:].rearrange("p a b -> p (a b)"), op=ALU.add),
                         r=[sel1, sel2], w=[idxf])
                    gate = gater.next()
                    S.op("dve", lambda e: e.tensor_tensor(out=ex[:], in0=tv[:], in1=tv[:, :, 0:1].to_broadcast([128, 8, 16]), op=ALU.subtract), r=[tv], w=[ex])
                    S.op("act", lambda e: e.activation(out=ex[:], in_=ex[:], func=AF.Exp), r=[ex], w=[ex])
                    S.op("dve", lambda e: e.reduce_sum(out=ssum[:], in_=ex[:], axis=AX.X), r=[ex], w=[ssum])
                    S.op("dve", lambda e: e.reciprocal(out=ssum[:], in_=ssum[:]), r=[ssum], w=[ssum])
                    S.op("dve", lambda e: e.tensor_tensor(out=gate[:], in0=ex[:], in1=ssum[:].unsqueeze(2).to_broadcast([128, 8, 16]), op=ALU.mult),
                         r=[ex, ssum], w=[gate])
                    ii = idxi.next()
                    S.op("dve", lambda e: e.tensor_scalar(out=idxf[:], in0=idxf[:], scalar1=float(NEXP - 1), scalar2=float(l * NEXP), op0=ALU.min, op1=ALU.add),
                         r=[idxf], w=[idxf])
                    S.op("dve", lambda e: e.tensor_copy(out=ii[:], in_=idxf[:]), r=[idxf], w=[ii])
                    holder.update(dict(x=x, xb=xb, ii=ii, gate=gate))

                GS = 8
                NG = 128 // GS
                CG = 1.5957691216057308

                def dcol(sl):
                    return (sl % 2) * 64 + sl // 2

                def dv(g):
                    return dots[:].rearrange("p (two k) -> p two k", two=2)[:, :, g * 4:(g + 1) * 4]

                def sv(t_, g):
                    return t_.rearrange("p (k two) -> p two k", two=2)[:, :, g * 4:(g + 1) * 4]

                def t3(t_):
                    return t_[:, 0:GS].rearrange("p (two k) -> p two k", two=2)

                def stage_a(sel, g, j0, j1):
                    xb, ii = sel["xb"], sel["ii"]
                    gbs = sel["gbs"].setdefault(g, [])
                    for j in range(j0, j1):
                        sl = g * GS + j
                        gb, jb = gbr.next(), junkb.next()
                        gbs.append(gb)
                        S.dma("pool", lambda e, sl=sl: e.indirect_dma_start(out=gb[:], out_offset=None, in_=UVB,
                                                                            in_offset=bass.IndirectOffsetOnAxis(ap=ii[:, sl:sl + 1], axis=0)),
                              r=[ii], w=[gb])
                        if j % 2 == 0:
                            S.op("dve", lambda e, sl=sl: e.scalar_tensor_tensor(out=jb[:], in0=gb[:, 0:D], scalar=1.0, in1=xb[:], op0=ALU.mult, op1=ALU.mult,
                                                                                 accum_out=dots[:, dcol(sl):dcol(sl) + 1]), r=[gb, xb], w=[jb, dotsb[sl % 2]])
                        else:
                            S.op("dve", lambda e: e.tensor_tensor(out=jb[:], in0=gb[:, 0:D], in1=xb[:], op=ALU.mult), r=[gb, xb], w=[jb])
                            ja = junka.next()
                            S.op("act", lambda e, sl=sl: e.activation(out=ja[:], in_=jb[:], func=AF.Copy, accum_out=dots[:, dcol(sl):dcol(sl) + 1]),
                                 r=[jb], w=[ja, dotsb[sl % 2]])

                def stage_b1(sel, g):
                    hs = slice(g * GS, (g + 1) * GS)
                    a, b_ = g1.next(), g2.next()
                    sel["g12"] = (a, b_)
                    S.op("dve", lambda e: e.tensor_tensor(out=t3(a), in0=dv(g), in1=dv(g), op=ALU.mult), r=dotsb, w=[a])
                    S.op("dve", lambda e: e.tensor_scalar(out=a[:, 0:GS], in0=a[:, 0:GS], scalar1=0.044715, scalar2=1.0, op0=ALU.mult, op1=ALU.add), r=[a], w=[a])
                    S.op("dve", lambda e: e.tensor_tensor(out=t3(a), in0=t3(a), in1=dv(g), op=ALU.mult), r=[a] + dotsb, w=[a])
                    S.op("act", lambda e: e.activation(out=b_[:, 0:GS], in_=a[:, 0:GS], func=AF.Exp, scale=-CG), r=[a], w=[b_])

                def stage_b2(sel, g):
                    gate, accp = sel["gate"], sel["accp"]
                    gf = gate[:].rearrange("p a b -> p (a b)")
                    hs = slice(g * GS, (g + 1) * GS)
                    a, b_ = sel["g12"]
                    c_ = g3.next()
                    gbs = sel["gbs"][g]
                    S.op("dve", lambda e: e.tensor_scalar(out=b_[:, 0:GS], in0=b_[:, 0:GS], scalar1=1.0, scalar2=None, op0=ALU.add), r=[b_], w=[b_])
                    S.op("dve", lambda e: e.reciprocal(out=c_[:, 0:GS], in_=b_[:, 0:GS]), r=[b_], w=[c_])
                    S.op("dve", lambda e: e.tensor_tensor(out=t3(c_), in0=t3(c_), in1=dv(g), op=ALU.mult), r=[c_] + dotsb, w=[c_])
                    S.op("dve", lambda e: e.tensor_tensor(out=sv(wgt[:], g), in0=t3(c_), in1=sv(gf, g), op=ALU.mult), r=[c_, gate], w=[wgt])

                def stage_b2b(sel, g):
                    accp = sel["accp"]
                    hs = slice(g * GS, (g + 1) * GS)
                    gbs = sel["gbs"][g]
                    dg = dgr.next()
                    for j in range(GS):
                        sl = g * GS + j
                        S.op("act", lambda e, j=j, sl=sl: e.activation(out=dg[:, j, :], in_=ident[:], func=AF.Copy, scale=wgt[:, sl:sl + 1]),
                             r=[ident, wgt], w=[dg])
                    for j in range(GS):
                        sl = g * GS + j
                        for hf in range(2):
                            S.op("pe", lambda e, j=j, hf=hf: e.matmul(accp[:, hf, :], lhsT=dg[:, j, :], rhs=gbs[j][:, D + hf * 512:D + (hf + 1) * 512],
                                                                       start=(sl == 0), stop=(sl == 127)),
                                 r=[dg, gbs[j]], w=[accp], same_ok=True)
                    del sel["gbs"][g]

                def finish(sel, tt):
                    x, accp = sel["x"], sel["accp"]
                    y = yr.next()
                    for hf in range(2):
                        S.op("dve", lambda e, hf=hf: e.scalar_tensor_tensor(out=y[:, hf * 512:(hf + 1) * 512], in0=x[:, hf * 512:(hf + 1) * 512],
                                                                             scalar=ALPHA, in1=accp[:, hf, :], op0=ALU.mult, op1=ALU.add),
                             r=[x, accp], w=[y])
                    ln.run(y, tt, x_dst)

                cur = {}
                for _ in select(0, cur):
                    pass
                prev = None
                for tt in range(NT):
                    nxt = {}
                    gen = select(tt + 1, nxt) if tt + 1 < NT else None
                    cur["accp"] = accr.next()
                    cur["gbs"] = {}
                    pend = None
                    for g in range(NG):
                        stage_a(cur, g, 0, 2)
                        if pend is not None:
                            stage_b1(cur, pend)
                        stage_a(cur, g, 2, 5)
                        if pend is not None:
                            stage_b2(cur, pend)
                        stage_a(cur, g, 5, GS)
                        if pend is not None:
                            stage_b2b(cur, pend)
                        pend = g
                        if g == 1 and prev is not None:
                            finish(*prev)
                            prev = None
                        if gen is not None:
                            if next(gen, "done") == "done":
                                gen = None
                    stage_b1(cur, pend)
                    stage_b2(cur, pend)
                    stage_b2b(cur, pend)
                    if gen is not None:
                        for _ in gen:
                            pass
                    prev = (cur, tt)
                    cur = nxt
                finish(*prev)
                S.barrier()

        def phase_tok_proj_rope(w_ap, scale, dstT, v_ap=None):
            with ExitStack() as st:
                xT = load_xT_all(st)
                dcos = sb(st, "t_cos", [128, 16, 8], F32)
                dsin = sb(st, "t_sin", [128, 16, 8], F32)
                S.dma("sp", lambda e: e.dma_start(out=dcos[:], in_=cst["dcos"]), w=[dcos])
                S.dma("sp", lambda e: e.dma_start(out=dsin[:], in_=cst["dsin"]), w=[dsin])
                nW = 2 if v_ap is not None else 1
                wsb = sb(st, "t_w", [128, 8, 1024 * nW], BF16)
                with ExitStack() as st2:
                    wl = WLoader(st2, 8, 512, 2, "tw")
                    for wi, wa in enumerate([w_ap, v_ap][:nW]):
                        for hf in range(2):
                            wb = wl.load(wa[:, hf * 512:(hf + 1) * 512], 512)
                            S.op("dve", lambda e: e.tensor_copy(out=wsb[:, :, wi * 1024 + hf * 512:wi * 1024 + (hf + 1) * 512], in_=wb[:]),
                                 r=[wb], w=[wsb])
                    S.barrier()
                ppr = rot(st, "t_pp", [128, 2, 512], F32, 2, psum=True)
                pTr = rot(st, "t_pT", [128, 8, 128], BF16, 2, psum=True)
                kfr = rot(st, "t_kf", [128, 16, 64], F32, 2)
                tmr = [rot(st, f"t_tm{i}", [128, 16, 8], F32, 2) for i in range(4)]
                kbr = rot(st, "t_kb", [128, D], BF16, 2)
                kTr = rot(st, "t_kT", [128, 8, 128], BF16, 2)
                vbr = rot(st, "t_vb", [128, D], BF16, 2)
                for tt in range(NT):
                    pp, kf, kb, pT, kT = ppr.next(), kfr.next(), kbr.next(), pTr.next(), kTr.next()
                    for hf in range(2):
                        for kc in range(8):
                            S.op("pe", lambda e, kc=kc, hf=hf: e.matmul(pp[:, hf, :], lhsT=xT[:, kc, tt * 128:(tt + 1) * 128],
                                                                         rhs=wsb[:, kc, hf * 512:(hf + 1) * 512], start=(kc == 0), stop=(kc == 7)),
                                 r=[xT, wsb], w=[pp], same_ok=True)
                    kff = kf[:].rearrange("p a b -> p (a b)")
                    for hf in range(2):
                        S.op("act", lambda e, hf=hf: e.mul(out=kff[:, hf * 512:(hf + 1) * 512], in_=pp[:, hf, :], mul=scale), r=[pp], w=[kf])
                    pt = tt % 16
                    cs = dcos[:, pt, :].unsqueeze(1).to_broadcast([128, 16, 8])
                    sn = dsin[:, pt, :].unsqueeze(1).to_broadcast([128, 16, 8])
                    t1, t2, t3, t4 = [r_.next() for r_ in tmr]
                    x1, x2 = kf[:, :, 0:8], kf[:, :, 8:16]
                    S.op("dve", lambda e: e.tensor_tensor(out=t1[:], in0=x1, in1=cs, op=ALU.mult), r=[kf, dcos], w=[t1])
                    S.op("dve", lambda e: e.tensor_tensor(out=t2[:], in0=x2, in1=sn, op=ALU.mult), r=[kf, dsin], w=[t2])
                    S.op("dve", lambda e: e.tensor_tensor(out=t3[:], in0=x1, in1=sn, op=ALU.mult), r=[kf, dsin], w=[t3])
                    S.op("dve", lambda e: e.tensor_tensor(out=t4[:], in0=x2, in1=cs, op=ALU.mult), r=[kf, dcos], w=[t4])
                    S.op("dve", lambda e: e.tensor_tensor(out=x1, in0=t1[:], in1=t2[:], op=ALU.subtract), r=[t1, t2], w=[kf])
                    S.op("dve", lambda e: e.tensor_tensor(out=x2, in0=t3[:], in1=t4[:], op=ALU.add), r=[t3, t4], w=[kf])
                    S.op("act", lambda e: e.copy(out=kb[:], in_=kff), r=[kf], w=[kb])
                    for c in range(8):
                        S.op("pe", lambda e, c=c: e.transpose(pT[:, c, :], kb[:, c * 128:(c + 1) * 128], ident[:]), r=[kb, ident], w=[pT], same_ok=True)
                    S.op("dve", lambda e: e.tensor_copy(out=kT[:], in_=pT[:]), r=[pT], w=[kT])
                    S.dma("sp", lambda e: e.dma_start(out=fm(dstT)[:, :, tt * 128:(tt + 1) * 128], in_=kT[:]), r=[kT])
                    if v_ap is not None:
                        pv, vb = ppr.next(), vbr.next()
                        for hf in range(2):
                            for kc in range(8):
                                S.op("pe", lambda e, kc=kc, hf=hf: e.matmul(pv[:, hf, :], lhsT=xT[:, kc, tt * 128:(tt + 1) * 128],
                                                                             rhs=wsb[:, kc, 1024 + hf * 512:1024 + (hf + 1) * 512],
                                                                             start=(kc == 0), stop=(kc == 7)),
                                     r=[xT, wsb], w=[pv], same_ok=True)
                        for hf in range(2):
                            copy_any(vb[:, hf * 512:(hf + 1) * 512], pv[:, hf, :], [pv], [vb])
                        S.dma("sp", lambda e: e.dma_start(out=VS[tt * 128:(tt + 1) * 128, :], in_=vb[:]), r=[vb])
                S.barrier()

        def phase_attn(j, layer_idx):
            lam_init = 0.8 - 0.6 * math.exp(-0.3 * layer_idx)
            with ExitStack() as st:
                lamt = sb(st, "a_lamt", [128, 256], F32)
                lj = sb(st, "a_lj", [128, 64], F32)
                lv = sb(st, "a_lv", [128, 8], F32)
                gsub = sb(st, "a_gsub", [128, 128], F32)
                tri = sb(st, "a_tri", [128, 128], BF16)
                S.dma("sp", lambda e: e.dma_start(out=lamt[:], in_=diff_lambda[j].partition_broadcast(128)), w=[lamt])
                S.dma("sp", lambda e: e.dma_start(out=gsub[:], in_=diff_subln_g[j].partition_broadcast(128)), w=[gsub])
                S.dma("sp", lambda e: e.dma_start(out=tri[:], in_=cst["tri"]), w=[tri])
                S.op("dve", lambda e: e.scalar_tensor_tensor(out=lj[:], in0=lamt[:, 0:64], scalar=1.0, in1=lamt[:, 64:128], op0=ALU.mult, op1=ALU.mult,
                                                              accum_out=lv[:, 0:1]), r=[lamt], w=[lj, lv])
                S.op("dve", lambda e: e.scalar_tensor_tensor(out=lj[:], in0=lamt[:, 128:192], scalar=1.0, in1=lamt[:, 192:256], op0=ALU.mult, op1=ALU.mult,
                                                              accum_out=lv[:, 1:2]), r=[lamt], w=[lj, lv])
                S.op("act", lambda e: e.activation(out=lv[:, 2:4], in_=lv[:, 0:2], func=AF.Exp), r=[lv], w=[lv])
                S.op("dve", lambda e: e.tensor_tensor(out=lv[:, 4:5], in0=lv[:, 3:4], in1=lv[:, 2:3], op=ALU.subtract), r=[lv], w=[lv])
                S.op("dve", lambda e: e.tensor_scalar(out=lv[:, 5:6], in0=lv[:, 4:5], scalar1=-lam_init, scalar2=None, op0=ALU.add), r=[lv], w=[lv])
                S.op("dve", lambda e: e.tensor_scalar(out=gsub[:], in0=gsub[:], scalar1=(1.0 - lam_init), scalar2=None, op0=ALU.mult), r=[gsub], w=[gsub])
                neglam = lv[:, 5:6]

                kTr = rot(st, "a_kT", [128, SEQ], BF16, 2)
                qTr = rot(st, "a_qT", [128, SEQ], BF16, 2)
                vhr = rot(st, "a_vh", [128, 16, 129], BF16, 2)
                for t_ in vhr.tiles:
                    S.op("pool", lambda e, t_=t_: e.memset(t_[:, :, 128:129], 1.0), w=[t_])
                o1b = sb(st, "a_o1", [128, 16, 128], F32)
                oat = rot(st, "a_oat", [128, 16, 128], BF16, 2)
                Er = rot(st, "a_E", [128, 512], BF16, 3)
                tmpr = rot(st, "a_tmp", [128, 128], F32, 2)
                o2r = rot(st, "a_o2", [128, 128], F32, 2)
                jr = rot(st, "a_j", [128, 128], F32, 2)
                rsr = rot(st, "a_rs", [128, 4], F32, 4)
                spr = rot(st, "a_sp", [128, 512], F32, 3, psum=True)
                accp = ps(st, "a_acc", [128, 4, 512], F32)
                accb = [Buf(f"a_accb{i}") for i in range(4)]

                def loads(s, h):
                    kT, qT, vh = kTr.next(), qTr.next(), vhr.next()
                    S.dma("sp", lambda e: e.dma_start(out=kT[:], in_=fm(KTS)[:, h, s * SEQ:(s + 1) * SEQ]), w=[kT])
                    S.dma("sp", lambda e: e.dma_start(out=qT[:], in_=fm(QTD)[:, h, s * SEQ:(s + 1) * SEQ]), w=[qT])
                    S.dma("sp", lambda e: e.dma_start(out=vh[:, :, 0:128], in_=tm(VS)[:, s * 16:(s + 1) * 16, h * 128:(h + 1) * 128]), w=[vh])
                    return kT, qT, vh
                items = [(s, h) for s in range(nseq) for h in range(8)]
                nxt = loads(*items[0])
                for idx, (s, h) in enumerate(items):
                    kT, qT, vh = nxt
                    if idx + 1 < len(items):
                        nxt = loads(*items[idx + 1])
                    ot = oat.next()
                    for m in range(2):
                        msl = slice(m * 64, (m + 1) * 64)
                        for G in range(4):
                            nkt = 4 * G + 4

                            def score(kt):
                                qi0 = max(0, kt - 4 * G)
                                sp_ = spr.next()
                                qs = slice(G * 512 + qi0 * 128, (G + 1) * 512)
                                es_ = slice(qi0 * 128, 512)
                                S.op("pe", lambda e: e.matmul(sp_[:, es_], lhsT=kT[msl, kt * 128:(kt + 1) * 128], rhs=qT[msl, qs], start=True, stop=True),
                                     r=[kT, qT], w=[sp_], same_ok=True)
                                return sp_
                            sp_next = score(0)
                            for kt in range(nkt):
                                qi0 = max(0, kt - 4 * G)
                                sp_, E = sp_next, Er.next()
                                es_ = slice(qi0 * 128, 512)
                                if kt + 1 < nkt:
                                    sp_next = score(kt + 1)
                                S.op("act", lambda e: e.activation(out=E[:, es_], in_=sp_[:, es_], func=AF.Exp), r=[sp_], w=[E])
                                if kt >= 4 * G:
                                    ds_ = slice(qi0 * 128, (qi0 + 1) * 128)
                                    S.op("dve", lambda e: e.tensor_tensor(out=E[:, ds_], in0=E[:, ds_], in1=tri[:], op=ALU.mult), r=[E, tri], w=[E])
                                for qi in range(qi0, 4):
                                    S.op("pe", lambda e, qi=qi: e.matmul(accp[:, qi, 0:129], lhsT=E[:, qi * 128:(qi + 1) * 128], rhs=vh[:, kt, :],
                                                                          start=(kt == 0), stop=(kt == 4 * G + qi)),
                                         r=[E, vh], w=[accb[qi]], same_ok=True)
                            for qi in range(4):
                                qt_ = 4 * G + qi
                                rs = rsr.next()
                                S.op("dve", lambda e: e.reciprocal(out=rs[:, 0:1], in_=accp[:, qi, 128:129]), r=[accb[qi]], w=[rs])
                                if m == 0:
                                    S.op("dve", lambda e: e.tensor_scalar(out=o1b[:, qt_, :], in0=accp[:, qi, 0:128], scalar1=rs[:, 0:1], scalar2=None,
                                                                          op0=ALU.mult), r=[accb[qi], rs], w=[o1b])
                                else:
                                    tmp, o2, jj = tmpr.next(), o2r.next(), jr.next()
                                    S.op("dve", lambda e: e.tensor_scalar(out=tmp[:], in0=accp[:, qi, 0:128], scalar1=rs[:, 0:1], scalar2=None,
                                                                          op0=ALU.mult), r=[accb[qi], rs], w=[tmp])
                                    S.op("dve", lambda e: e.scalar_tensor_tensor(out=o2[:], in0=tmp[:], scalar=neglam, in1=o1b[:, qt_, :],
                                                                                  op0=ALU.mult, op1=ALU.add), r=[tmp, lv, o1b], w=[o2])
                                    S.op("dve", lambda e: e.scalar_tensor_tensor(out=jj[:], in0=o2[:], scalar=1.0, in1=o2[:], op0=ALU.mult, op1=ALU.mult,
                                                                                  accum_out=rs[:, 1:2]), r=[o2], w=[jj, rs])
                                    S.op("dve", lambda e: e.tensor_scalar(out=rs[:, 2:3], in0=rs[:, 1:2], scalar1=1.0 / 128.0, scalar2=EPS,
                                                                          op0=ALU.mult, op1=ALU.add), r=[rs], w=[rs])
                                    S.op("act", lambda e: e.activation(out=rs[:, 2:3], in_=rs[:, 2:3], func=AF.Sqrt), r=[rs], w=[rs])
                                    S.op("dve", lambda e: e.reciprocal(out=rs[:, 3:4], in_=rs[:, 2:3]), r=[rs], w=[rs])
                                    S.op("dve", lambda e: e.scalar_tensor_tensor(out=ot[:, qt_, :], in0=o2[:], scalar=rs[:, 3:4], in1=gsub[:],
                                                                                   op0=ALU.mult, op1=ALU.mult), r=[o2, rs, gsub], w=[ot])
                    S.dma("sp", lambda e: e.dma_start(out=tm(OATT)[:, s * 16:(s + 1) * 16, h * 128:(h + 1) * 128], in_=ot[:]), r=[ot])
                S.barrier()

        steps = []
        phase_init()
        phase_uv_prep()
        cur = x_in
        for l in range(DEPTH):
            last = (l == DEPTH - 1)
            if l < 2:
                steps.append(lambda l=l, cur=cur: (phase_ret_proj(l), phase_ret_core(),
                                                   phase_outproj_ln(Z, 2048, ret_w_out[l], cur, X, l, 0)))
            else:
                steps.append(lambda l=l, cur=cur: (phase_tok_proj_rope(diff_w_q[l - 2], 0.125, QTD), phase_attn(l - 2, l),
                                                   phase_outproj_ln(OATT, 1024, diff_w_out[l - 2], cur, X, l, 0)))
            cur = X
            steps.append(lambda l=l, last=last: (phase_peer_q(l), phase_peer_main(l, X, out if last else X)))
            if l == 1:
                steps.append(lambda: phase_tok_proj_rope(kv_w[:, 0:1024], 1.0, KTS, v_ap=kv_w[:, 1024:2048]))
        for i, f in enumerate(steps):
            if i < n_steps:
                f()
        S.barrier()
        print("instructions issued:", S.n_ins, {k: c.count for k, c in S.ctr.items()})
    return nc


_CACHE = {}


def _in_maps(inputs, nseq, n_cores):
    c = make_consts()
    x = np.asarray(inputs["x"], dtype=np.float32).reshape(-1, D)
    skT = np.ascontiguousarray(np.asarray(inputs["peer_subkeys"], dtype=np.float32).transpose(0, 4, 1, 2, 3).reshape(4, 128, 16, 128))
    shared = {
        "ret_w_in": np.asarray(inputs["ret_w_in"], np.float32), "ret_w_out": np.asarray(inputs["ret_w_out"], np.float32),
        "kv_w": np.asarray(inputs["kv_w"], np.float32), "diff_w_q": np.asarray(inputs["diff_w_q"], np.float32),
        "diff_lambda": np.asarray(inputs["diff_lambda"], np.float32).reshape(2, 256),
        "diff_subln_g": np.asarray(inputs["diff_subln_g"], np.float32), "diff_w_out": np.asarray(inputs["diff_w_out"], np.float32),
        "peer_w_q": np.asarray(inputs["peer_w_q"], np.float32), "peer_skT": skT,
        "peer_u": np.asarray(inputs["peer_u"], np.float32), "peer_v": np.asarray(inputs["peer_v"], np.float32),
        "ln_g": np.asarray(inputs["ln_g"], np.float32), "ln_b": np.asarray(inputs["ln_b"], np.float32),
    }
    for k in CONST_SPECS:
        shared["c_" + k] = c[k]
    maps = []
    tk = nseq * SEQ
    for i in range(n_cores):
        m = dict(shared)
        m["x"] = np.ascontiguousarray(x[i * tk:(i + 1) * tk])
        maps.append(m)
    return maps


def kernel(**inputs):
    nseq = 2
    if "nc" not in _CACHE:
        _CACHE["nc"] = build_program(nseq=nseq)
    nc = _CACHE["nc"]
    maps = _in_maps(inputs, nseq, N_CORES)
    res = run_bass_kernel_spmd(nc, maps, core_ids=list(range(N_CORES)))
    outs = [np.asarray(r["out"], dtype=np.float32) for r in res.results]
    return np.concatenate(outs, axis=0).reshape(16, SEQ, D)
```

```python
import math
from contextlib import ExitStack

import numpy as np
import ml_dtypes
import concourse.bass as bass
import concourse.mybir as mybir
from concourse.bass_utils import run_bass_kernel_spmd

F32 = mybir.dt.float32
BF16 = mybir.dt.bfloat16
I32 = mybir.dt.int32
U32 = mybir.dt.uint32
AF = mybir.ActivationFunctionType
ALU = mybir.AluOpType
AX = mybir.AxisListType

D = 1024
SEQ = 2048
DEPTH = 4
ALPHA = (2 * DEPTH) ** 0.25
EPS = 1e-5
NEXP = 16384
N_CORES = 8


class Ctr:
    def __init__(self, sem, step):
        self.sem = sem
        self.step = step
        self.count = 0


class Buf:
    def __init__(self, name=""):
        self.name = name
        self.w = None
        self.r = {}


class T:
    def __init__(self, t, name):
        self.t = t
        self.b = Buf(name)

    def __getitem__(self, k):
        return self.t[k]


def _b(x):
    return x.b if hasattr(x, "b") else x


class Sched:
    def __init__(self, nc, es, n_dma_sems=(12, 4, 12)):
        self.nc = nc
        self.engs = {"pe": nc.tensor, "dve": nc.vector, "act": nc.scalar, "pool": nc.gpsimd, "sp": nc.sync}
        self.ctr = {}
        for k in ("pe", "dve", "act", "pool"):
            self.ctr[k] = Ctr(es.enter_context(nc.semaphore("c_" + k)), 1)
        self.dq = {}
        for k, n in zip(("sp", "act", "pool"), n_dma_sems):
            self.dq[k] = [Ctr(es.enter_context(nc.semaphore(f"d_{k}{i}")), 16) for i in range(n)]
        self.dq_i = {"sp": 0, "act": 0, "pool": 0}
        self.bar_sem = es.enter_context(nc.semaphore("bar"))
        self.bar_n = 0
        self.waited = {k: {} for k in ("pe", "dve", "act", "pool", "sp")}
        self.n_ins = 0

    def _wait(self, ek, deps):
        best = {}
        for c, v in deps:
            if v > best.get(c, 0):
                best[c] = v
        for c, v in best.items():
            if self.waited[ek].get(c, 0) >= v:
                continue
            self.engs[ek].wait_ge(c.sem, v)
            self.waited[ek][c] = v
            self.n_ins += 1

    def _deps(self, r, w, skip=None):
        deps = []
        for b in r:
            if b.w is not None:
                deps.append(b.w)
        for b in w:
            if b.w is not None:
                deps.append(b.w)
            for c, v in b.r.items():
                deps.append((c, v))
        if skip is not None:
            deps = [(c, v) for c, v in deps if c is not skip]
        return deps

    def _mark(self, r, w, c, v):
        for b in w:
            b.w = (c, v)
            b.r = {}
        for b in r:
            if b.r.get(c, 0) < v:
                b.r[c] = v

    def op(self, ek, fn, r=(), w=(), same_ok=False):
        r = [_b(x) for x in r]
        w = [_b(x) for x in w]
        c = self.ctr[ek]
        self._wait(ek, self._deps(r, w, skip=c if same_ok else None))
        ins = fn(self.engs[ek])
        c.count += 1
        ins.then_inc(c.sem, 1)
        self.n_ins += 1
        self._mark(r, w, c, c.count)
        return ins

    def dma(self, qk, fn, r=(), w=()):
        r = [_b(x) for x in r]
        w = [_b(x) for x in w]
        lst = self.dq[qk]
        i = self.dq_i[qk]
        self.dq_i[qk] = (i + 1) % len(lst)
        c = lst[i]
        deps = self._deps(r, w)
        if c.count > 0:
            deps.append((c, c.count))
        self._wait(qk, deps)
        ins = fn(self.engs[qk])
        c.count += 16
        ins.then_inc(c.sem, 16)
        self.n_ins += 1
        self._mark(r, w, c, c.count)
        return ins

    def barrier(self):
        deps = []
        for k in ("pe", "dve", "act", "pool"):
            c = self.ctr[k]
            if c.count:
                deps.append((c, c.count))
        for k in self.dq:
            for c in self.dq[k]:
                if c.count:
                    deps.append((c, c.count))
        self._wait("sp", deps)
        self.bar_n += 1
        self.engs["sp"].sem_inc(self.bar_sem, 1)
        for k in ("pe", "dve", "act", "pool"):
            self.engs[k].wait_ge(self.bar_sem, self.bar_n)
            for c, v in deps:
                self.waited[k][c] = v
        for c, v in deps:
            self.waited["sp"][c] = v


class Rot:
    def __init__(self, tiles):
        self.tiles = tiles
        self.i = 0

    def next(self):
        t = self.tiles[self.i % len(self.tiles)]
        self.i += 1
        return t


def make_consts():
    c = {}
    pos = np.arange(SEQ, dtype=np.float32)
    ret_freqs = (1.0 / (np.float32(10000.0) ** np.linspace(0.0, 1.0, 128, dtype=np.float32))).astype(np.float32)
    ang = (pos[:, None] * ret_freqs[None, :]).astype(np.float32)
    c["ret_cos"] = np.ascontiguousarray(np.cos(ang).T).astype(np.float32)
    c["ret_sin"] = np.ascontiguousarray(np.sin(ang).T).astype(np.float32)
    log_g = np.log(1.0 - np.exp2(-5.0 - np.arange(4, dtype=np.float32))).astype(np.float32)
    ar = np.arange(128, dtype=np.float32)
    rel = ar[:, None] - ar[None, :]
    dm = np.where(rel[None] >= 0, np.exp(np.maximum(rel, 0.0)[None] * log_g[:, None, None]), 0.0)
    c["maskT"] = np.ascontiguousarray(dm.transpose(2, 0, 1)).astype(np.float32)
    qd = np.exp((ar + 1.0)[None] * log_g[:, None]).astype(np.float32)
    c["qdec"] = np.ascontiguousarray(np.broadcast_to(qd[None], (128, 4, 128))).astype(np.float32)
    kd = np.exp((128 - 1.0 - ar)[None] * log_g[:, None]).astype(np.float32)
    c["kdec"] = np.ascontiguousarray(kd.T).astype(np.float32)
    c["cdec"] = [float(x) for x in np.exp(128 * log_g)]
    dfreq = (np.float32(500000.0) ** (-np.arange(0, 16, 2, dtype=np.float32) / 16)).astype(np.float32)
    dang = (pos[:, None] * dfreq[None, :]).astype(np.float32)
    c["dcos"] = np.ascontiguousarray(np.cos(dang).reshape(16, 128, 8).transpose(1, 0, 2)).astype(np.float32)
    c["dsin"] = np.ascontiguousarray(np.sin(dang).reshape(16, 128, 8).transpose(1, 0, 2)).astype(np.float32)
    c["tri"] = (ar[None, :] >= ar[:, None]).astype(np.float32).astype(ml_dtypes.bfloat16)
    c["ident"] = np.eye(128, dtype=np.float32).astype(ml_dtypes.bfloat16)
    io = np.arange(16, dtype=np.float32)
    c["iota16"] = np.ascontiguousarray(np.broadcast_to(np.stack([io, io * 16.0])[None], (128, 2, 16))).astype(np.float32)
    return c


CONST_SPECS = {
    "ret_cos": ([128, SEQ], F32), "ret_sin": ([128, SEQ], F32), "maskT": ([128, 4, 128], F32),
    "qdec": ([128, 4, 128], F32), "kdec": ([128, 4], F32), "dcos": ([128, 16, 8], F32),
    "dsin": ([128, 16, 8], F32), "tri": ([128, 128], BF16), "ident": ([128, 128], BF16),
    "iota16": ([128, 2, 16], F32),
}


def build_program(nseq=2, n_steps=99, dbg=False):
    Tk = nseq * SEQ
    NT = Tk // 128
    NTB = Tk // 512
    nc = bass.Bass("TRN2", target_bir_lowering=False)
    cdec = make_consts()["cdec"]

    def din(name, shape, dt=F32):
        return nc.dram_tensor(name, shape, dt, kind="ExternalInput").ap()

    def dscr(name, shape, dt):
        return nc.dram_tensor(name, shape, dt, kind="ExternalOutput" if dbg else "Internal").ap()

    x_in = din("x", [Tk, D])
    ret_w_in = din("ret_w_in", [2, D, 6144])
    ret_w_out = din("ret_w_out", [2, 2048, D])
    kv_w = din("kv_w", [D, 2048])
    diff_w_q = din("diff_w_q", [2, D, D])
    diff_lambda = din("diff_lambda", [2, 256])
    diff_subln_g = din("diff_subln_g", [2, 128])
    diff_w_out = din("diff_w_out", [2, D, D])
    peer_w_q = din("peer_w_q", [4, D, 2048])
    peer_skT = din("peer_skT", [4, 128, 16, 128])
    peer_u = din("peer_u", [4, NEXP, D])
    peer_v = din("peer_v", [4, NEXP, D])
    peer_u_flat = peer_u.rearrange("l n d -> (l n) d")
    peer_v_flat = peer_v.rearrange("l n d -> (l n) d")
    ln_g = din("ln_g", [4, 2, D])
    ln_b = din("ln_b", [4, 2, D])
    cst = {k: din("c_" + k, shp, dt) for k, (shp, dt) in CONST_SPECS.items()}
    out = nc.dram_tensor("out", [Tk, D], F32, kind="ExternalOutput").ap()

    X = dscr("X", [Tk, D], F32)
    XT = dscr("XT", [D, Tk], BF16)
    QT = dscr("QT", [D, Tk], BF16)
    QDT = dscr("QDT", [D, Tk], BF16)
    KT = dscr("KT", [D, Tk], BF16)
    KD = dscr("KD", [Tk, D], BF16)
    V = dscr("V", [Tk, 2048], BF16)
    SG = dscr("SG", [Tk, 2048], BF16)
    Z = dscr("Z", [Tk, 2048], BF16)
    QP = dscr("QP", [2048, Tk], BF16)
    KTS = dscr("KTS", [D, Tk], BF16)
    VS = dscr("VS", [Tk, D], BF16)
    QTD = dscr("QTD", [D, Tk], BF16)
    OATT = dscr("OATT", [Tk, D], BF16)
    UVB = nc.dram_tensor("UVB", [4 * NEXP, 2048], BF16, kind="Internal").ap()

    def fm(ap):
        return ap.rearrange("(c p) t -> p c t", p=128)

    def tm(ap):
        return ap.rearrange("(n p) f -> p n f", p=128)

    with ExitStack() as es:
        S = Sched(nc, es)

        uid = {"n": 0}

        def sb(st, name, shape, dt):
            uid["n"] += 1
            name = f"{name}_{uid['n']}"
            return T(st.enter_context(nc.sbuf_tensor(name, shape, dt)), name)

        def ps(st, name, shape, dt):
            uid["n"] += 1
            name = f"{name}_{uid['n']}"
            return T(st.enter_context(nc.psum_tensor(name, shape, dt)), name)

        def rot(st, name, shape, dt, n, psum=False):
            mk = ps if psum else sb
            return Rot([mk(st, f"{name}{i}", shape, dt) for i in range(n)])

        ident = sb(es, "ident", [128, 128], BF16)
        S.dma("sp", lambda e: e.dma_start(out=ident[:], in_=cst["ident"]), w=[ident])

        flip = {"i": 0}

        def copy_any(out_ap, in_ap, r, w, engines=("act", "dve")):
            ek = engines[flip["i"] % len(engines)]
            flip["i"] += 1
            if ek == "act":
                S.op("act", lambda e: e.copy(out=out_ap, in_=in_ap), r=r, w=w)
            else:
                S.op(ek, lambda e: e.tensor_copy(out=out_ap, in_=in_ap), r=r, w=w)

        def load_xT_all(st):
            xT = sb(st, "xT_all", [128, 8, Tk], BF16)
            for c in range(8):
                S.dma("sp", lambda e, c=c: e.dma_start(out=xT[:, c, :], in_=fm(XT)[:, c, :]), w=[xT])
            return xT

        class WLoader:
            def __init__(self, st, kc, wmax, nbuf=2, name="wl"):
                self.kc = kc
                self.wf = rot(st, name + "f", [128, kc, wmax], F32, nbuf)
                self.wb = rot(st, name + "b", [128, kc, wmax], BF16, nbuf)

            def load(self, w_ap, w):
                wf = self.wf.next()
                wb = self.wb.next()
                src = w_ap.rearrange("(c p) n -> p c n", p=128)
                half = self.kc // 2
                S.dma("sp", lambda e: e.dma_start(out=wf[:, 0:half, 0:w], in_=src[:, 0:half, :]), w=[wf])
                S.dma("sp", lambda e: e.dma_start(out=wf[:, half:, 0:w], in_=src[:, half:, :]), w=[wf])
                S.op("pool", lambda e: e.tensor_copy(out=wb[:, :, 0:w], in_=wf[:, :, 0:w]), r=[wf], w=[wb])
                return wb

        class LNEpilogue:
            def __init__(self, st, l, which, ew="pool", npt=2, nbuf=2):
                self.ew = ew
                self.gb = sb(st, "ln_gbc", [128, D], F32)
                self.bb = sb(st, "ln_bbc", [128, D], F32)
                S.dma("sp", lambda e: e.dma_start(out=self.gb[:], in_=ln_g[l, which].partition_broadcast(128)), w=[self.gb])
                S.dma("sp", lambda e: e.dma_start(out=self.bb[:], in_=ln_b[l, which].partition_broadcast(128)), w=[self.bb])
                self.st = rot(st, "ln_st", [128, 2, 6], F32, 2)
                self.mv = rot(st, "ln_mv", [128, 4], F32, 2)
                self.xn = rot(st, "ln_xn", [128, D], F32, nbuf)
                self.xo = rot(st, "ln_xo", [128, D], F32, nbuf)
                self.xb = rot(st, "ln_xb", [128, D], BF16, nbuf)
                self.xT = rot(st, "ln_xT", [128, 8, 128], BF16, nbuf)
                self.pT = rot(st, "ln_pT", [128, 8, 128], BF16, npt, psum=True)

            def run(self, y, tt, x_dst):
                st_, mv, xn, xo, xb, xT, pT = (self.st.next(), self.mv.next(), self.xn.next(), self.xo.next(),
                                               self.xb.next(), self.xT.next(), self.pT.next())
                S.op("dve", lambda e: e.bn_stats(out=st_[:, 0, :], in_=y[:, 0:512]), r=[y], w=[st_])
                S.op("dve", lambda e: e.bn_stats(out=st_[:, 1, :], in_=y[:, 512:1024]), r=[y], w=[st_])
                S.op("dve", lambda e: e.bn_aggr(out=mv[:, 0:2], in_=st_[:].rearrange("p a b -> p (a b)")), r=[st_], w=[mv])
                S.op("dve", lambda e: e.tensor_scalar(out=mv[:, 2:3], in0=mv[:, 1:2], scalar1=1.0, scalar2=EPS,
                                                      op0=ALU.mult, op1=ALU.add), r=[mv], w=[mv])
                S.op("act", lambda e: e.activation(out=mv[:, 2:3], in_=mv[:, 2:3], func=AF.Sqrt), r=[mv], w=[mv])
                S.op("dve", lambda e: e.reciprocal(out=mv[:, 3:4], in_=mv[:, 2:3]), r=[mv], w=[mv])
                S.op("dve", lambda e: e.tensor_scalar(out=xn[:], in0=y[:], scalar1=mv[:, 0:1], scalar2=mv[:, 3:4],
                                                      op0=ALU.subtract, op1=ALU.mult), r=[y, mv], w=[xn])
                S.op(self.ew, lambda e: e.tensor_tensor(out=xn[:], in0=xn[:], in1=self.gb[:], op=ALU.mult), r=[xn, self.gb], w=[xn])
                S.op(self.ew, lambda e: e.tensor_tensor(out=xo[:], in0=xn[:], in1=self.bb[:], op=ALU.add), r=[xn, self.bb], w=[xo])
                sq = "pool" if self.ew == "pool" else "sp"
                S.dma(sq, lambda e: e.dma_start(out=x_dst[tt * 128:(tt + 1) * 128, :], in_=xo[:]), r=[xo])
                S.op("act", lambda e: e.copy(out=xb[:], in_=xo[:]), r=[xo], w=[xb])
                for c in range(8):
                    S.op("pe", lambda e, c=c: e.transpose(pT[:, c, :], xb[:, c * 128:(c + 1) * 128], ident[:]),
                         r=[xb, ident], w=[pT], same_ok=True)
                S.op("act", lambda e: e.copy(out=xT[:], in_=pT[:]), r=[pT], w=[xT])
                S.dma(sq, lambda e: e.dma_start(out=fm(XT)[:, :, tt * 128:(tt + 1) * 128], in_=xT[:]), r=[xT])

        def phase_init():
            with ExitStack() as st:
                xf = rot(st, "i_xf", [128, D], F32, 2)
                xb = rot(st, "i_xb", [128, D], BF16, 2)
                xT = rot(st, "i_xT", [128, 8, 128], BF16, 2)
                pT = rot(st, "i_pT", [128, 8, 128], BF16, 2, psum=True)
                for tt in range(NT):
                    a, b_, c_, p = xf.next(), xb.next(), xT.next(), pT.next()
                    S.dma("sp", lambda e: e.dma_start(out=a[:], in_=x_in[tt * 128:(tt + 1) * 128, :]), w=[a])
                    S.op("dve", lambda e: e.tensor_copy(out=b_[:], in_=a[:]), r=[a], w=[b_])
                    for c in range(8):
                        S.op("pe", lambda e, c=c: e.transpose(p[:, c, :], b_[:, c * 128:(c + 1) * 128], ident[:]),
                             r=[b_, ident], w=[p], same_ok=True)
                    S.op("act", lambda e: e.copy(out=c_[:], in_=p[:]), r=[p], w=[c_])
                    S.dma("sp", lambda e: e.dma_start(out=fm(XT)[:, :, tt * 128:(tt + 1) * 128], in_=c_[:]), r=[c_])
                S.barrier()

        def phase_ret_proj(l):
            with ExitStack() as st:
                xT = load_xT_all(st)
                cosT = sb(st, "r_cos", [128, SEQ], F32)
                sinT = sb(st, "r_sin", [128, SEQ], F32)
                qdec = sb(st, "r_qdec", [128, 4, 128], F32)
                kdec = sb(st, "r_kdec", [128, 4], F32)
                S.dma("sp", lambda e: e.dma_start(out=cosT[:], in_=cst["ret_cos"]), w=[cosT])
                S.dma("sp", lambda e: e.dma_start(out=sinT[:], in_=cst["ret_sin"]), w=[sinT])
                S.dma("sp", lambda e: e.dma_start(out=qdec[:], in_=cst["qdec"]), w=[qdec])
                S.dma("sp", lambda e: e.dma_start(out=kdec[:], in_=cst["kdec"]), w=[kdec])
                wl = WLoader(st, 8, 512, 2, "rw")
                p1r = rot(st, "r_p1", [128, 512], F32, 2, psum=True)
                p2r = rot(st, "r_p2", [128, 512], F32, 2, psum=True)
                pkr = rot(st, "r_pk", [128, 8, 128], BF16, 2, psum=True)
                a1r = rot(st, "r_a1", [128, 512], F32, 2)
                a2r = rot(st, "r_a2", [128, 512], F32, 2)
                tr = [rot(st, f"r_t{i}", [128, 512], F32, 2) for i in range(4)]
                o1r = rot(st, "r_o1", [128, 512], F32, 2)
                o2r = rot(st, "r_o2", [128, 512], F32, 2)
                obr = rot(st, "r_ob", [128, 2, 512], BF16, 3)
                odr = rot(st, "r_od", [128, 2, 512], BF16, 3)
                kdr = rot(st, "r_kd", [128, 4, 256], BF16, 2)
                for kind in range(2):
                    for h in range(4):
                        c0 = kind * 1024 + h * 256
                        wb = wl.load(ret_w_in[l, :, c0:c0 + 256], 256)
                        for tb in range(NTB):
                            p1, p2 = p1r.next(), p2r.next()
                            tsl = slice(tb * 512, (tb + 1) * 512)
                            for kc in range(8):
                                S.op("pe", lambda e, kc=kc: e.matmul(p1[:], lhsT=wb[:, kc, 0:128], rhs=xT[:, kc, tsl],
                                                                      start=(kc == 0), stop=(kc == 7)),
                                     r=[wb, xT], w=[p1], same_ok=True)
                            for kc in range(8):
                                S.op("pe", lambda e, kc=kc: e.matmul(p2[:], lhsT=wb[:, kc, 128:256], rhs=xT[:, kc, tsl],
                                                                      start=(kc == 0), stop=(kc == 7)),
                                     r=[wb, xT], w=[p2], same_ok=True)
                            a1, a2 = a1r.next(), a2r.next()
                            sc = 1.0 if kind == 0 else 1.0 / 16.0
                            S.op("act", lambda e: e.mul(out=a1[:], in_=p1[:], mul=sc), r=[p1], w=[a1])
                            S.op("act", lambda e: e.mul(out=a2[:], in_=p2[:], mul=sc), r=[p2], w=[a2])
                            p0 = (tb * 512) % SEQ
                            cs, sn = cosT[:, p0:p0 + 512], sinT[:, p0:p0 + 512]
                            t1, t2, t3, t4 = [r_.next() for r_ in tr]
                            S.op("dve", lambda e: e.tensor_tensor(out=t1[:], in0=a1[:], in1=cs, op=ALU.mult), r=[a1, cosT], w=[t1])
                            S.op("pool", lambda e: e.tensor_tensor(out=t2[:], in0=a2[:], in1=sn, op=ALU.mult), r=[a2, sinT], w=[t2])
                            S.op("dve", lambda e: e.tensor_tensor(out=t3[:], in0=a1[:], in1=sn, op=ALU.mult), r=[a1, sinT], w=[t3])
                            S.op("pool", lambda e: e.tensor_tensor(out=t4[:], in0=a2[:], in1=cs, op=ALU.mult), r=[a2, cosT], w=[t4])
                            ob = obr.next()
                            if kind == 0:
                                o1, o2, od = o1r.next(), o2r.next(), odr.next()
                                S.op("dve", lambda e: e.tensor_tensor(out=o1[:], in0=t1[:], in1=t2[:], op=ALU.subtract), r=[t1, t2], w=[o1])
                                S.op("pool", lambda e: e.tensor_tensor(out=o2[:], in0=t3[:], in1=t4[:], op=ALU.add), r=[t3, t4], w=[o2])
                                S.op("act", lambda e: e.copy(out=ob[:, 0, :], in_=o1[:]), r=[o1], w=[ob])
                                S.op("act", lambda e: e.copy(out=ob[:, 1, :], in_=o2[:]), r=[o2], w=[ob])
                                qd_b = qdec[:, h, :].unsqueeze(1).to_broadcast([128, 4, 128])
                                S.op("dve", lambda e: e.tensor_tensor(out=od[:, 0, :].rearrange("p (a b) -> p a b", a=4),
                                                                      in0=o1[:].rearrange("p (a b) -> p a b", a=4), in1=qd_b, op=ALU.mult),
                                     r=[o1, qdec], w=[od])
                                S.op("pool", lambda e: e.tensor_tensor(out=od[:, 1, :].rearrange("p (a b) -> p a b", a=4),
                                                                       in0=o2[:].rearrange("p (a b) -> p a b", a=4), in1=qd_b, op=ALU.mult),
                                     r=[o2, qdec], w=[od])
                                S.dma("sp", lambda e: e.dma_start(out=fm(QT)[:, 2 * h:2 * h + 2, tsl], in_=ob[:]), r=[ob])
                                S.dma("sp", lambda e: e.dma_start(out=fm(QDT)[:, 2 * h:2 * h + 2, tsl], in_=od[:]), r=[od])
                            else:
                                S.op("dve", lambda e: e.tensor_tensor(out=ob[:, 0, :], in0=t1[:], in1=t2[:], op=ALU.subtract), r=[t1, t2], w=[ob])
                                S.op("pool", lambda e: e.tensor_tensor(out=ob[:, 1, :], in0=t3[:], in1=t4[:], op=ALU.add), r=[t3, t4], w=[ob])
                                S.dma("sp", lambda e: e.dma_start(out=fm(KT)[:, 2 * h:2 * h + 2, tsl], in_=ob[:]), r=[ob])
                                pk, kd = pkr.next(), kdr.next()
                                for i in range(4):
                                    for j in range(2):
                                        S.op("pe", lambda e, i=i, j=j: e.transpose(pk[:, i * 2 + j, :], ob[:, j, i * 128:(i + 1) * 128], ident[:]),
                                             r=[ob, ident], w=[pk], same_ok=True)
                                S.op("act", lambda e: e.activation(out=kd[:].rearrange("p a b -> p (a b)"),
                                                                   in_=pk[:].rearrange("p a b -> p (a b)"), func=AF.Copy,
                                                                   scale=kdec[:, h:h + 1]), r=[pk, kdec], w=[kd])
                                S.dma("sp", lambda e: e.dma_start(out=tm(KD)[:, tb * 4:(tb + 1) * 4, h * 256:(h + 1) * 256], in_=kd[:]), r=[kd])
                pvr = rot(st, "r_pv", [128, 512], F32, 2, psum=True)
                ovr = rot(st, "r_ov", [128, 512], BF16, 3)
                for kind in range(2):
                    for cg in range(4):
                        c0 = 2048 + kind * 2048 + cg * 512
                        wb = wl.load(ret_w_in[l, :, c0:c0 + 512], 512)
                        dst = V if kind == 0 else SG
                        for tt in range(NT):
                            pv, ov = pvr.next(), ovr.next()
                            for kc in range(8):
                                S.op("pe", lambda e, kc=kc: e.matmul(pv[:], lhsT=xT[:, kc, tt * 128:(tt + 1) * 128], rhs=wb[:, kc, :],
                                                                      start=(kc == 0), stop=(kc == 7)),
                                     r=[wb, xT], w=[pv], same_ok=True)
                            if kind == 0:
                                copy_any(ov[:], pv[:], [pv], [ov])
                            else:
                                S.op("act", lambda e: e.activation(out=ov[:], in_=pv[:], func=AF.Silu), r=[pv], w=[ov])
                            S.dma("sp", lambda e: e.dma_start(out=dst[tt * 128:(tt + 1) * 128, cg * 512:(cg + 1) * 512], in_=ov[:]), r=[ov])
                S.barrier()

        def phase_ret_core():
            with ExitStack() as st:
                maskT = sb(st, "c_mask", [128, 4, 128], F32)
                S.dma("sp", lambda e: e.dma_start(out=maskT[:], in_=cst["maskT"]), w=[maskT])
                st_fs = [sb(st, f"c_stf{i}", [128, 2, 512], F32) for i in range(2)]
                st_bs = [sb(st, f"c_stb{i}", [128, 2, 512], BF16) for i in range(2)]
                qtr = rot(st, "c_qt", [128, 2, 128], BF16, 3)
                qdr = rot(st, "c_qd", [128, 2, 128], BF16, 3)
                ktr = rot(st, "c_kt", [128, 2, 128], BF16, 3)
                kdr = rot(st, "c_kd", [128, 256], BF16, 3)
                vr = rot(st, "c_v", [128, 512], BF16, 3)
                sgr = rot(st, "c_sg", [128, 512], BF16, 3)
                scmr = rot(st, "c_scm", [128, 128], BF16, 2)
                ynr = rot(st, "c_yn", [128, 512], F32, 2)
                zr = rot(st, "c_z", [128, 512], BF16, 2)
                str_ = rot(st, "c_st", [128, 6], F32, 2)
                mvr = rot(st, "c_mv", [128, 4], F32, 2)
                scp = rot(st, "c_scp", [128, 128], F32, 2, psum=True)
                ypr = rot(st, "c_yp", [128, 512], F32, 2, psum=True)
                upr = rot(st, "c_up", [128, 2, 512], F32, 2, psum=True)

                def loads(s, h, c):
                    tt = s * 16 + c
                    tsl = slice(tt * 128, (tt + 1) * 128)
                    qt, qd, kt, kd, v, sg = qtr.next(), qdr.next(), ktr.next(), kdr.next(), vr.next(), sgr.next()
                    S.dma("sp", lambda e: e.dma_start(out=qt[:], in_=fm(QT)[:, 2 * h:2 * h + 2, tsl]), w=[qt])
                    S.dma("sp", lambda e: e.dma_start(out=qd[:], in_=fm(QDT)[:, 2 * h:2 * h + 2, tsl]), w=[qd])
                    S.dma("sp", lambda e: e.dma_start(out=kt[:], in_=fm(KT)[:, 2 * h:2 * h + 2, tsl]), w=[kt])
                    S.dma("sp", lambda e: e.dma_start(out=kd[:], in_=KD[tsl, h * 256:(h + 1) * 256]), w=[kd])
                    S.dma("sp", lambda e: e.dma_start(out=v[:], in_=V[tsl, h * 512:(h + 1) * 512]), w=[v])
                    S.dma("sp", lambda e: e.dma_start(out=sg[:], in_=SG[tsl, h * 512:(h + 1) * 512]), w=[sg])
                    return qt, qd, kt, kd, v, sg

                items = [(s, h, c) for s in range(nseq) for hp in (0, 2) for c in range(16) for h in (hp, hp + 1)]
                nxt = loads(*items[0])
                for idx, (s, h, c) in enumerate(items):
                    st_f, st_b = st_fs[h % 2], st_bs[h % 2]
                    qt, qd, kt, kd, v, sg = nxt
                    if idx + 1 < len(items):
                        nxt = loads(*items[idx + 1])
                    tt = s * 16 + c
                    if c == 0:
                        S.op("pool", lambda e: e.memset(st_f[:], 0.0), w=[st_f])
                        S.op("pool", lambda e: e.memset(st_b[:], 0.0), w=[st_b])
                    sp_, yp, up = scp.next(), ypr.next(), upr.next()
                    for dc in range(2):
                        S.op("pe", lambda e, dc=dc: e.matmul(sp_[:], lhsT=kt[:, dc, :], rhs=qt[:, dc, :], start=(dc == 0), stop=(dc == 1)),
                             r=[kt, qt], w=[sp_], same_ok=True)
                    scm = scmr.next()
                    S.op("dve", lambda e: e.tensor_tensor(out=scm[:], in0=sp_[:], in1=maskT[:, h, :], op=ALU.mult), r=[sp_, maskT], w=[scm])
                    S.op("pe", lambda e: e.matmul(yp[:], lhsT=scm[:], rhs=v[:], start=True, stop=False), r=[scm, v], w=[yp], same_ok=True)
                    for dc in range(2):
                        S.op("pe", lambda e, dc=dc: e.matmul(yp[:], lhsT=qd[:, dc, :], rhs=st_b[:, dc, :], start=False, stop=(dc == 1)),
                             r=[qd, st_b], w=[yp], same_ok=True)
                    for dc in range(2):
                        S.op("pe", lambda e, dc=dc: e.matmul(up[:, dc, :], lhsT=kd[:, dc * 128:(dc + 1) * 128], rhs=v[:], start=True, stop=True),
                             r=[kd, v], w=[up], same_ok=True)
                    if c < 15:
                        S.op("dve", lambda e: e.scalar_tensor_tensor(out=st_f[:].rearrange("p a b -> p (a b)"),
                                                                      in0=st_f[:].rearrange("p a b -> p (a b)"), scalar=cdec[h],
                                                                      in1=up[:].rearrange("p a b -> p (a b)"), op0=ALU.mult, op1=ALU.add),
                             r=[st_f, up], w=[st_f])
                        S.op("act", lambda e: e.copy(out=st_b[:], in_=st_f[:]), r=[st_f], w=[st_b])
                    st_, mv, yn, z = str_.next(), mvr.next(), ynr.next(), zr.next()
                    S.op("dve", lambda e: e.bn_stats(out=st_[:], in_=yp[:]), r=[yp], w=[st_])
                    S.op("dve", lambda e: e.bn_aggr(out=mv[:, 0:2], in_=st_[:]), r=[st_], w=[mv])
                    S.op("dve", lambda e: e.tensor_scalar(out=mv[:, 2:3], in0=mv[:, 1:2], scalar1=1.0, scalar2=EPS,
                                                          op0=ALU.mult, op1=ALU.add), r=[mv], w=[mv])
                    S.op("act", lambda e: e.activation(out=mv[:, 2:3], in_=mv[:, 2:3], func=AF.Sqrt), r=[mv], w=[mv])
                    S.op("dve", lambda e: e.reciprocal(out=mv[:, 3:4], in_=mv[:, 2:3]), r=[mv], w=[mv])
                    S.op("dve", lambda e: e.tensor_scalar(out=yn[:], in0=yp[:], scalar1=mv[:, 0:1], scalar2=mv[:, 3:4],
                                                          op0=ALU.subtract, op1=ALU.mult), r=[yp, mv], w=[yn])
                    S.op("pool", lambda e: e.tensor_tensor(out=z[:], in0=yn[:], in1=sg[:], op=ALU.mult), r=[yn, sg], w=[z])
                    S.dma("pool", lambda e: e.dma_start(out=Z[tt * 128:(tt + 1) * 128, h * 512:(h + 1) * 512], in_=z[:]), r=[z])
                S.barrier()

        def phase_outproj_ln(Zsrc, F, w_ap, x_src, x_dst, l, which):
            FC = F // 128
            with ExitStack() as st:
                wsb = sb(st, "o_w", [128, FC, D], BF16)
                with ExitStack() as st2:
                    wl = WLoader(st2, 8, 512, 2, "ow")
                    for fg in range(FC // 8):
                        for hf in range(2):
                            wb = wl.load(w_ap[fg * 1024:(fg + 1) * 1024, hf * 512:(hf + 1) * 512], 512)
                            S.op("dve", lambda e: e.tensor_copy(out=wsb[:, fg * 8:(fg + 1) * 8, hf * 512:(hf + 1) * 512], in_=wb[:]),
                                 r=[wb], w=[wsb])
                    S.barrier()
                ln = LNEpilogue(st, l, which)
                zr = rot(st, "o_z", [128, F], BF16, 3)
                xr = rot(st, "o_x", [128, D], F32, 3)
                zTr = rot(st, "o_zT", [128, FC, 128], BF16, 2)
                yr = rot(st, "o_y", [128, D], F32, 2)
                pzr = rot(st, "o_pz", [128, FC, 128], BF16, 1, psum=True)
                pmr = rot(st, "o_pm", [128, 2, 512], F32, 2, psum=True)

                def loads(tt):
                    z, x = zr.next(), xr.next()
                    S.dma("sp", lambda e: e.dma_start(out=z[:], in_=Zsrc[tt * 128:(tt + 1) * 128, :]), w=[z])
                    S.dma("sp", lambda e: e.dma_start(out=x[:], in_=x_src[tt * 128:(tt + 1) * 128, :]), w=[x])
                    return z, x
                def front(tt, z, x):
                    pz, zT, pm, y = pzr.next(), zTr.next(), pmr.next(), yr.next()
                    for fc in range(FC):
                        S.op("pe", lambda e, fc=fc: e.transpose(pz[:, fc, :], z[:, fc * 128:(fc + 1) * 128], ident[:]),
                             r=[z, ident], w=[pz], same_ok=True)
                    for g8 in range(FC // 8):
                        copy_any(zT[:, g8 * 8:(g8 + 1) * 8, :], pz[:, g8 * 8:(g8 + 1) * 8, :], [pz], [zT])
                    for hf in range(2):
                        for fc in range(FC):
                            S.op("pe", lambda e, fc=fc, hf=hf: e.matmul(pm[:, hf, :], lhsT=zT[:, fc, :], rhs=wsb[:, fc, hf * 512:(hf + 1) * 512],
                                                                         start=(fc == 0), stop=(fc == FC - 1)),
                                 r=[zT, wsb], w=[pm], same_ok=True)
                    for hf in range(2):
                        S.op("dve", lambda e, hf=hf: e.scalar_tensor_tensor(out=y[:, hf * 512:(hf + 1) * 512], in0=x[:, hf * 512:(hf + 1) * 512],
                                                                             scalar=ALPHA, in1=pm[:, hf, :], op0=ALU.mult, op1=ALU.add),
                             r=[x, pm], w=[y])
                    return y

                ld = {0: loads(0)}
                if NT > 1:
                    ld[1] = loads(1)
                y_cur = front(0, *ld.pop(0))
                for tt in range(NT):
                    y_next = None
                    if tt + 1 < NT:
                        zx = ld.pop(tt + 1)
                        if tt + 2 < NT:
                            ld[tt + 2] = loads(tt + 2)
                        y_next = front(tt + 1, *zx)
                    ln.run(y_cur, tt, x_dst)
                    y_cur = y_next
                S.barrier()

        def phase_uv_prep():
            R = 4
            NI = 4 * NEXP // (128 * R)
            uview = peer_u_flat.rearrange("(n p r) d -> n p r d", p=128, r=R)
            vview = peer_v_flat.rearrange("(n p r) d -> n p r d", p=128, r=R)
            oview = UVB.rearrange("(n p r) d -> n p r d", p=128, r=R)
            with ExitStack() as st:
                ufr = rot(st, "uv_uf", [128, R, D], F32, 3)
                vfr = rot(st, "uv_vf", [128, R, D], F32, 3)
                obr = rot(st, "uv_ob", [128, R, 2048], BF16, 3)

                def loads(n):
                    uf, vf = ufr.next(), vfr.next()
                    S.dma("sp", lambda e: e.dma_start(out=uf[:], in_=uview[n]), w=[uf])
                    S.dma("sp", lambda e: e.dma_start(out=vf[:], in_=vview[n]), w=[vf])
                    return uf, vf
                nxt = loads(0)
                for n in range(NI):
                    uf, vf = nxt
                    if n + 1 < NI:
                        nxt = loads(n + 1)
                    ob = obr.next()
                    S.op("dve", lambda e: e.tensor_copy(out=ob[:, :, 0:D], in_=uf[:]), r=[uf], w=[ob])
                    if n % 3 == 2:
                        S.op("pool", lambda e: e.tensor_copy(out=ob[:, :, D:2 * D], in_=vf[:]), r=[vf], w=[ob])
                    else:
                        S.op("act", lambda e: e.copy(out=ob[:, :, D:2 * D], in_=vf[:]), r=[vf], w=[ob])
                    S.dma("sp", lambda e: e.dma_start(out=oview[n], in_=ob[:]), r=[ob])
                S.barrier()

        def phase_peer_q(l):
            with ExitStack() as st:
                xT = load_xT_all(st)
                wl = WLoader(st, 8, 512, 2, "pw")
                ppr = rot(st, "p_pp", [128, 512], F32, 3, psum=True)
                obr = rot(st, "p_ob", [128, 512], BF16, 3)
                for g in range(4):
                    wb = wl.load(peer_w_q[l, :, g * 512:(g + 1) * 512], 512)
                    for j in range(4):
                        for tb in range(NTB):
                            pp, ob = ppr.next(), obr.next()
                            tsl = slice(tb * 512, (tb + 1) * 512)
                            for kc in range(8):
                                S.op("pe", lambda e, kc=kc: e.matmul(pp[:], lhsT=wb[:, kc, j * 128:(j + 1) * 128], rhs=xT[:, kc, tsl],
                                                                      start=(kc == 0), stop=(kc == 7)),
                                     r=[wb, xT], w=[pp], same_ok=True)
                            copy_any(ob[:], pp[:], [pp], [ob])
                            S.dma("sp", lambda e: e.dma_start(out=fm(QP)[:, g * 4 + j, tsl], in_=ob[:]), r=[ob])
                S.barrier()

        def phase_peer_main(l, x_src, x_dst):
            with ExitStack() as st:
                skb = sb(st, "m_skb", [128, 16, 128], BF16)
                with ExitStack() as st2:
                    skf = sb(st2, "m_skf", [128, 16, 128], F32)
                    S.dma("sp", lambda e: e.dma_start(out=skf[:], in_=peer_skT[l]), w=[skf])
                    S.op("dve", lambda e: e.tensor_copy(out=skb[:], in_=skf[:]), r=[skf], w=[skb])
                    S.barrier()
                io16 = sb(st, "m_io16", [128, 2, 16], F32)
                S.dma("sp", lambda e: e.dma_start(out=io16[:], in_=cst["iota16"]), w=[io16])
                ln = LNEpilogue(st, l, 1, ew="dve", npt=1, nbuf=1)
                xr = rot(st, "m_x", [128, D], F32, 3)
                qpr = rot(st, "m_qp", [128, 16, 128], BF16, 2)
                scps = ps(st, "m_scp", [128, 2, 512], F32)
                sc = sb(st, "m_sc", [128, 16, 128], F32)
                scr = rot(st, "m_scr", [128, 128], F32, 4)
                s16h = [Buf(f"s16h{i}") for i in range(16)]
                i16h = [Buf(f"i16h{i}") for i in range(16)]
                s16 = sb(st, "m_s16", [128, 16, 16], F32)
                i16u = sb(st, "m_i16u", [128, 16, 16], U32)
                i16f = sb(st, "m_i16f", [128, 16, 16], F32)
                candr = rot(st, "m_cand", [128, 16, 16], F32, 2)
                scr2 = rot(st, "m_scr2", [128, 256], F32, 2)
                tv = sb(st, "m_tv", [128, 8, 16], F32)
                posu = sb(st, "m_posu", [128, 8, 16], U32)
                posf = sb(st, "m_posf", [128, 8, 16], F32)
                posb = sb(st, "m_posb", [128, 8, 16], U32)
                bfm = sb(st, "m_bfm", [128, 8, 16], F32)
                a16 = sb(st, "m_a16", [128, 8, 16], F32)
                eq4 = rot(st, "m_eq4", [128, 8, 16, 16], F32, 1)
                sel1 = sb(st, "m_sel1", [128, 8, 16], F32)
                sel2 = sb(st, "m_sel2", [128, 8, 16], F32)
                idxf = sb(st, "m_idxf", [128, 128], F32)
                idxi = rot(st, "m_idxi", [128, 128], I32, 2)
                ex = sb(st, "m_ex", [128, 8, 16], F32)
                ssum = sb(st, "m_ssum", [128, 8], F32)
                gater = rot(st, "m_gate", [128, 8, 16], F32, 2)
                dots = sb(st, "m_dots", [128, 128], F32)
                wgt = sb(st, "m_wgt", [128, 128], F32)
                g1 = rot(st, "m_g1", [128, 16], F32, 2)
                g2 = rot(st, "m_g2", [128, 16], F32, 2)
                gbr = rot(st, "m_gb", [128, 2048], BF16, 24)
                junkb = rot(st, "m_junkb", [128, D], BF16, 4)
                junka = rot(st, "m_junka", [128, D], BF16, 2)
                dotsb = [Buf("dots_e"), Buf("dots_o")]
                xbr = rot(st, "m_xb", [128, D], BF16, 2)
                dgr = rot(st, "m_dg", [128, 8, 128], BF16, 3)
                accr = rot(st, "m_accp", [128, 2, 512], F32, 2, psum=True)
                g3 = rot(st, "m_g3", [128, 16], F32, 2)
                yr = rot(st, "m_y", [128, D], F32, 1)

                def select(tt, holder):
                    x, qp = xr.next(), qpr.next()
                    S.dma("sp", lambda e: e.dma_start(out=x[:], in_=x_src[tt * 128:(tt + 1) * 128, :]), w=[x])
                    S.dma("sp", lambda e: e.dma_start(out=qp[:], in_=fm(QP)[:, :, tt * 128:(tt + 1) * 128]), w=[qp])
                    for rnd in range(2):
                        for hq in range(8):
                            hp = rnd * 8 + hq
                            S.op("pe", lambda e, hp=hp, hq=hq: e.matmul(scps[:, hq // 4, (hq % 4) * 128:(hq % 4 + 1) * 128], lhsT=qp[:, hp, :], rhs=skb[:, hp, :],
                                                                         start=True, stop=True), r=[qp, skb], w=[scps], same_ok=True)
                        for bk in range(2):
                            S.op("act", lambda e, bk=bk, rnd=rnd: e.copy(out=sc[:, rnd * 8 + bk * 4:rnd * 8 + (bk + 1) * 4, :].rearrange("p a b -> p (a b)"),
                                                                          in_=scps[:, bk, :]), r=[scps], w=[sc])
                    xb = xbr.next()
                    S.op("act", lambda e: e.copy(out=xb[:], in_=x[:]), r=[x], w=[xb])
                    yield
                    for hp0 in range(0, 16, 2):
                        pr = [(hp0, scr.next()), (hp0 + 1, scr.next())]
                        for (hp, sr) in pr:
                            S.op("dve", lambda e: e.max(out=s16[:, hp, 0:8], in_=sc[:, hp, :]), r=[sc], w=[s16h[hp]])
                        for (hp, sr) in pr:
                            S.op("dve", lambda e: e.max_index(out=i16u[:, hp, 0:8], in_max=s16[:, hp, 0:8], in_values=sc[:, hp, :]), r=[sc, s16h[hp]], w=[i16h[hp]])
                        for (hp, sr) in pr:
                            S.op("dve", lambda e: e.match_replace(out=sr[:], in_to_replace=s16[:, hp, 0:8], in_values=sc[:, hp, :], imm_value=-1e30),
                                 r=[sc, s16h[hp]], w=[sr])
                        for (hp, sr) in pr:
                            S.op("dve", lambda e: e.max(out=s16[:, hp, 8:16], in_=sr[:]), r=[sr], w=[s16h[hp]])
                        for (hp, sr) in pr:
                            S.op("dve", lambda e: e.max_index(out=i16u[:, hp, 8:16], in_max=s16[:, hp, 8:16], in_values=sr[:]), r=[sr, s16h[hp]], w=[i16h[hp]])
                        if hp0 % 4 == 2:
                            yield
                    S.op("dve", lambda e: e.tensor_copy(out=i16f[:], in_=i16u[:]), r=i16h, w=[i16f])
                    i4 = i16f[:].rearrange("p (h two) k -> p h two k", two=2)
                    i1v, i2v = i4[:, :, 0, :], i4[:, :, 1, :]
                    S.op("dve", lambda e: e.tensor_scalar(out=i1v, in0=i1v, scalar1=128.0, scalar2=None, op0=ALU.mult), r=[i16f], w=[i16f])
                    for h in range(8):
                        cand, s2 = candr.next(), scr2.next()
                        a_b = lambda t_, hp_: t_[:, hp_, :].unsqueeze(2).to_broadcast([128, 16, 16])
                        b_b = lambda t_, hp_: t_[:, hp_, :].unsqueeze(1).to_broadcast([128, 16, 16])
                        S.op("dve", lambda e: e.tensor_tensor(out=cand[:], in0=a_b(s16, 2 * h), in1=b_b(s16, 2 * h + 1), op=ALU.add), r=[s16h[2 * h], s16h[2 * h + 1]], w=[cand])
                        cf = cand[:].rearrange("p a b -> p (a b)")
                        S.op("dve", lambda e: e.max(out=tv[:, h, 0:8], in_=cf), r=[cand], w=[tv])
                        S.op("dve", lambda e: e.max_index(out=posu[:, h, 0:8], in_max=tv[:, h, 0:8], in_values=cf), r=[cand, tv], w=[posu])
                        S.op("dve", lambda e: e.match_replace(out=s2[:], in_to_replace=tv[:, h, 0:8], in_values=cf, imm_value=-1e30), r=[cand, tv], w=[s2])
                        S.op("dve", lambda e: e.max(out=tv[:, h, 8:16], in_=s2[:]), r=[s2], w=[tv])
                        S.op("dve", lambda e: e.max_index(out=posu[:, h, 8:16], in_max=tv[:, h, 8:16], in_values=s2[:]), r=[s2, tv], w=[posu])
                        if h % 2 == 1:
                            yield
                    S.op("dve", lambda e: e.tensor_copy(out=posf[:], in_=posu[:]), r=[posu], w=[posf])
                    S.op("dve", lambda e: e.tensor_single_scalar(out=posb[:], in_=posu[:], scalar=15, op=ALU.bitwise_and), r=[posu], w=[posb])
                    S.op("dve", lambda e: e.tensor_copy(out=bfm[:], in_=posb[:]), r=[posb], w=[bfm])
                    S.op("dve", lambda e: e.tensor_tensor(out=a16[:], in0=posf[:], in1=bfm[:], op=ALU.subtract), r=[posf, bfm], w=[a16])
                    shp = [128, 8, 16, 16]
                    for (keyt, iot, valv, selo) in ((a16, 1, i1v, sel1), (bfm, 0, i2v, sel2)):
                        e4 = eq4.next()
                        S.op("dve", lambda e: e.tensor_tensor(out=e4[:], in0=keyt[:].unsqueeze(3).to_broadcast(shp),
                                                              in1=io16[:, iot, :].unsqueeze(1).unsqueeze(1).to_broadcast(shp), op=ALU.is_equal),
                             r=[keyt, io16], w=[e4])
                        S.op("dve", lambda e: e.tensor_tensor(out=e4[:], in0=e4[:], in1=valv.unsqueeze(2).to_broadcast(shp), op=ALU.mult),
                             r=[e4, i16f], w=[e4])
                        S.op("dve", lambda e: e.reduce_sum(out=selo[:].rearrange("p a b -> p (a b)"), in_=e4[:].rearrange("p a b c -> p (a b) c"), axis=AX.X),
                             r=[e4], w=[selo])
                        yield
                    S.op("dve", lambda e: e.tensor_tensor(out=idxf[:], in0=sel1[:].rearrange("p a b -> p (a b)"), in1=sel2[:].rearrange("p a b -> p (a b)"), op=ALU.add),
                         r=[sel1, sel2], w=[idxf])
                    gate = gater.next()
                    S.op("dve", lambda e: e.tensor_tensor(out=ex[:], in0=tv[:], in1=tv[:, :, 0:1].to_broadcast([128, 8, 16]), op=ALU.subtract), r=[tv], w=[ex])
                    S.op("act", lambda e: e.activation(out=ex[:], in_=ex[:], func=AF.Exp), r=[ex], w=[ex])
                    S.op("dve", lambda e: e.reduce_sum(out=ssum[:], in_=ex[:], axis=AX.X), r=[ex], w=[ssum])
                    S.op("dve", lambda e: e.reciprocal(out=ssum[:], in_=ssum[:]), r=[ssum], w=[ssum])
                    S.op("dve", lambda e: e.tensor_tensor(out=gate[:], in0=ex[:], in1=ssum[:].unsqueeze(2).to_broadcast([128, 8, 16]), op=ALU.mult),
                         r=[ex, ssum], w=[gate])
                    ii = idxi.next()
                    S.op("dve", lambda e: e.tensor_scalar(out=idxf[:], in0=idxf[:], scalar1=float(NEXP - 1), scalar2=float(l * NEXP), op0=ALU.min, op1=ALU.add),
                         r=[idxf], w=[idxf])
                    S.op("dve", lambda e: e.tensor_copy(out=ii[:], in_=idxf[:]), r=[idxf], w=[ii])
                    holder.update(dict(x=x, xb=xb, ii=ii, gate=gate))

                GS = 8
                NG = 128 // GS
                CG = 1.5957691216057308

                def dcol(sl):
                    return (sl % 2) * 64 + sl // 2

                def dv(g):
                    return dots[:].rearrange("p (two k) -> p two k", two=2)[:, :, g * 4:(g + 1) * 4]

                def sv(t_, g):
                    return t_.rearrange("p (k two) -> p two k", two=2)[:, :, g * 4:(g + 1) * 4]

                def t3(t_):
                    return t_[:, 0:GS].rearrange("p (two k) -> p two k", two=2)

                def stage_a(sel, g, j0, j1):
                    xb, ii = sel["xb"], sel["ii"]
                    gbs = sel["gbs"].setdefault(g, [])
                    for j in range(j0, j1):
                        sl = g * GS + j
                        gb, jb = gbr.next(), junkb.next()
                        gbs.append(gb)
                        S.dma("pool", lambda e, sl=sl: e.indirect_dma_start(out=gb[:], out_offset=None, in_=UVB,
                                                                            in_offset=bass.IndirectOffsetOnAxis(ap=ii[:, sl:sl + 1], axis=0)),
                              r=[ii], w=[gb])
                        if j % 2 == 0:
                            S.op("dve", lambda e, sl=sl: e.scalar_tensor_tensor(out=jb[:], in0=gb[:, 0:D], scalar=1.0, in1=xb[:], op0=ALU.mult, op1=ALU.mult,
                                                                                 accum_out=dots[:, dcol(sl):dcol(sl) + 1]), r=[gb, xb], w=[jb, dotsb[sl % 2]])
                        else:
                            S.op("dve", lambda e: e.tensor_tensor(out=jb[:], in0=gb[:, 0:D], in1=xb[:], op=ALU.mult), r=[gb, xb], w=[jb])
                            ja = junka.next()
                            S.op("act", lambda e, sl=sl: e.activation(out=ja[:], in_=jb[:], func=AF.Copy, accum_out=dots[:, dcol(sl):dcol(sl) + 1]),
                                 r=[jb], w=[ja, dotsb[sl % 2]])

                def stage_b1(sel, g):
                    hs = slice(g * GS, (g + 1) * GS)
                    a, b_ = g1.next(), g2.next()
                    sel["g12"] = (a, b_)
                    S.op("dve", lambda e: e.tensor_tensor(out=t3(a), in0=dv(g), in1=dv(g), op=ALU.mult), r=dotsb, w=[a])
                    S.op("dve", lambda e: e.tensor_scalar(out=a[:, 0:GS], in0=a[:, 0:GS], scalar1=0.044715, scalar2=1.0, op0=ALU.mult, op1=ALU.add), r=[a], w=[a])
                    S.op("dve", lambda e: e.tensor_tensor(out=t3(a), in0=t3(a), in1=dv(g), op=ALU.mult), r=[a] + dotsb, w=[a])
                    S.op("act", lambda e: e.activation(out=b_[:, 0:GS], in_=a[:, 0:GS], func=AF.Exp, scale=-CG), r=[a], w=[b_])

                def stage_b2(sel, g):
                    gate, accp = sel["gate"], sel["accp"]
                    gf = gate[:].rearrange("p a b -> p (a b)")
                    hs = slice(g * GS, (g + 1) * GS)
                    a, b_ = sel["g12"]
                    c_ = g3.next()
                    gbs = sel["gbs"][g]
                    S.op("dve", lambda e: e.tensor_scalar(out=b_[:, 0:GS], in0=b_[:, 0:GS], scalar1=1.0, scalar2=None, op0=ALU.add), r=[b_], w=[b_])
                    S.op("dve", lambda e: e.reciprocal(out=c_[:, 0:GS], in_=b_[:, 0:GS]), r=[b_], w=[c_])
                    S.op("dve", lambda e: e.tensor_tensor(out=t3(c_), in0=t3(c_), in1=dv(g), op=ALU.mult), r=[c_] + dotsb, w=[c_])
                    S.op("dve", lambda e: e.tensor_tensor(out=sv(wgt[:], g), in0=t3(c_), in1=sv(gf, g), op=ALU.mult), r=[c_, gate], w=[wgt])

                def stage_b2b(sel, g):
                    accp = sel["accp"]
                    hs = slice(g * GS, (g + 1) * GS)
                    gbs = sel["gbs"][g]
                    dg = dgr.next()
                    for j in range(GS):
                        sl = g * GS + j
                        S.op("act", lambda e, j=j, sl=sl: e.activation(out=dg[:, j, :], in_=ident[:], func=AF.Copy, scale=wgt[:, sl:sl + 1]),
                             r=[ident, wgt], w=[dg])
                    for j in range(GS):
                        sl = g * GS + j
                        for hf in range(2):
                            S.op("pe", lambda e, j=j, hf=hf: e.matmul(accp[:, hf, :], lhsT=dg[:, j, :], rhs=gbs[j][:, D + hf * 512:D + (hf + 1) * 512],
                                                                       start=(sl == 0), stop=(sl == 127)),
                                 r=[dg, gbs[j]], w=[accp], same_ok=True)
                    del sel["gbs"][g]

                def finish(sel, tt):
                    x, accp = sel["x"], sel["accp"]
                    y = yr.next()
                    for hf in range(2):
                        S.op("dve", lambda e, hf=hf: e.scalar_tensor_tensor(out=y[:, hf * 512:(hf + 1) * 512], in0=x[:, hf * 512:(hf + 1) * 512],
                                                                             scalar=ALPHA, in1=accp[:, hf, :], op0=ALU.mult, op1=ALU.add),
                             r=[x, accp], w=[y])
                    ln.run(y, tt, x_dst)

                cur = {}
                for _ in select(0, cur):
                    pass
                prev = None
                for tt in range(NT):
                    nxt = {}
                    gen = select(tt + 1, nxt) if tt + 1 < NT else None
                    cur["accp"] = accr.next()
                    cur["gbs"] = {}
                    pend = None
                    for g in range(NG):
                        stage_a(cur, g, 0, 2)
                        if pend is not None:
                            stage_b1(cur, pend)
                        stage_a(cur, g, 2, 5)
                        if pend is not None:
                            stage_b2(cur, pend)
                        stage_a(cur, g, 5, GS)
                        if pend is not None:
                            stage_b2b(cur, pend)
                        pend = g
                        if g == 1 and prev is not None:
                            finish(*prev)
                            prev = None
                        if gen is not None:
                            if next(gen, "done") == "done":
                                gen = None
                    stage_b1(cur, pend)
                    stage_b2(cur, pend)
                    stage_b2b(cur, pend)
                    if gen is not None:
                        for _ in gen:
                            pass
                    prev = (cur, tt)
                    cur = nxt
                finish(*prev)
                S.barrier()

        def phase_tok_proj_rope(w_ap, scale, dstT, v_ap=None):
            with ExitStack() as st:
                xT = load_xT_all(st)
                dcos = sb(st, "t_cos", [128, 16, 8], F32)
                dsin = sb(st, "t_sin", [128, 16, 8], F32)
                S.dma("sp", lambda e: e.dma_start(out=dcos[:], in_=cst["dcos"]), w=[dcos])
                S.dma("sp", lambda e: e.dma_start(out=dsin[:], in_=cst["dsin"]), w=[dsin])
                nW = 2 if v_ap is not None else 1
                wsb = sb(st, "t_w", [128, 8, 1024 * nW], BF16)
                with ExitStack() as st2:
                    wl = WLoader(st2, 8, 512, 2, "tw")
                    for wi, wa in enumerate([w_ap, v_ap][:nW]):
                        for hf in range(2):
                            wb = wl.load(wa[:, hf * 512:(hf + 1) * 512], 512)
                            S.op("dve", lambda e: e.tensor_copy(out=wsb[:, :, wi * 1024 + hf * 512:wi * 1024 + (hf + 1) * 512], in_=wb[:]),
                                 r=[wb], w=[wsb])
                    S.barrier()
                ppr = rot(st, "t_pp", [128, 2, 512], F32, 2, psum=True)
                pTr = rot(st, "t_pT", [128, 8, 128], BF16, 2, psum=True)
                kfr = rot(st, "t_kf", [128, 16, 64], F32, 2)
                tmr = [rot(st, f"t_tm{i}", [128, 16, 8], F32, 2) for i in range(4)]
                kbr = rot(st, "t_kb", [128, D], BF16, 2)
                kTr = rot(st, "t_kT", [128, 8, 128], BF16, 2)
                vbr = rot(st, "t_vb", [128, D], BF16, 2)
                for tt in range(NT):
                    pp, kf, kb, pT, kT = ppr.next(), kfr.next(), kbr.next(), pTr.next(), kTr.next()
                    for hf in range(2):
                        for kc in range(8):
                            S.op("pe", lambda e, kc=kc, hf=hf: e.matmul(pp[:, hf, :], lhsT=xT[:, kc, tt * 128:(tt + 1) * 128],
                                                                         rhs=wsb[:, kc, hf * 512:(hf + 1) * 512], start=(kc == 0), stop=(kc == 7)),
                                 r=[xT, wsb], w=[pp], same_ok=True)
                    kff = kf[:].rearrange("p a b -> p (a b)")
                    for hf in range(2):
                        S.op("act", lambda e, hf=hf: e.mul(out=kff[:, hf * 512:(hf + 1) * 512], in_=pp[:, hf, :], mul=scale), r=[pp], w=[kf])
                    pt = tt % 16
                    cs = dcos[:, pt, :].unsqueeze(1).to_broadcast([128, 16, 8])
                    sn = dsin[:, pt, :].unsqueeze(1).to_broadcast([128, 16, 8])
                    t1, t2, t3, t4 = [r_.next() for r_ in tmr]
                    x1, x2 = kf[:, :, 0:8], kf[:, :, 8:16]
                    S.op("dve", lambda e: e.tensor_tensor(out=t1[:], in0=x1, in1=cs, op=ALU.mult), r=[kf, dcos], w=[t1])
                    S.op("dve", lambda e: e.tensor_tensor(out=t2[:], in0=x2, in1=sn, op=ALU.mult), r=[kf, dsin], w=[t2])
                    S.op("dve", lambda e: e.tensor_tensor(out=t3[:], in0=x1, in1=sn, op=ALU.mult), r=[kf, dsin], w=[t3])
                    S.op("dve", lambda e: e.tensor_tensor(out=t4[:], in0=x2, in1=cs, op=ALU.mult), r=[kf, dcos], w=[t4])
                    S.op("dve", lambda e: e.tensor_tensor(out=x1, in0=t1[:], in1=t2[:], op=ALU.subtract), r=[t1, t2], w=[kf])
                    S.op("dve", lambda e: e.tensor_tensor(out=x2, in0=t3[:], in1=t4[:], op=ALU.add), r=[t3, t4], w=[kf])
                    S.op("act", lambda e: e.copy(out=kb[:], in_=kff), r=[kf], w=[kb])
                    for c in range(8):
                        S.op("pe", lambda e, c=c: e.transpose(pT[:, c, :], kb[:, c * 128:(c + 1) * 128], ident[:]), r=[kb, ident], w=[pT], same_ok=True)
                    S.op("dve", lambda e: e.tensor_copy(out=kT[:], in_=pT[:]), r=[pT], w=[kT])
                    S.dma("sp", lambda e: e.dma_start(out=fm(dstT)[:, :, tt * 128:(tt + 1) * 128], in_=kT[:]), r=[kT])
                    if v_ap is not None:
                        pv, vb = ppr.next(), vbr.next()
                        for hf in range(2):
                            for kc in range(8):
                                S.op("pe", lambda e, kc=kc, hf=hf: e.matmul(pv[:, hf, :], lhsT=xT[:, kc, tt * 128:(tt + 1) * 128],
                                                                             rhs=wsb[:, kc, 1024 + hf * 512:1024 + (hf + 1) * 512],
                                                                             start=(kc == 0), stop=(kc == 7)),
                                     r=[xT, wsb], w=[pv], same_ok=True)
                        for hf in range(2):
                            copy_any(vb[:, hf * 512:(hf + 1) * 512], pv[:, hf, :], [pv], [vb])
                        S.dma("sp", lambda e: e.dma_start(out=VS[tt * 128:(tt + 1) * 128, :], in_=vb[:]), r=[vb])
                S.barrier()

        def phase_attn(j, layer_idx):
            lam_init = 0.8 - 0.6 * math.exp(-0.3 * layer_idx)
            with ExitStack() as st:
                lamt = sb(st, "a_lamt", [128, 256], F32)
                lj = sb(st, "a_lj", [128, 64], F32)
                lv = sb(st, "a_lv", [128, 8], F32)
                gsub = sb(st, "a_gsub", [128, 128], F32)
                tri = sb(st, "a_tri", [128, 128], BF16)
                S.dma("sp", lambda e: e.dma_start(out=lamt[:], in_=diff_lambda[j].partition_broadcast(128)), w=[lamt])
                S.dma("sp", lambda e: e.dma_start(out=gsub[:], in_=diff_subln_g[j].partition_broadcast(128)), w=[gsub])
                S.dma("sp", lambda e: e.dma_start(out=tri[:], in_=cst["tri"]), w=[tri])
                S.op("dve", lambda e: e.scalar_tensor_tensor(out=lj[:], in0=lamt[:, 0:64], scalar=1.0, in1=lamt[:, 64:128], op0=ALU.mult, op1=ALU.mult,
                                                              accum_out=lv[:, 0:1]), r=[lamt], w=[lj, lv])
                S.op("dve", lambda e: e.scalar_tensor_tensor(out=lj[:], in0=lamt[:, 128:192], scalar=1.0, in1=lamt[:, 192:256], op0=ALU.mult, op1=ALU.mult,
                                                              accum_out=lv[:, 1:2]), r=[lamt], w=[lj, lv])
                S.op("act", lambda e: e.activation(out=lv[:, 2:4], in_=lv[:, 0:2], func=AF.Exp), r=[lv], w=[lv])
                S.op("dve", lambda e: e.tensor_tensor(out=lv[:, 4:5], in0=lv[:, 3:4], in1=lv[:, 2:3], op=ALU.subtract), r=[lv], w=[lv])
                S.op("dve", lambda e: e.tensor_scalar(out=lv[:, 5:6], in0=lv[:, 4:5], scalar1=-lam_init, scalar2=None, op0=ALU.add), r=[lv], w=[lv])
                S.op("dve", lambda e: e.tensor_scalar(out=gsub[:], in0=gsub[:], scalar1=(1.0 - lam_init), scalar2=None, op0=ALU.mult), r=[gsub], w=[gsub])
                neglam = lv[:, 5:6]

                kTr = rot(st, "a_kT", [128, SEQ], BF16, 2)
                qTr = rot(st, "a_qT", [128, SEQ], BF16, 2)
                vhr = rot(st, "a_vh", [128, 16, 129], BF16, 2)
                for t_ in vhr.tiles:
                    S.op("pool", lambda e, t_=t_: e.memset(t_[:, :, 128:129], 1.0), w=[t_])
                o1b = sb(st, "a_o1", [128, 16, 128], F32)
                oat = rot(st, "a_oat", [128, 16, 128], BF16, 2)
                Er = rot(st, "a_E", [128, 512], BF16, 3)
                tmpr = rot(st, "a_tmp", [128, 128], F32, 2)
                o2r = rot(st, "a_o2", [128, 128], F32, 2)
                jr = rot(st, "a_j", [128, 128], F32, 2)
                rsr = rot(st, "a_rs", [128, 4], F32, 4)
                spr = rot(st, "a_sp", [128, 512], F32, 3, psum=True)
                accp = ps(st, "a_acc", [128, 4, 512], F32)
                accb = [Buf(f"a_accb{i}") for i in range(4)]

                def loads(s, h):
                    kT, qT, vh = kTr.next(), qTr.next(), vhr.next()
                    S.dma("sp", lambda e: e.dma_start(out=kT[:], in_=fm(KTS)[:, h, s * SEQ:(s + 1) * SEQ]), w=[kT])
                    S.dma("sp", lambda e: e.dma_start(out=qT[:], in_=fm(QTD)[:, h, s * SEQ:(s + 1) * SEQ]), w=[qT])
                    S.dma("sp", lambda e: e.dma_start(out=vh[:, :, 0:128], in_=tm(VS)[:, s * 16:(s + 1) * 16, h * 128:(h + 1) * 128]), w=[vh])
                    return kT, qT, vh
                items = [(s, h) for s in range(nseq) for h in range(8)]
                nxt = loads(*items[0])
                for idx, (s, h) in enumerate(items):
                    kT, qT, vh = nxt
                    if idx + 1 < len(items):
                        nxt = loads(*items[idx + 1])
                    ot = oat.next()
                    for m in range(2):
                        msl = slice(m * 64, (m + 1) * 64)
                        for G in range(4):
                            nkt = 4 * G + 4

                            def score(kt):
                                qi0 = max(0, kt - 4 * G)
                                sp_ = spr.next()
                                qs = slice(G * 512 + qi0 * 128, (G + 1) * 512)
                                es_ = slice(qi0 * 128, 512)
                                S.op("pe", lambda e: e.matmul(sp_[:, es_], lhsT=kT[msl, kt * 128:(kt + 1) * 128], rhs=qT[msl, qs], start=True, stop=True),
                                     r=[kT, qT], w=[sp_], same_ok=True)
                                return sp_
                            sp_next = score(0)
                            for kt in range(nkt):
                                qi0 = max(0, kt - 4 * G)
                                sp_, E = sp_next, Er.next()
                                es_ = slice(qi0 * 128, 512)
                                if kt + 1 < nkt:
                                    sp_next = score(kt + 1)
                                S.op("act", lambda e: e.activation(out=E[:, es_], in_=sp_[:, es_], func=AF.Exp), r=[sp_], w=[E])
                                if kt >= 4 * G:
                                    ds_ = slice(qi0 * 128, (qi0 + 1) * 128)
                                    S.op("dve", lambda e: e.tensor_tensor(out=E[:, ds_], in0=E[:, ds_], in1=tri[:], op=ALU.mult), r=[E, tri], w=[E])
                                for qi in range(qi0, 4):
                                    S.op("pe", lambda e, qi=qi: e.matmul(accp[:, qi, 0:129], lhsT=E[:, qi * 128:(qi + 1) * 128], rhs=vh[:, kt, :],
                                                                          start=(kt == 0), stop=(kt == 4 * G + qi)),
                                         r=[E, vh], w=[accb[qi]], same_ok=True)
                            for qi in range(4):
                                qt_ = 4 * G + qi
                                rs = rsr.next()
                                S.op("dve", lambda e: e.reciprocal(out=rs[:, 0:1], in_=accp[:, qi, 128:129]), r=[accb[qi]], w=[rs])
                                if m == 0:
                                    S.op("dve", lambda e: e.tensor_scalar(out=o1b[:, qt_, :], in0=accp[:, qi, 0:128], scalar1=rs[:, 0:1], scalar2=None,
                                                                          op0=ALU.mult), r=[accb[qi], rs], w=[o1b])
                                else:
                                    tmp, o2, jj = tmpr.next(), o2r.next(), jr.next()
                                    S.op("dve", lambda e: e.tensor_scalar(out=tmp[:], in0=accp[:, qi, 0:128], scalar1=rs[:, 0:1], scalar2=None,
                                                                          op0=ALU.mult), r=[accb[qi], rs], w=[tmp])
                                    S.op("dve", lambda e: e.scalar_tensor_tensor(out=o2[:], in0=tmp[:], scalar=neglam, in1=o1b[:, qt_, :],
                                                                                  op0=ALU.mult, op1=ALU.add), r=[tmp, lv, o1b], w=[o2])
                                    S.op("dve", lambda e: e.scalar_tensor_tensor(out=jj[:], in0=o2[:], scalar=1.0, in1=o2[:], op0=ALU.mult, op1=ALU.mult,
                                                                                  accum_out=rs[:, 1:2]), r=[o2], w=[jj, rs])
                                    S.op("dve", lambda e: e.tensor_scalar(out=rs[:, 2:3], in0=rs[:, 1:2], scalar1=1.0 / 128.0, scalar2=EPS,
                                                                          op0=ALU.mult, op1=ALU.add), r=[rs], w=[rs])
                                    S.op("act", lambda e: e.activation(out=rs[:, 2:3], in_=rs[:, 2:3], func=AF.Sqrt), r=[rs], w=[rs])
                                    S.op("dve", lambda e: e.reciprocal(out=rs[:, 3:4], in_=rs[:, 2:3]), r=[rs], w=[rs])
                                    S.op("dve", lambda e: e.scalar_tensor_tensor(out=ot[:, qt_, :], in0=o2[:], scalar=rs[:, 3:4], in1=gsub[:],
                                                                                   op0=ALU.mult, op1=ALU.mult), r=[o2, rs, gsub], w=[ot])
                    S.dma("sp", lambda e: e.dma_start(out=tm(OATT)[:, s * 16:(s + 1) * 16, h * 128:(h + 1) * 128], in_=ot[:]), r=[ot])
                S.barrier()

        steps = []
        phase_init()
        phase_uv_prep()
        cur = x_in
        for l in range(DEPTH):
            last = (l == DEPTH - 1)
            if l < 2:
                steps.append(lambda l=l, cur=cur: (phase_ret_proj(l), phase_ret_core(),
                                                   phase_outproj_ln(Z, 2048, ret_w_out[l], cur, X, l, 0)))
            else:
                steps.append(lambda l=l, cur=cur: (phase_tok_proj_rope(diff_w_q[l - 2], 0.125, QTD), phase_attn(l - 2, l),
                                                   phase_outproj_ln(OATT, 1024, diff_w_out[l - 2], cur, X, l, 0)))
            cur = X
            steps.append(lambda l=l, last=last: (phase_peer_q(l), phase_peer_main(l, X, out if last else X)))
            if l == 1:
                steps.append(lambda: phase_tok_proj_rope(kv_w[:, 0:1024], 1.0, KTS, v_ap=kv_w[:, 1024:2048]))
        for i, f in enumerate(steps):
            if i < n_steps:
                f()
        S.barrier()
        print("instructions issued:", S.n_ins, {k: c.count for k, c in S.ctr.items()})
    return nc


_CACHE = {}


def _in_maps(inputs, nseq, n_cores):
    c = make_consts()
    x = np.asarray(inputs["x"], dtype=np.float32).reshape(-1, D)
    skT = np.ascontiguousarray(np.asarray(inputs["peer_subkeys"], dtype=np.float32).transpose(0, 4, 1, 2, 3).reshape(4, 128, 16, 128))
    shared = {
        "ret_w_in": np.asarray(inputs["ret_w_in"], np.float32), "ret_w_out": np.asarray(inputs["ret_w_out"], np.float32),
        "kv_w": np.asarray(inputs["kv_w"], np.float32), "diff_w_q": np.asarray(inputs["diff_w_q"], np.float32),
        "diff_lambda": np.asarray(inputs["diff_lambda"], np.float32).reshape(2, 256),
        "diff_subln_g": np.asarray(inputs["diff_subln_g"], np.float32), "diff_w_out": np.asarray(inputs["diff_w_out"], np.float32),
        "peer_w_q": np.asarray(inputs["peer_w_q"], np.float32), "peer_skT": skT,
        "peer_u": np.asarray(inputs["peer_u"], np.float32), "peer_v": np.asarray(inputs["peer_v"], np.float32),
        "ln_g": np.asarray(inputs["ln_g"], np.float32), "ln_b": np.asarray(inputs["ln_b"], np.float32),
    }
    for k in CONST_SPECS:
        shared["c_" + k] = c[k]
    maps = []
    tk = nseq * SEQ
    for i in range(n_cores):
        m = dict(shared)
        m["x"] = np.ascontiguousarray(x[i * tk:(i + 1) * tk])
        maps.append(m)
    return maps


def kernel(**inputs):
    nseq = 2
    if "nc" not in _CACHE:
        _CACHE["nc"] = build_program(nseq=nseq)
    nc = _CACHE["nc"]
    maps = _in_maps(inputs, nseq, N_CORES)
    res = run_bass_kernel_spmd(nc, maps, core_ids=list(range(N_CORES)))
    outs = [np.asarray(r["out"], dtype=np.float32) for r in res.results]
    return np.concatenate(outs, axis=0).reshape(16, SEQ, D)
```

```python
import math
from contextlib import ExitStack

import numpy as np
import ml_dtypes
import concourse.bass as bass
import concourse.mybir as mybir
from concourse.bass_utils import run_bass_kernel_spmd

F32 = mybir.dt.float32
BF16 = mybir.dt.bfloat16
I32 = mybir.dt.int32
U32 = mybir.dt.uint32
AF = mybir.ActivationFunctionType
ALU = mybir.AluOpType
AX = mybir.AxisListType

D = 1024
SEQ = 2048
DEPTH = 4
ALPHA = (2 * DEPTH) ** 0.25
EPS = 1e-5
NEXP = 16384
N_CORES = 8


class Ctr:
    def __init__(self, sem, step):
        self.sem = sem
        self.step = step
        self.count = 0


class Buf:
    def __init__(self, name=""):
        self.name = name
        self.w = None
        self.r = {}


class T:
    def __init__(self, t, name):
        self.t = t
        self.b = Buf(name)

    def __getitem__(self, k):
        return self.t[k]


def _b(x):
    return x.b if hasattr(x, "b") else x


class Sched:
    def __init__(self, nc, es, n_dma_sems=(12, 4, 12)):
        self.nc = nc
        self.engs = {"pe": nc.tensor, "dve": nc.vector, "act": nc.scalar, "pool": nc.gpsimd, "sp": nc.sync}
        self.ctr = {}
        for k in ("pe", "dve", "act", "pool"):
            self.ctr[k] = Ctr(es.enter_context(nc.semaphore("c_" + k)), 1)
        self.dq = {}
        for k, n in zip(("sp", "act", "pool"), n_dma_sems):
            self.dq[k] = [Ctr(es.enter_context(nc.semaphore(f"d_{k}{i}")), 16) for i in range(n)]
        self.dq_i = {"sp": 0, "act": 0, "pool": 0}
        self.bar_sem = es.enter_context(nc.semaphore("bar"))
        self.bar_n = 0
        self.waited = {k: {} for k in ("pe", "dve", "act", "pool", "sp")}
        self.n_ins = 0

    def _wait(self, ek, deps):
        best = {}
        for c, v in deps:
            if v > best.get(c, 0):
                best[c] = v
        for c, v in best.items():
            if self.waited[ek].get(c, 0) >= v:
                continue
            self.engs[ek].wait_ge(c.sem, v)
            self.waited[ek][c] = v
            self.n_ins += 1

    def _deps(self, r, w, skip=None):
        deps = []
        for b in r:
            if b.w is not None:
                deps.append(b.w)
        for b in w:
            if b.w is not None:
                deps.append(b.w)
            for c, v in b.r.items():
                deps.append((c, v))
        if skip is not None:
            deps = [(c, v) for c, v in deps if c is not skip]
        return deps

    def _mark(self, r, w, c, v):
        for b in w:
            b.w = (c, v)
            b.r = {}
        for b in r:
            if b.r.get(c, 0) < v:
                b.r[c] = v

    def op(self, ek, fn, r=(), w=(), same_ok=False):
        r = [_b(x) for x in r]
        w = [_b(x) for x in w]
        c = self.ctr[ek]
        self._wait(ek, self._deps(r, w, skip=c if same_ok else None))
        ins = fn(self.engs[ek])
        c.count += 1
        ins.then_inc(c.sem, 1)
        self.n_ins += 1
        self._mark(r, w, c, c.count)
        return ins

    def dma(self, qk, fn, r=(), w=()):
        r = [_b(x) for x in r]
        w = [_b(x) for x in w]
        lst = self.dq[qk]
        i = self.dq_i[qk]
        self.dq_i[qk] = (i + 1) % len(lst)
        c = lst[i]
        deps = self._deps(r, w)
        if c.count > 0:
            deps.append((c, c.count))
        self._wait(qk, deps)
        ins = fn(self.engs[qk])
        c.count += 16
        ins.then_inc(c.sem, 16)
        self.n_ins += 1
        self._mark(r, w, c, c.count)
        return ins

    def barrier(self):
        deps = []
        for k in ("pe", "dve", "act", "pool"):
            c = self.ctr[k]
            if c.count:
                deps.append((c, c.count))
        for k in self.dq:
            for c in self.dq[k]:
                if c.count:
                    deps.append((c, c.count))
        self._wait("sp", deps)
        self.bar_n += 1
        self.engs["sp"].sem_inc(self.bar_sem, 1)
        for k in ("pe", "dve", "act", "pool"):
            self.engs[k].wait_ge(self.bar_sem, self.bar_n)
            for c, v in deps:
                self.waited[k][c] = v
        for c, v in deps:
            self.waited["sp"][c] = v


class Rot:
    def __init__(self, tiles):
        self.tiles = tiles
        self.i = 0

    def next(self):
        t = self.tiles[self.i % len(self.tiles)]
        self.i += 1
        return t


def make_consts():
    c = {}
    pos = np.arange(SEQ, dtype=np.float32)
    ret_freqs = (1.0 / (np.float32(10000.0) ** np.linspace(0.0, 1.0, 128, dtype=np.float32))).astype(np.float32)
    ang = (pos[:, None] * ret_freqs[None, :]).astype(np.float32)
    c["ret_cos"] = np.ascontiguousarray(np.cos(ang).T).astype(np.float32)
    c["ret_sin"] = np.ascontiguousarray(np.sin(ang).T).astype(np.float32)
    log_g = np.log(1.0 - np.exp2(-5.0 - np.arange(4, dtype=np.float32))).astype(np.float32)
    ar = np.arange(128, dtype=np.float32)
    rel = ar[:, None] - ar[None, :]
    dm = np.where(rel[None] >= 0, np.exp(np.maximum(rel, 0.0)[None] * log_g[:, None, None]), 0.0)
    c["maskT"] = np.ascontiguousarray(dm.transpose(2, 0, 1)).astype(np.float32)
    qd = np.exp((ar + 1.0)[None] * log_g[:, None]).astype(np.float32)
    c["qdec"] = np.ascontiguousarray(np.broadcast_to(qd[None], (128, 4, 128))).astype(np.float32)
    kd = np.exp((128 - 1.0 - ar)[None] * log_g[:, None]).astype(np.float32)
    c["kdec"] = np.ascontiguousarray(kd.T).astype(np.float32)
    c["cdec"] = [float(x) for x in np.exp(128 * log_g)]
    dfreq = (np.float32(500000.0) ** (-np.arange(0, 16, 2, dtype=np.float32) / 16)).astype(np.float32)
    dang = (pos[:, None] * dfreq[None, :]).astype(np.float32)
    c["dcos"] = np.ascontiguousarray(np.cos(dang).reshape(16, 128, 8).transpose(1, 0, 2)).astype(np.float32)
    c["dsin"] = np.ascontiguousarray(np.sin(dang).reshape(16, 128, 8).transpose(1, 0, 2)).astype(np.float32)
    c["tri"] = (ar[None, :] >= ar[:, None]).astype(np.float32).astype(ml_dtypes.bfloat16)
    c["ident"] = np.eye(128, dtype=np.float32).astype(ml_dtypes.bfloat16)
    io = np.arange(16, dtype=np.float32)
    c["iota16"] = np.ascontiguousarray(np.broadcast_to(np.stack([io, io * 16.0])[None], (128, 2, 16))).astype(np.float32)
    return c


CONST_SPECS = {
    "ret_cos": ([128, SEQ], F32), "ret_sin": ([128, SEQ], F32), "maskT": ([128, 4, 128], F32),
    "qdec": ([128, 4, 128], F32), "kdec": ([128, 4], F32), "dcos": ([128, 16, 8], F32),
    "dsin": ([128, 16, 8], F32), "tri": ([128, 128], BF16), "ident": ([128, 128], BF16),
    "iota16": ([128, 2, 16], F32),
}


def build_program(nseq=2, n_steps=99, dbg=False):
    Tk = nseq * SEQ
    NT = Tk // 128
    NTB = Tk // 512
    nc = bass.Bass("TRN2", target_bir_lowering=False)
    cdec = make_consts()["cdec"]

    def din(name, shape, dt=F32):
        return nc.dram_tensor(name, shape, dt, kind="ExternalInput").ap()

    def dscr(name, shape, dt):
        return nc.dram_tensor(name, shape, dt, kind="ExternalOutput" if dbg else "Internal").ap()

    x_in = din("x", [Tk, D])
    ret_w_in = din("ret_w_in", [2, D, 6144])
    ret_w_out = din("ret_w_out", [2, 2048, D])
    kv_w = din("kv_w", [D, 2048])
    diff_w_q = din("diff_w_q", [2, D, D])
    diff_lambda = din("diff_lambda", [2, 256])
    diff_subln_g = din("diff_subln_g", [2, 128])
    diff_w_out = din("diff_w_out", [2, D, D])
    peer_w_q = din("peer_w_q", [4, D, 2048])
    peer_skT = din("peer_skT", [4, 128, 16, 128])
    peer_u = din("peer_u", [4, NEXP, D])
    peer_v = din("peer_v", [4, NEXP, D])
    peer_u_flat = peer_u.rearrange("l n d -> (l n) d")
    peer_v_flat = peer_v.rearrange("l n d -> (l n) d")
    ln_g = din("ln_g", [4, 2, D])
    ln_b = din("ln_b", [4, 2, D])
    cst = {k: din("c_" + k, shp, dt) for k, (shp, dt) in CONST_SPECS.items()}
    out = nc.dram_tensor("out", [Tk, D], F32, kind="ExternalOutput").ap()

    X = dscr("X", [Tk, D], F32)
    XT = dscr("XT", [D, Tk], BF16)
    QT = dscr("QT", [D, Tk], BF16)
    QDT = dscr("QDT", [D, Tk], BF16)
    KT = dscr("KT", [D, Tk], BF16)
    KD = dscr("KD", [Tk, D], BF16)
    V = dscr("V", [Tk, 2048], BF16)
    SG = dscr("SG", [Tk, 2048], BF16)
    Z = dscr("Z", [Tk, 2048], BF16)
    QP = dscr("QP", [2048, Tk], BF16)
    KTS = dscr("KTS", [D, Tk], BF16)
    VS = dscr("VS", [Tk, D], BF16)
    QTD = dscr("QTD", [D, Tk], BF16)
    OATT = dscr("OATT", [Tk, D], BF16)
    UVB = nc.dram_tensor("UVB", [4 * NEXP, 2048], BF16, kind="Internal").ap()

    def fm(ap):
        return ap.rearrange("(c p) t -> p c t", p=128)

    def tm(ap):
        return ap.rearrange("(n p) f -> p n f", p=128)

    with ExitStack() as es:
        S = Sched(nc, es)

        uid = {"n": 0}

        def sb(st, name, shape, dt):
            uid["n"] += 1
            name = f"{name}_{uid['n']}"
            return T(st.enter_context(nc.sbuf_tensor(name, shape, dt)), name)

        def ps(st, name, shape, dt):
            uid["n"] += 1
            name = f"{name}_{uid['n']}"
            return T(st.enter_context(nc.psum_tensor(name, shape, dt)), name)

        def rot(st, name, shape, dt, n, psum=False):
            mk = ps if psum else sb
            return Rot([mk(st, f"{name}{i}", shape, dt) for i in range(n)])

        ident = sb(es, "ident", [128, 128], BF16)
        S.dma("sp", lambda e: e.dma_start(out=ident[:], in_=cst["ident"]), w=[ident])

        flip = {"i": 0}

        def copy_any(out_ap, in_ap, r, w, engines=("act", "dve")):
            ek = engines[flip["i"] % len(engines)]
            flip["i"] += 1
            if ek == "act":
                S.op("act", lambda e: e.copy(out=out_ap, in_=in_ap), r=r, w=w)
            else:
                S.op(ek, lambda e: e.tensor_copy(out=out_ap, in_=in_ap), r=r, w=w)

        def load_xT_all(st):
            xT = sb(st, "xT_all", [128, 8, Tk], BF16)
            for c in range(8):
                S.dma("sp", lambda e, c=c: e.dma_start(out=xT[:, c, :], in_=fm(XT)[:, c, :]), w=[xT])
            return xT

        class WLoader:
            def __init__(self, st, kc, wmax, nbuf=2, name="wl"):
                self.kc = kc
                self.wf = rot(st, name + "f", [128, kc, wmax], F32, nbuf)
                self.wb = rot(st, name + "b", [128, kc, wmax], BF16, nbuf)

            def load(self, w_ap, w):
                wf = self.wf.next()
                wb = self.wb.next()
                src = w_ap.rearrange("(c p) n -> p c n", p=128)
                half = self.kc // 2
                S.dma("sp", lambda e: e.dma_start(out=wf[:, 0:half, 0:w], in_=src[:, 0:half, :]), w=[wf])
                S.dma("sp", lambda e: e.dma_start(out=wf[:, half:, 0:w], in_=src[:, half:, :]), w=[wf])
                S.op("pool", lambda e: e.tensor_copy(out=wb[:, :, 0:w], in_=wf[:, :, 0:w]), r=[wf], w=[wb])
                return wb

        class LNEpilogue:
            def __init__(self, st, l, which, ew="pool", npt=2, nbuf=2):
                self.ew = ew
                self.gb = sb(st, "ln_gbc", [128, D], F32)
                self.bb = sb(st, "ln_bbc", [128, D], F32)
                S.dma("sp", lambda e: e.dma_start(out=self.gb[:], in_=ln_g[l, which].partition_broadcast(128)), w=[self.gb])
                S.dma("sp", lambda e: e.dma_start(out=self.bb[:], in_=ln_b[l, which].partition_broadcast(128)), w=[self.bb])
                self.st = rot(st, "ln_st", [128, 2, 6], F32, 2)
                self.mv = rot(st, "ln_mv", [128, 4], F32, 2)
                self.xn = rot(st, "ln_xn", [128, D], F32, nbuf)
                self.xo = rot(st, "ln_xo", [128, D], F32, nbuf)
                self.xb = rot(st, "ln_xb", [128, D], BF16, nbuf)
                self.xT = rot(st, "ln_xT", [128, 8, 128], BF16, nbuf)
                self.pT = rot(st, "ln_pT", [128, 8, 128], BF16, npt, psum=True)

            def run(self, y, tt, x_dst):
                st_, mv, xn, xo, xb, xT, pT = (self.st.next(), self.mv.next(), self.xn.next(), self.xo.next(),
                                               self.xb.next(), self.xT.next(), self.pT.next())
                S.op("dve", lambda e: e.bn_stats(out=st_[:, 0, :], in_=y[:, 0:512]), r=[y], w=[st_])
                S.op("dve", lambda e: e.bn_stats(out=st_[:, 1, :], in_=y[:, 512:1024]), r=[y], w=[st_])
                S.op("dve", lambda e: e.bn_aggr(out=mv[:, 0:2], in_=st_[:].rearrange("p a b -> p (a b)")), r=[st_], w=[mv])
                S.op("dve", lambda e: e.tensor_scalar(out=mv[:, 2:3], in0=mv[:, 1:2], scalar1=1.0, scalar2=EPS,
                                                      op0=ALU.mult, op1=ALU.add), r=[mv], w=[mv])
                S.op("act", lambda e: e.activation(out=mv[:, 2:3], in_=mv[:, 2:3], func=AF.Sqrt), r=[mv], w=[mv])
                S.op("dve", lambda e: e.reciprocal(out=mv[:, 3:4], in_=mv[:, 2:3]), r=[mv], w=[mv])
                S.op("dve", lambda e: e.tensor_scalar(out=xn[:], in0=y[:], scalar1=mv[:, 0:1], scalar2=mv[:, 3:4],
                                                      op0=ALU.subtract, op1=ALU.mult), r=[y, mv], w=[xn])
                S.op(self.ew, lambda e: e.tensor_tensor(out=xn[:], in0=xn[:], in1=self.gb[:], op=ALU.mult), r=[xn, self.gb], w=[xn])
                S.op(self.ew, lambda e: e.tensor_tensor(out=xo[:], in0=xn[:], in1=self.bb[:], op=ALU.add), r=[xn, self.bb], w=[xo])
                sq = "pool" if self.ew == "pool" else "sp"
                S.dma(sq, lambda e: e.dma_start(out=x_dst[tt * 128:(tt + 1) * 128, :], in_=xo[:]), r=[xo])
                S.op("act", lambda e: e.copy(out=xb[:], in_=xo[:]), r=[xo], w=[xb])
                for c in range(8):
                    S.op("pe", lambda e, c=c: e.transpose(pT[:, c, :], xb[:, c * 128:(c + 1) * 128], ident[:]),
                         r=[xb, ident], w=[pT], same_ok=True)
                S.op("act", lambda e: e.copy(out=xT[:], in_=pT[:]), r=[pT], w=[xT])
                S.dma(sq, lambda e: e.dma_start(out=fm(XT)[:, :, tt * 128:(tt + 1) * 128], in_=xT[:]), r=[xT])

        def phase_init():
            with ExitStack() as st:
                xf = rot(st, "i_xf", [128, D], F32, 2)
                xb = rot(st, "i_xb", [128, D], BF16, 2)
                xT = rot(st, "i_xT", [128, 8, 128], BF16, 2)
                pT = rot(st, "i_pT", [128, 8, 128], BF16, 2, psum=True)
                for tt in range(NT):
                    a, b_, c_, p = xf.next(), xb.next(), xT.next(), pT.next()
                    S.dma("sp", lambda e: e.dma_start(out=a[:], in_=x_in[tt * 128:(tt + 1) * 128, :]), w=[a])
                    S.op("dve", lambda e: e.tensor_copy(out=b_[:], in_=a[:]), r=[a], w=[b_])
                    for c in range(8):
                        S.op("pe", lambda e, c=c: e.transpose(p[:, c, :], b_[:, c * 128:(c + 1) * 128], ident[:]),
                             r=[b_, ident], w=[p], same_ok=True)
                    S.op("act", lambda e: e.copy(out=c_[:], in_=p[:]), r=[p], w=[c_])
                    S.dma("sp", lambda e: e.dma_start(out=fm(XT)[:, :, tt * 128:(tt + 1) * 128], in_=c_[:]), r=[c_])
                S.barrier()

        def phase_ret_proj(l):
            with ExitStack() as st:
                xT = load_xT_all(st)
                cosT = sb(st, "r_cos", [128, SEQ], F32)
                sinT = sb(st, "r_sin", [128, SEQ], F32)
                qdec = sb(st, "r_qdec", [128, 4, 128], F32)
                kdec = sb(st, "r_kdec", [128, 4], F32)
                S.dma("sp", lambda e: e.dma_start(out=cosT[:], in_=cst["ret_cos"]), w=[cosT])
                S.dma("sp", lambda e: e.dma_start(out=sinT[:], in_=cst["ret_sin"]), w=[sinT])
                S.dma("sp", lambda e: e.dma_start(out=qdec[:], in_=cst["qdec"]), w=[qdec])
                S.dma("sp", lambda e: e.dma_start(out=kdec[:], in_=cst["kdec"]), w=[kdec])
                wl = WLoader(st, 8, 512, 2, "rw")
                p1r = rot(st, "r_p1", [128, 512], F32, 2, psum=True)
                p2r = rot(st, "r_p2", [128, 512], F32, 2, psum=True)
                pkr = rot(st, "r_pk", [128, 8, 128], BF16, 2, psum=True)
                a1r = rot(st, "r_a1", [128, 512], F32, 2)
                a2r = rot(st, "r_a2", [128, 512], F32, 2)
                tr = [rot(st, f"r_t{i}", [128, 512], F32, 2) for i in range(4)]
                o1r = rot(st, "r_o1", [128, 512], F32, 2)
                o2r = rot(st, "r_o2", [128, 512], F32, 2)
                obr = rot(st, "r_ob", [128, 2, 512], BF16, 3)
                odr = rot(st, "r_od", [128, 2, 512], BF16, 3)
                kdr = rot(st, "r_kd", [128, 4, 256], BF16, 2)
                for kind in range(2):
                    for h in range(4):
                        c0 = kind * 1024 + h * 256
                        wb = wl.load(ret_w_in[l, :, c0:c0 + 256], 256)
                        for tb in range(NTB):
                            p1, p2 = p1r.next(), p2r.next()
                            tsl = slice(tb * 512, (tb + 1) * 512)
                            for kc in range(8):
                                S.op("pe", lambda e, kc=kc: e.matmul(p1[:], lhsT=wb[:, kc, 0:128], rhs=xT[:, kc, tsl],
                                                                      start=(kc == 0), stop=(kc == 7)),
                                     r=[wb, xT], w=[p1], same_ok=True)
                            for kc in range(8):
                                S.op("pe", lambda e, kc=kc: e.matmul(p2[:], lhsT=wb[:, kc, 128:256], rhs=xT[:, kc, tsl],
                                                                      start=(kc == 0), stop=(kc == 7)),
                                     r=[wb, xT], w=[p2], same_ok=True)
                            a1, a2 = a1r.next(), a2r.next()
                            sc = 1.0 if kind == 0 else 1.0 / 16.0
                            S.op("act", lambda e: e.mul(out=a1[:], in_=p1[:], mul=sc), r=[p1], w=[a1])
                            S.op("act", lambda e: e.mul(out=a2[:], in_=p2[:], mul=sc), r=[p2], w=[a2])
                            p0 = (tb * 512) % SEQ
                            cs, sn = cosT[:, p0:p0 + 512], sinT[:, p0:p0 + 512]
                            t1, t2, t3, t4 = [r_.next() for r_ in tr]
                            S.op("dve", lambda e: e.tensor_tensor(out=t1[:], in0=a1[:], in1=cs, op=ALU.mult), r=[a1, cosT], w=[t1])
                            S.op("pool", lambda e: e.tensor_tensor(out=t2[:], in0=a2[:], in1=sn, op=ALU.mult), r=[a2, sinT], w=[t2])
                            S.op("dve", lambda e: e.tensor_tensor(out=t3[:], in0=a1[:], in1=sn, op=ALU.mult), r=[a1, sinT], w=[t3])
                            S.op("pool", lambda e: e.tensor_tensor(out=t4[:], in0=a2[:], in1=cs, op=ALU.mult), r=[a2, cosT], w=[t4])
                            ob = obr.next()
                            if kind == 0:
                                o1, o2, od = o1r.next(), o2r.next(), odr.next()
                                S.op("dve", lambda e: e.tensor_tensor(out=o1[:], in0=t1[:], in1=t2[:], op=ALU.subtract), r=[t1, t2], w=[o1])
                                S.op("pool", lambda e: e.tensor_tensor(out=o2[:], in0=t3[:], in1=t4[:], op=ALU.add), r=[t3, t4], w=[o2])
                                S.op("act", lambda e: e.copy(out=ob[:, 0, :], in_=o1[:]), r=[o1], w=[ob])
                                S.op("act", lambda e: e.copy(out=ob[:, 1, :], in_=o2[:]), r=[o2], w=[ob])
                                qd_b = qdec[:, h, :].unsqueeze(1).to_broadcast([128, 4, 128])
                                S.op("dve", lambda e: e.tensor_tensor(out=od[:, 0, :].rearrange("p (a b) -> p a b", a=4),
                                                                      in0=o1[:].rearrange("p (a b) -> p a b", a=4), in1=qd_b, op=ALU.mult),
                                     r=[o1, qdec], w=[od])
                                S.op("pool", lambda e: e.tensor_tensor(out=od[:, 1, :].rearrange("p (a b) -> p a b", a=4),
                                                                       in0=o2[:].rearrange("p (a b) -> p a b", a=4), in1=qd_b, op=ALU.mult),
                                     r=[o2, qdec], w=[od])
                                S.dma("sp", lambda e: e.dma_start(out=fm(QT)[:, 2 * h:2 * h + 2, tsl], in_=ob[:]), r=[ob])
                                S.dma("sp", lambda e: e.dma_start(out=fm(QDT)[:, 2 * h:2 * h + 2, tsl], in_=od[:]), r=[od])
                            else:
                                S.op("dve", lambda e: e.tensor_tensor(out=ob[:, 0, :], in0=t1[:], in1=t2[:], op=ALU.subtract), r=[t1, t2], w=[ob])
                                S.op("pool", lambda e: e.tensor_tensor(out=ob[:, 1, :], in0=t3[:], in1=t4[:], op=ALU.add), r=[t3, t4], w=[ob])
                                S.dma("sp", lambda e: e.dma_start(out=fm(KT)[:, 2 * h:2 * h + 2, tsl], in_=ob[:]), r=[ob])
                                pk, kd = pkr.next(), kdr.next()
                                for i in range(4):
                                    for j in range(2):
                                        S.op("pe", lambda e, i=i, j=j: e.transpose(pk[:, i * 2 + j, :], ob[:, j, i * 128:(i + 1) * 128], ident[:]),
                                             r=[ob, ident], w=[pk], same_ok=True)
                                S.op("act", lambda e: e.activation(out=kd[:].rearrange("p a b -> p (a b)"),
                                                                   in_=pk[:].rearrange("p a b -> p (a b)"), func=AF.Copy,
                                                                   scale=kdec[:, h:h + 1]), r=[pk, kdec], w=[kd])
                                S.dma("sp", lambda e: e.dma_start(out=tm(KD)[:, tb * 4:(tb + 1) * 4, h * 256:(h + 1) * 256], in_=kd[:]), r=[kd])
                pvr = rot(st, "r_pv", [128, 512], F32, 2, psum=True)
                ovr = rot(st, "r_ov", [128, 512], BF16, 3)
                for kind in range(2):
                    for cg in range(4):
                        c0 = 2048 + kind * 2048 + cg * 512
                        wb = wl.load(ret_w_in[l, :, c0:c0 + 512], 512)
                        dst = V if kind == 0 else SG
                        for tt in range(NT):
                            pv, ov = pvr.next(), ovr.next()
                            for kc in range(8):
                                S.op("pe", lambda e, kc=kc: e.matmul(pv[:], lhsT=xT[:, kc, tt * 128:(tt + 1) * 128], rhs=wb[:, kc, :],
                                                                      start=(kc == 0), stop=(kc == 7)),
                                     r=[wb, xT], w=[pv], same_ok=True)
                            if kind == 0:
                                copy_any(ov[:], pv[:], [pv], [ov])
                            else:
                                S.op("act", lambda e: e.activation(out=ov[:], in_=pv[:], func=AF.Silu), r=[pv], w=[ov])
                            S.dma("sp", lambda e: e.dma_start(out=dst[tt * 128:(tt + 1) * 128, cg * 512:(cg + 1) * 512], in_=ov[:]), r=[ov])
                S.barrier()

        def phase_ret_core():
            with ExitStack() as st:
                maskT = sb(st, "c_mask", [128, 4, 128], F32)
                S.dma("sp", lambda e: e.dma_start(out=maskT[:], in_=cst["maskT"]), w=[maskT])
                st_fs = [sb(st, f"c_stf{i}", [128, 2, 512], F32) for i in range(2)]
                st_bs = [sb(st, f"c_stb{i}", [128, 2, 512], BF16) for i in range(2)]
                qtr = rot(st, "c_qt", [128, 2, 128], BF16, 3)
                qdr = rot(st, "c_qd", [128, 2, 128], BF16, 3)
                ktr = rot(st, "c_kt", [128, 2, 128], BF16, 3)
                kdr = rot(st, "c_kd", [128, 256], BF16, 3)
                vr = rot(st, "c_v", [128, 512], BF16, 3)
                sgr = rot(st, "c_sg", [128, 512], BF16, 3)
                scmr = rot(st, "c_scm", [128, 128], BF16, 2)
                ynr = rot(st, "c_yn", [128, 512], F32, 2)
                zr = rot(st, "c_z", [128, 512], BF16, 2)
                str_ = rot(st, "c_st", [128, 6], F32, 2)
                mvr = rot(st, "c_mv", [128, 4], F32, 2)
                scp = rot(st, "c_scp", [128, 128], F32, 2, psum=True)
                ypr = rot(st, "c_yp", [128, 512], F32, 2, psum=True)
                upr = rot(st, "c_up", [128, 2, 512], F32, 2, psum=True)

                def loads(s, h, c):
                    tt = s * 16 + c
                    tsl = slice(tt * 128, (tt + 1) * 128)
                    qt, qd, kt, kd, v, sg = qtr.next(), qdr.next(), ktr.next(), kdr.next(), vr.next(), sgr.next()
                    S.dma("sp", lambda e: e.dma_start(out=qt[:], in_=fm(QT)[:, 2 * h:2 * h + 2, tsl]), w=[qt])
                    S.dma("sp", lambda e: e.dma_start(out=qd[:], in_=fm(QDT)[:, 2 * h:2 * h + 2, tsl]), w=[qd])
                    S.dma("sp", lambda e: e.dma_start(out=kt[:], in_=fm(KT)[:, 2 * h:2 * h + 2, tsl]), w=[kt])
                    S.dma("sp", lambda e: e.dma_start(out=kd[:], in_=KD[tsl, h * 256:(h + 1) * 256]), w=[kd])
                    S.dma("sp", lambda e: e.dma_start(out=v[:], in_=V[tsl, h * 512:(h + 1) * 512]), w=[v])
                    S.dma("sp", lambda e: e.dma_start(out=sg[:], in_=SG[tsl, h * 512:(h + 1) * 512]), w=[sg])
                    return qt, qd, kt, kd, v, sg

                items = [(s, h, c) for s in range(nseq) for hp in (0, 2) for c in range(16) for h in (hp, hp + 1)]
                nxt = loads(*items[0])
                for idx, (s, h, c) in enumerate(items):
                    st_f, st_b = st_fs[h % 2], st_bs[h % 2]
                    qt, qd, kt, kd, v, sg = nxt
                    if idx + 1 < len(items):
                        nxt = loads(*items[idx + 1])
                    tt = s * 16 + c
                    if c == 0:
                        S.op("pool", lambda e: e.memset(st_f[:], 0.0), w=[st_f])
                        S.op("pool", lambda e: e.memset(st_b[:], 0.0), w=[st_b])
                    sp_, yp, up = scp.next(), ypr.next(), upr.next()
                    for dc in range(2):
                        S.op("pe", lambda e, dc=dc: e.matmul(sp_[:], lhsT=kt[:, dc, :], rhs=qt[:, dc, :], start=(dc == 0), stop=(dc == 1)),
                             r=[kt, qt], w=[sp_], same_ok=True)
                    scm = scmr.next()
                    S.op("dve", lambda e: e.tensor_tensor(out=scm[:], in0=sp_[:], in1=maskT[:, h, :], op=ALU.mult), r=[sp_, maskT], w=[scm])
                    S.op("pe", lambda e: e.matmul(yp[:], lhsT=scm[:], rhs=v[:], start=True, stop=False), r=[scm, v], w=[yp], same_ok=True)
                    for dc in range(2):
                        S.op("pe", lambda e, dc=dc: e.matmul(yp[:], lhsT=qd[:, dc, :], rhs=st_b[:, dc, :], start=False, stop=(dc == 1)),
                             r=[qd, st_b], w=[yp], same_ok=True)
                    for dc in range(2):
                        S.op("pe", lambda e, dc=dc: e.matmul(up[:, dc, :], lhsT=kd[:, dc * 128:(dc + 1) * 128], rhs=v[:], start=True, stop=True),
                             r=[kd, v], w=[up], same_ok=True)
                    if c < 15:
                        S.op("dve", lambda e: e.scalar_tensor_tensor(out=st_f[:].rearrange("p a b -> p (a b)"),
                                                                      in0=st_f[:].rearrange("p a b -> p (a b)"), scalar=cdec[h],
                                                                      in1=up[:].rearrange("p a b -> p (a b)"), op0=ALU.mult, op1=ALU.add),
                             r=[st_f, up], w=[st_f])
                        S.op("act", lambda e: e.copy(out=st_b[:], in_=st_f[:]), r=[st_f], w=[st_b])
                    st_, mv, yn, z = str_.next(), mvr.next(), ynr.next(), zr.next()
                    S.op("dve", lambda e: e.bn_stats(out=st_[:], in_=yp[:]), r=[yp], w=[st_])
                    S.op("dve", lambda e: e.bn_aggr(out=mv[:, 0:2], in_=st_[:]), r=[st_], w=[mv])
                    S.op("dve", lambda e: e.tensor_scalar(out=mv[:, 2:3], in0=mv[:, 1:2], scalar1=1.0, scalar2=EPS,
                                                          op0=ALU.mult, op1=ALU.add), r=[mv], w=[mv])
                    S.op("act", lambda e: e.activation(out=mv[:, 2:3], in_=mv[:, 2:3], func=AF.Sqrt), r=[mv], w=[mv])
                    S.op("dve", lambda e: e.reciprocal(out=mv[:, 3:4], in_=mv[:, 2:3]), r=[mv], w=[mv])
                    S.op("dve", lambda e: e.tensor_scalar(out=yn[:], in0=yp[:], scalar1=mv[:, 0:1], scalar2=mv[:, 3:4],
                                                          op0=ALU.subtract, op1=ALU.mult), r=[yp, mv], w=[yn])
                    S.op("pool", lambda e: e.tensor_tensor(out=z[:], in0=yn[:], in1=sg[:], op=ALU.mult), r=[yn, sg], w=[z])
                    S.dma("pool", lambda e: e.dma_start(out=Z[tt * 128:(tt + 1) * 128, h * 512:(h + 1) * 512], in_=z[:]), r=[z])
                S.barrier()

        def phase_outproj_ln(Zsrc, F, w_ap, x_src, x_dst, l, which):
            FC = F // 128
            with ExitStack() as st:
                wsb = sb(st, "o_w", [128, FC, D], BF16)
                with ExitStack() as st2:
                    wl = WLoader(st2, 8, 512, 2, "ow")
                    for fg in range(FC // 8):
                        for hf in range(2):
                            wb = wl.load(w_ap[fg * 1024:(fg + 1) * 1024, hf * 512:(hf + 1) * 512], 512)
                            S.op("dve", lambda e: e.tensor_copy(out=wsb[:, fg * 8:(fg + 1) * 8, hf * 512:(hf + 1) * 512], in_=wb[:]),
                                 r=[wb], w=[wsb])
                    S.barrier()
                ln = LNEpilogue(st, l, which)
                zr = rot(st, "o_z", [128, F], BF16, 3)
                xr = rot(st, "o_x", [128, D], F32, 3)
                zTr = rot(st, "o_zT", [128, FC, 128], BF16, 2)
                yr = rot(st, "o_y", [128, D], F32, 2)
                pzr = rot(st, "o_pz", [128, FC, 128], BF16, 1, psum=True)
                pmr = rot(st, "o_pm", [128, 2, 512], F32, 2, psum=True)

                def loads(tt):
                    z, x = zr.next(), xr.next()
                    S.dma("sp", lambda e: e.dma_start(out=z[:], in_=Zsrc[tt * 128:(tt + 1) * 128, :]), w=[z])
                    S.dma("sp", lambda e: e.dma_start(out=x[:], in_=x_src[tt * 128:(tt + 1) * 128, :]), w=[x])
                    return z, x
                def front(tt, z, x):
                    pz, zT, pm, y = pzr.next(), zTr.next(), pmr.next(), yr.next()
                    for fc in range(FC):
                        S.op("pe", lambda e, fc=fc: e.transpose(pz[:, fc, :], z[:, fc * 128:(fc + 1) * 128], ident[:]),
                             r=[z, ident], w=[pz], same_ok=True)
                    for g8 in range(FC // 8):
                        copy_any(zT[:, g8 * 8:(g8 + 1) * 8, :], pz[:, g8 * 8:(g8 + 1) * 8, :], [pz], [zT])
                    for hf in range(2):
                        for fc in range(FC):
                            S.op("pe", lambda e, fc=fc, hf=hf: e.matmul(pm[:, hf, :], lhsT=zT[:, fc, :], rhs=wsb[:, fc, hf * 512:(hf + 1) * 512],
                                                                         start=(fc == 0), stop=(fc == FC - 1)),
                                 r=[zT, wsb], w=[pm], same_ok=True)
                    return (x, pm, y)

                def front_b(x, pm, y):
                    for hf in range(2):
                        S.op("dve", lambda e, hf=hf: e.scalar_tensor_tensor(out=y[:, hf * 512:(hf + 1) * 512], in0=x[:, hf * 512:(hf + 1) * 512],
                                                                             scalar=ALPHA, in1=pm[:, hf, :], op0=ALU.mult, op1=ALU.add),
                             r=[x, pm], w=[y])
                    return y

                ld = {0: loads(0)}
                if NT > 1:
                    ld[1] = loads(1)
                y_cur = front_b(*front(0, *ld.pop(0)))
                for tt in range(NT):
                    fa = None
                    if tt + 1 < NT:
                        zx = ld.pop(tt + 1)
                        if tt + 2 < NT:
                            ld[tt + 2] = loads(tt + 2)
                        fa = front(tt + 1, *zx)
                    ln.run(y_cur, tt, x_dst)
                    y_cur = front_b(*fa) if fa is not None else None
                S.barrier()

        def phase_uv_prep():
            R = 4
            NI = 4 * NEXP // (128 * R)
            uview = peer_u_flat.rearrange("(n p r) d -> n p r d", p=128, r=R)
            vview = peer_v_flat.rearrange("(n p r) d -> n p r d", p=128, r=R)
            oview = UVB.rearrange("(n p r) d -> n p r d", p=128, r=R)
            with ExitStack() as st:
                ufr = rot(st, "uv_uf", [128, R, D], F32, 3)
                vfr = rot(st, "uv_vf", [128, R, D], F32, 3)
                obr = rot(st, "uv_ob", [128, R, 2048], BF16, 3)

                def loads(n):
                    uf, vf = ufr.next(), vfr.next()
                    S.dma("sp", lambda e: e.dma_start(out=uf[:], in_=uview[n]), w=[uf])
                    S.dma("sp", lambda e: e.dma_start(out=vf[:], in_=vview[n]), w=[vf])
                    return uf, vf
                nxt = loads(0)
                for n in range(NI):
                    uf, vf = nxt
                    if n + 1 < NI:
                        nxt = loads(n + 1)
                    ob = obr.next()
                    S.op("dve", lambda e: e.tensor_copy(out=ob[:, :, 0:D], in_=uf[:]), r=[uf], w=[ob])
                    if n % 3 == 2:
                        S.op("pool", lambda e: e.tensor_copy(out=ob[:, :, D:2 * D], in_=vf[:]), r=[vf], w=[ob])
                    else:
                        S.op("act", lambda e: e.copy(out=ob[:, :, D:2 * D], in_=vf[:]), r=[vf], w=[ob])
                    S.dma("sp", lambda e: e.dma_start(out=oview[n], in_=ob[:]), r=[ob])
                S.barrier()

        def phase_peer_q(l):
            with ExitStack() as st:
                xT = load_xT_all(st)
                wl = WLoader(st, 8, 512, 2, "pw")
                ppr = rot(st, "p_pp", [128, 512], F32, 3, psum=True)
                obr = rot(st, "p_ob", [128, 512], BF16, 3)
                for g in range(4):
                    wb = wl.load(peer_w_q[l, :, g * 512:(g + 1) * 512], 512)
                    for j in range(4):
                        for tb in range(NTB):
                            pp, ob = ppr.next(), obr.next()
                            tsl = slice(tb * 512, (tb + 1) * 512)
                            for kc in range(8):
                                S.op("pe", lambda e, kc=kc: e.matmul(pp[:], lhsT=wb[:, kc, j * 128:(j + 1) * 128], rhs=xT[:, kc, tsl],
                                                                      start=(kc == 0), stop=(kc == 7)),
                                     r=[wb, xT], w=[pp], same_ok=True)
                            copy_any(ob[:], pp[:], [pp], [ob])
                            S.dma("sp", lambda e: e.dma_start(out=fm(QP)[:, g * 4 + j, tsl], in_=ob[:]), r=[ob])
                S.barrier()

        def phase_peer_main(l, x_src, x_dst):
            with ExitStack() as st:
                skb = sb(st, "m_skb", [128, 16, 128], BF16)
                with ExitStack() as st2:
                    skf = sb(st2, "m_skf", [128, 16, 128], F32)
                    S.dma("sp", lambda e: e.dma_start(out=skf[:], in_=peer_skT[l]), w=[skf])
                    S.op("dve", lambda e: e.tensor_copy(out=skb[:], in_=skf[:]), r=[skf], w=[skb])
                    S.barrier()
                io16 = sb(st, "m_io16", [128, 2, 16], F32)
                S.dma("sp", lambda e: e.dma_start(out=io16[:], in_=cst["iota16"]), w=[io16])
                ln = LNEpilogue(st, l, 1, ew="dve", npt=1, nbuf=1)
                xr = rot(st, "m_x", [128, D], F32, 3)
                qpr = rot(st, "m_qp", [128, 16, 128], BF16, 2)
                scps = ps(st, "m_scp", [128, 2, 512], F32)
                sc = sb(st, "m_sc", [128, 16, 128], F32)
                scr = rot(st, "m_scr", [128, 128], F32, 4)
                s16h = [Buf(f"s16h{i}") for i in range(16)]
                i16h = [Buf(f"i16h{i}") for i in range(16)]
                s16 = sb(st, "m_s16", [128, 16, 16], F32)
                i16u = sb(st, "m_i16u", [128, 16, 16], U32)
                i16f = sb(st, "m_i16f", [128, 16, 16], F32)
                candr = rot(st, "m_cand", [128, 16, 16], F32, 2)
                scr2 = rot(st, "m_scr2", [128, 256], F32, 2)
                tv = sb(st, "m_tv", [128, 8, 16], F32)
                posu = sb(st, "m_posu", [128, 8, 16], U32)
                posf = sb(st, "m_posf", [128, 8, 16], F32)
                posb = sb(st, "m_posb", [128, 8, 16], U32)
                bfm = sb(st, "m_bfm", [128, 8, 16], F32)
                a16 = sb(st, "m_a16", [128, 8, 16], F32)
                eq4 = rot(st, "m_eq4", [128, 8, 16, 16], F32, 1)
                sel1 = sb(st, "m_sel1", [128, 8, 16], F32)
                sel2 = sb(st, "m_sel2", [128, 8, 16], F32)
                idxf = sb(st, "m_idxf", [128, 128], F32)
                idxi = rot(st, "m_idxi", [128, 128], I32, 2)
                ex = sb(st, "m_ex", [128, 8, 16], F32)
                ssum = sb(st, "m_ssum", [128, 8], F32)
                gater = rot(st, "m_gate", [128, 8, 16], F32, 2)
                dots = sb(st, "m_dots", [128, 128], F32)
                wgt = sb(st, "m_wgt", [128, 128], F32)
                g1 = rot(st, "m_g1", [128, 16], F32, 2)
                g2 = rot(st, "m_g2", [128, 16], F32, 2)
                gbr = rot(st, "m_gb", [128, 2048], BF16, 24)
                junkb = rot(st, "m_junkb", [128, D], BF16, 4)
                junka = rot(st, "m_junka", [128, D], BF16, 2)
                dotsb = [Buf("dots_e"), Buf("dots_o")]
                xbr = rot(st, "m_xb", [128, D], BF16, 2)
                dgr = rot(st, "m_dg", [128, 8, 128], BF16, 3)
                accr = rot(st, "m_accp", [128, 2, 512], F32, 2, psum=True)
                g3 = rot(st, "m_g3", [128, 16], F32, 2)
                yr = rot(st, "m_y", [128, D], F32, 1)

                def select(tt, holder):
                    x, qp = xr.next(), qpr.next()
                    S.dma("sp", lambda e: e.dma_start(out=x[:], in_=x_src[tt * 128:(tt + 1) * 128, :]), w=[x])
                    S.dma("sp", lambda e: e.dma_start(out=qp[:], in_=fm(QP)[:, :, tt * 128:(tt + 1) * 128]), w=[qp])
                    for rnd in range(2):
                        for hq in range(8):
                            hp = rnd * 8 + hq
                            S.op("pe", lambda e, hp=hp, hq=hq: e.matmul(scps[:, hq // 4, (hq % 4) * 128:(hq % 4 + 1) * 128], lhsT=qp[:, hp, :], rhs=skb[:, hp, :],
                                                                         start=True, stop=True), r=[qp, skb], w=[scps], same_ok=True)
                        for bk in range(2):
                            S.op("act", lambda e, bk=bk, rnd=rnd: e.copy(out=sc[:, rnd * 8 + bk * 4:rnd * 8 + (bk + 1) * 4, :].rearrange("p a b -> p (a b)"),
                                                                          in_=scps[:, bk, :]), r=[scps], w=[sc])
                    xb = xbr.next()
                    S.op("act", lambda e: e.copy(out=xb[:], in_=x[:]), r=[x], w=[xb])
                    yield
                    for hp0 in range(0, 16, 2):
                        pr = [(hp0, scr.next()), (hp0 + 1, scr.next())]
                        for (hp, sr) in pr:
                            S.op("dve", lambda e: e.max(out=s16[:, hp, 0:8], in_=sc[:, hp, :]), r=[sc], w=[s16h[hp]])
                        for (hp, sr) in pr:
                            S.op("dve", lambda e: e.max_index(out=i16u[:, hp, 0:8], in_max=s16[:, hp, 0:8], in_values=sc[:, hp, :]), r=[sc, s16h[hp]], w=[i16h[hp]])
                        for (hp, sr) in pr:
                            S.op("dve", lambda e: e.match_replace(out=sr[:], in_to_replace=s16[:, hp, 0:8], in_values=sc[:, hp, :], imm_value=-1e30),
                                 r=[sc, s16h[hp]], w=[sr])
                        for (hp, sr) in pr:
                            S.op("dve", lambda e: e.max(out=s16[:, hp, 8:16], in_=sr[:]), r=[sr], w=[s16h[hp]])
                        for (hp, sr) in pr:
                            S.op("dve", lambda e: e.max_index(out=i16u[:, hp, 8:16], in_max=s16[:, hp, 8:16], in_values=sr[:]), r=[sr, s16h[hp]], w=[i16h[hp]])
                        if hp0 % 4 == 2:
                            yield
                    S.op("dve", lambda e: e.tensor_copy(out=i16f[:], in_=i16u[:]), r=i16h, w=[i16f])
                    i4 = i16f[:].rearrange("p (h two) k -> p h two k", two=2)
                    i1v, i2v = i4[:, :, 0, :], i4[:, :, 1, :]
                    S.op("dve", lambda e: e.tensor_scalar(out=i1v, in0=i1v, scalar1=128.0, scalar2=None, op0=ALU.mult), r=[i16f], w=[i16f])
                    for h in range(8):
                        cand, s2 = candr.next(), scr2.next()
                        a_b = lambda t_, hp_: t_[:, hp_, :].unsqueeze(2).to_broadcast([128, 16, 16])
                        b_b = lambda t_, hp_: t_[:, hp_, :].unsqueeze(1).to_broadcast([128, 16, 16])
                        S.op("dve", lambda e: e.tensor_tensor(out=cand[:], in0=a_b(s16, 2 * h), in1=b_b(s16, 2 * h + 1), op=ALU.add), r=[s16h[2 * h], s16h[2 * h + 1]], w=[cand])
                        cf = cand[:].rearrange("p a b -> p (a b)")
                        S.op("dve", lambda e: e.max(out=tv[:, h, 0:8], in_=cf), r=[cand], w=[tv])
                        S.op("dve", lambda e: e.max_index(out=posu[:, h, 0:8], in_max=tv[:, h, 0:8], in_values=cf), r=[cand, tv], w=[posu])
                        S.op("dve", lambda e: e.match_replace(out=s2[:], in_to_replace=tv[:, h, 0:8], in_values=cf, imm_value=-1e30), r=[cand, tv], w=[s2])
                        S.op("dve", lambda e: e.max(out=tv[:, h, 8:16], in_=s2[:]), r=[s2], w=[tv])
                        S.op("dve", lambda e: e.max_index(out=posu[:, h, 8:16], in_max=tv[:, h, 8:16], in_values=s2[:]), r=[s2, tv], w=[posu])
                        if h % 2 == 1:
                            yield
                    S.op("dve", lambda e: e.tensor_copy(out=posf[:], in_=posu[:]), r=[posu], w=[posf])
                    S.op("dve", lambda e: e.tensor_single_scalar(out=posb[:], in_=posu[:], scalar=15, op=ALU.bitwise_and), r=[posu], w=[posb])
                    S.op("dve", lambda e: e.tensor_copy(out=bfm[:], in_=posb[:]), r=[posb], w=[bfm])
                    S.op("dve", lambda e: e.tensor_tensor(out=a16[:], in0=posf[:], in1=bfm[:], op=ALU.subtract), r=[posf, bfm], w=[a16])
                    shp = [128, 8, 16, 16]
                    for (keyt, iot, valv, selo) in ((a16, 1, i1v, sel1), (bfm, 0, i2v, sel2)):
                        e4 = eq4.next()
                        S.op("dve", lambda e: e.tensor_tensor(out=e4[:], in0=keyt[:].unsqueeze(3).to_broadcast(shp),
                                                              in1=io16[:, iot, :].unsqueeze(1).unsqueeze(1).to_broadcast(shp), op=ALU.is_equal),
                             r=[keyt, io16], w=[e4])
                        S.op("dve", lambda e: e.tensor_tensor(out=e4[:], in0=e4[:], in1=valv.unsqueeze(2).to_broadcast(shp), op=ALU.mult),
                             r=[e4, i16f], w=[e4])
                        S.op("dve", lambda e: e.reduce_sum(out=selo[:].rearrange("p a b -> p (a b)"), in_=e4[:].rearrange("p a b c -> p (a b) c"), axis=AX.X),
                             r=[e4], w=[selo])
                        yield
                    S.op("dve", lambda e: e.tensor_tensor(out=idxf[:], in0=sel1[:].rearrange("p a b -> p (a b)"), in1=sel2[:].rearrange("p a b -> p (a b)"), op=ALU.add),
                         r=[sel1, sel2], w=[idxf])
                    gate = gater.next()
                    S.op("dve", lambda e: e.tensor_tensor(out=ex[:], in0=tv[:], in1=tv[:, :, 0:1].to_broadcast([128, 8, 16]), op=ALU.subtract), r=[tv], w=[ex])
                    S.op("act", lambda e: e.activation(out=ex[:], in_=ex[:], func=AF.Exp), r=[ex], w=[ex])
                    S.op("dve", lambda e: e.reduce_sum(out=ssum[:], in_=ex[:], axis=AX.X), r=[ex], w=[ssum])
                    S.op("dve", lambda e: e.reciprocal(out=ssum[:], in_=ssum[:]), r=[ssum], w=[ssum])
                    S.op("dve", lambda e: e.tensor_tensor(out=gate[:], in0=ex[:], in1=ssum[:].unsqueeze(2).to_broadcast([128, 8, 16]), op=ALU.mult),
                         r=[ex, ssum], w=[gate])
                    ii = idxi.next()
                    S.op("dve", lambda e: e.tensor_scalar(out=idxf[:], in0=idxf[:], scalar1=float(NEXP - 1), scalar2=float(l * NEXP), op0=ALU.min, op1=ALU.add),
                         r=[idxf], w=[idxf])
                    S.op("dve", lambda e: e.tensor_copy(out=ii[:], in_=idxf[:]), r=[idxf], w=[ii])
                    holder.update(dict(x=x, xb=xb, ii=ii, gate=gate))

                GS = 8
                NG = 128 // GS
                CG = 1.5957691216057308

                def dcol(sl):
                    return (sl % 2) * 64 + sl // 2

                def dv(g):
                    return dots[:].rearrange("p (two k) -> p two k", two=2)[:, :, g * 4:(g + 1) * 4]

                def sv(t_, g):
                    return t_.rearrange("p (k two) -> p two k", two=2)[:, :, g * 4:(g + 1) * 4]

                def t3(t_):
                    return t_[:, 0:GS].rearrange("p (two k) -> p two k", two=2)

                def stage_a(sel, g, j0, j1):
                    xb, ii = sel["xb"], sel["ii"]
                    gbs = sel["gbs"].setdefault(g, [])
                    for j in range(j0, j1):
                        sl = g * GS + j
                        gb, jb = gbr.next(), junkb.next()
                        gbs.append(gb)
                        S.dma("pool", lambda e, sl=sl: e.indirect_dma_start(out=gb[:], out_offset=None, in_=UVB,
                                                                            in_offset=bass.IndirectOffsetOnAxis(ap=ii[:, sl:sl + 1], axis=0)),
                              r=[ii], w=[gb])
                        if j % 2 == 0:
                            S.op("dve", lambda e, sl=sl: e.scalar_tensor_tensor(out=jb[:], in0=gb[:, 0:D], scalar=1.0, in1=xb[:], op0=ALU.mult, op1=ALU.mult,
                                                                                 accum_out=dots[:, dcol(sl):dcol(sl) + 1]), r=[gb, xb], w=[jb, dotsb[sl % 2]])
                        else:
                            S.op("dve", lambda e: e.tensor_tensor(out=jb[:], in0=gb[:, 0:D], in1=xb[:], op=ALU.mult), r=[gb, xb], w=[jb])
                            ja = junka.next()
                            S.op("act", lambda e, sl=sl: e.activation(out=ja[:], in_=jb[:], func=AF.Copy, accum_out=dots[:, dcol(sl):dcol(sl) + 1]),
                                 r=[jb], w=[ja, dotsb[sl % 2]])

                def stage_b1(sel, g):
                    hs = slice(g * GS, (g + 1) * GS)
                    a, b_ = g1.next(), g2.next()
                    sel["g12"] = (a, b_)
                    S.op("dve", lambda e: e.tensor_tensor(out=t3(a), in0=dv(g), in1=dv(g), op=ALU.mult), r=dotsb, w=[a])
                    S.op("dve", lambda e: e.tensor_scalar(out=a[:, 0:GS], in0=a[:, 0:GS], scalar1=0.044715, scalar2=1.0, op0=ALU.mult, op1=ALU.add), r=[a], w=[a])
                    S.op("dve", lambda e: e.tensor_tensor(out=t3(a), in0=t3(a), in1=dv(g), op=ALU.mult), r=[a] + dotsb, w=[a])
                    S.op("act", lambda e: e.activation(out=b_[:, 0:GS], in_=a[:, 0:GS], func=AF.Exp, scale=-CG), r=[a], w=[b_])

                def stage_b2(sel, g):
                    gate, accp = sel["gate"], sel["accp"]
                    gf = gate[:].rearrange("p a b -> p (a b)")
                    hs = slice(g * GS, (g + 1) * GS)
                    a, b_ = sel["g12"]
                    c_ = g3.next()
                    gbs = sel["gbs"][g]
                    S.op("dve", lambda e: e.tensor_scalar(out=b_[:, 0:GS], in0=b_[:, 0:GS], scalar1=1.0, scalar2=None, op0=ALU.add), r=[b_], w=[b_])
                    S.op("dve", lambda e: e.reciprocal(out=c_[:, 0:GS], in_=b_[:, 0:GS]), r=[b_], w=[c_])
                    S.op("dve", lambda e: e.tensor_tensor(out=t3(c_), in0=t3(c_), in1=dv(g), op=ALU.mult), r=[c_] + dotsb, w=[c_])
                    S.op("dve", lambda e: e.tensor_tensor(out=sv(wgt[:], g), in0=t3(c_), in1=sv(gf, g), op=ALU.mult), r=[c_, gate], w=[wgt])

                def stage_b2b(sel, g):
                    accp = sel["accp"]
                    hs = slice(g * GS, (g + 1) * GS)
                    gbs = sel["gbs"][g]
                    dg = dgr.next()
                    for j in range(GS):
                        sl = g * GS + j
                        S.op("act", lambda e, j=j, sl=sl: e.activation(out=dg[:, j, :], in_=ident[:], func=AF.Copy, scale=wgt[:, sl:sl + 1]),
                             r=[ident, wgt], w=[dg])
                    for j in range(GS):
                        sl = g * GS + j
                        for hf in range(2):
                            S.op("pe", lambda e, j=j, hf=hf: e.matmul(accp[:, hf, :], lhsT=dg[:, j, :], rhs=gbs[j][:, D + hf * 512:D + (hf + 1) * 512],
                                                                       start=(sl == 0), stop=(sl == 127)),
                                 r=[dg, gbs[j]], w=[accp], same_ok=True)
                    del sel["gbs"][g]

                def finish(sel, tt):
                    x, accp = sel["x"], sel["accp"]
                    y = yr.next()
                    for hf in range(2):
                        S.op("dve", lambda e, hf=hf: e.scalar_tensor_tensor(out=y[:, hf * 512:(hf + 1) * 512], in0=x[:, hf * 512:(hf + 1) * 512],
                                                                             scalar=ALPHA, in1=accp[:, hf, :], op0=ALU.mult, op1=ALU.add),
                             r=[x, accp], w=[y])
                    ln.run(y, tt, x_dst)

                cur = {}
                for _ in select(0, cur):
                    pass
                prev = None
                for tt in range(NT):
                    nxt = {}
                    gen = select(tt + 1, nxt) if tt + 1 < NT else None
                    cur["accp"] = accr.next()
                    cur["gbs"] = {}
                    pend = None
                    for g in range(NG):
                        stage_a(cur, g, 0, 2)
                        if pend is not None:
                            stage_b1(cur, pend)
                        stage_a(cur, g, 2, 5)
                        if pend is not None:
                            stage_b2(cur, pend)
                        stage_a(cur, g, 5, GS)
                        if pend is not None:
                            stage_b2b(cur, pend)
                        pend = g
                        if g == 1 and prev is not None:
                            finish(*prev)
                            prev = None
                        if gen is not None:
                            if next(gen, "done") == "done":
                                gen = None
                    stage_b1(cur, pend)
                    stage_b2(cur, pend)
                    stage_b2b(cur, pend)
                    if gen is not None:
                        for _ in gen:
                            pass
                    prev = (cur, tt)
                    cur = nxt
                finish(*prev)
                S.barrier()

        def phase_tok_proj_rope(w_ap, scale, dstT, v_ap=None):
            with ExitStack() as st:
                xT = load_xT_all(st)
                dcos = sb(st, "t_cos", [128, 16, 8], F32)
                dsin = sb(st, "t_sin", [128, 16, 8], F32)
                S.dma("sp", lambda e: e.dma_start(out=dcos[:], in_=cst["dcos"]), w=[dcos])
                S.dma("sp", lambda e: e.dma_start(out=dsin[:], in_=cst["dsin"]), w=[dsin])
                nW = 2 if v_ap is not None else 1
                wsb = sb(st, "t_w", [128, 8, 1024 * nW], BF16)
                with ExitStack() as st2:
                    wl = WLoader(st2, 8, 512, 2, "tw")
                    for wi, wa in enumerate([w_ap, v_ap][:nW]):
                        for hf in range(2):
                            wb = wl.load(wa[:, hf * 512:(hf + 1) * 512], 512)
                            S.op("dve", lambda e: e.tensor_copy(out=wsb[:, :, wi * 1024 + hf * 512:wi * 1024 + (hf + 1) * 512], in_=wb[:]),
                                 r=[wb], w=[wsb])
                    S.barrier()
                ppr = rot(st, "t_pp", [128, 2, 512], F32, 2, psum=True)
                pTr = rot(st, "t_pT", [128, 8, 128], BF16, 2, psum=True)
                kfr = rot(st, "t_kf", [128, 16, 64], F32, 2)
                tmr = [rot(st, f"t_tm{i}", [128, 16, 8], F32, 2) for i in range(4)]
                kbr = rot(st, "t_kb", [128, D], BF16, 2)
                kTr = rot(st, "t_kT", [128, 8, 128], BF16, 2)
                vbr = rot(st, "t_vb", [128, D], BF16, 2)
                for tt in range(NT):
                    pp, kf, kb, pT, kT = ppr.next(), kfr.next(), kbr.next(), pTr.next(), kTr.next()
                    for hf in range(2):
                        for kc in range(8):
                            S.op("pe", lambda e, kc=kc, hf=hf: e.matmul(pp[:, hf, :], lhsT=xT[:, kc, tt * 128:(tt + 1) * 128],
                                                                         rhs=wsb[:, kc, hf * 512:(hf + 1) * 512], start=(kc == 0), stop=(kc == 7)),
                                 r=[xT, wsb], w=[pp], same_ok=True)
                    kff = kf[:].rearrange("p a b -> p (a b)")
                    for hf in range(2):
                        S.op("act", lambda e, hf=hf: e.mul(out=kff[:, hf * 512:(hf + 1) * 512], in_=pp[:, hf, :], mul=scale), r=[pp], w=[kf])
                    pt = tt % 16
                    cs = dcos[:, pt, :].unsqueeze(1).to_broadcast([128, 16, 8])
                    sn = dsin[:, pt, :].unsqueeze(1).to_broadcast([128, 16, 8])
                    t1, t2, t3, t4 = [r_.next() for r_ in tmr]
                    x1, x2 = kf[:, :, 0:8], kf[:, :, 8:16]
                    S.op("dve", lambda e: e.tensor_tensor(out=t1[:], in0=x1, in1=cs, op=ALU.mult), r=[kf, dcos], w=[t1])
                    S.op("dve", lambda e: e.tensor_tensor(out=t2[:], in0=x2, in1=sn, op=ALU.mult), r=[kf, dsin], w=[t2])
                    S.op("dve", lambda e: e.tensor_tensor(out=t3[:], in0=x1, in1=sn, op=ALU.mult), r=[kf, dsin], w=[t3])
                    S.op("dve", lambda e: e.tensor_tensor(out=t4[:], in0=x2, in1=cs, op=ALU.mult), r=[kf, dcos], w=[t4])
                    S.op("dve", lambda e: e.tensor_tensor(out=x1, in0=t1[:], in1=t2[:], op=ALU.subtract), r=[t1, t2], w=[kf])
                    S.op("dve", lambda e: e.tensor_tensor(out=x2, in0=t3[:], in1=t4[:], op=ALU.add), r=[t3, t4], w=[kf])
                    S.op("act", lambda e: e.copy(out=kb[:], in_=kff), r=[kf], w=[kb])
                    for c in range(8):
                        S.op("pe", lambda e, c=c: e.transpose(pT[:, c, :], kb[:, c * 128:(c + 1) * 128], ident[:]), r=[kb, ident], w=[pT], same_ok=True)
                    S.op("dve", lambda e: e.tensor_copy(out=kT[:], in_=pT[:]), r=[pT], w=[kT])
                    S.dma("sp", lambda e: e.dma_start(out=fm(dstT)[:, :, tt * 128:(tt + 1) * 128], in_=kT[:]), r=[kT])
                    if v_ap is not None:
                        pv, vb = ppr.next(), vbr.next()
                        for hf in range(2):
                            for kc in range(8):
                                S.op("pe", lambda e, kc=kc, hf=hf: e.matmul(pv[:, hf, :], lhsT=xT[:, kc, tt * 128:(tt + 1) * 128],
                                                                             rhs=wsb[:, kc, 1024 + hf * 512:1024 + (hf + 1) * 512],
                                                                             start=(kc == 0), stop=(kc == 7)),
                                     r=[xT, wsb], w=[pv], same_ok=True)
                        for hf in range(2):
                            copy_any(vb[:, hf * 512:(hf + 1) * 512], pv[:, hf, :], [pv], [vb])
                        S.dma("sp", lambda e: e.dma_start(out=VS[tt * 128:(tt + 1) * 128, :], in_=vb[:]), r=[vb])
                S.barrier()

        def phase_attn(j, layer_idx):
            lam_init = 0.8 - 0.6 * math.exp(-0.3 * layer_idx)
            with ExitStack() as st:
                lamt = sb(st, "a_lamt", [128, 256], F32)
                lj = sb(st, "a_lj", [128, 64], F32)
                lv = sb(st, "a_lv", [128, 8], F32)
                gsub = sb(st, "a_gsub", [128, 128], F32)
                tri = sb(st, "a_tri", [128, 128], BF16)
                S.dma("sp", lambda e: e.dma_start(out=lamt[:], in_=diff_lambda[j].partition_broadcast(128)), w=[lamt])
                S.dma("sp", lambda e: e.dma_start(out=gsub[:], in_=diff_subln_g[j].partition_broadcast(128)), w=[gsub])
                S.dma("sp", lambda e: e.dma_start(out=tri[:], in_=cst["tri"]), w=[tri])
                S.op("dve", lambda e: e.scalar_tensor_tensor(out=lj[:], in0=lamt[:, 0:64], scalar=1.0, in1=lamt[:, 64:128], op0=ALU.mult, op1=ALU.mult,
                                                              accum_out=lv[:, 0:1]), r=[lamt], w=[lj, lv])
                S.op("dve", lambda e: e.scalar_tensor_tensor(out=lj[:], in0=lamt[:, 128:192], scalar=1.0, in1=lamt[:, 192:256], op0=ALU.mult, op1=ALU.mult,
                                                              accum_out=lv[:, 1:2]), r=[lamt], w=[lj, lv])
                S.op("act", lambda e: e.activation(out=lv[:, 2:4], in_=lv[:, 0:2], func=AF.Exp), r=[lv], w=[lv])
                S.op("dve", lambda e: e.tensor_tensor(out=lv[:, 4:5], in0=lv[:, 3:4], in1=lv[:, 2:3], op=ALU.subtract), r=[lv], w=[lv])
                S.op("dve", lambda e: e.tensor_scalar(out=lv[:, 5:6], in0=lv[:, 4:5], scalar1=-lam_init, scalar2=None, op0=ALU.add), r=[lv], w=[lv])
                S.op("dve", lambda e: e.tensor_scalar(out=gsub[:], in0=gsub[:], scalar1=(1.0 - lam_init), scalar2=None, op0=ALU.mult), r=[gsub], w=[gsub])
                neglam = lv[:, 5:6]

                kTr = rot(st, "a_kT", [128, SEQ], BF16, 2)
                qTr = rot(st, "a_qT", [128, SEQ], BF16, 2)
                vhr = rot(st, "a_vh", [128, 16, 129], BF16, 2)
                for t_ in vhr.tiles:
                    S.op("pool", lambda e, t_=t_: e.memset(t_[:, :, 128:129], 1.0), w=[t_])
                o1b = sb(st, "a_o1", [128, 16, 128], F32)
                oat = rot(st, "a_oat", [128, 16, 128], BF16, 2)
                Er = rot(st, "a_E", [128, 512], BF16, 3)
                tmpr = rot(st, "a_tmp", [128, 128], F32, 2)
                o2r = rot(st, "a_o2", [128, 128], F32, 2)
                jr = rot(st, "a_j", [128, 128], F32, 2)
                rsr = rot(st, "a_rs", [128, 4], F32, 4)
                spr = rot(st, "a_sp", [128, 512], F32, 3, psum=True)
                accp = ps(st, "a_acc", [128, 4, 512], F32)
                accb = [Buf(f"a_accb{i}") for i in range(4)]

                def loads(s, h):
                    kT, qT, vh = kTr.next(), qTr.next(), vhr.next()
                    S.dma("sp", lambda e: e.dma_start(out=kT[:], in_=fm(KTS)[:, h, s * SEQ:(s + 1) * SEQ]), w=[kT])
                    S.dma("sp", lambda e: e.dma_start(out=qT[:], in_=fm(QTD)[:, h, s * SEQ:(s + 1) * SEQ]), w=[qT])
                    S.dma("sp", lambda e: e.dma_start(out=vh[:, :, 0:128], in_=tm(VS)[:, s * 16:(s + 1) * 16, h * 128:(h + 1) * 128]), w=[vh])
                    return kT, qT, vh
                items = [(s, h) for s in range(nseq) for h in range(8)]
                nxt = loads(*items[0])
                for idx, (s, h) in enumerate(items):
                    kT, qT, vh = nxt
                    if idx + 1 < len(items):
                        nxt = loads(*items[idx + 1])
                    ot = oat.next()
                    for m in range(2):
                        msl = slice(m * 64, (m + 1) * 64)
                        for G in range(4):
                            nkt = 4 * G + 4

                            def score(kt):
                                qi0 = max(0, kt - 4 * G)
                                sp_ = spr.next()
                                qs = slice(G * 512 + qi0 * 128, (G + 1) * 512)
                                es_ = slice(qi0 * 128, 512)
                                S.op("pe", lambda e: e.matmul(sp_[:, es_], lhsT=kT[msl, kt * 128:(kt + 1) * 128], rhs=qT[msl, qs], start=True, stop=True),
                                     r=[kT, qT], w=[sp_], same_ok=True)
                                return sp_
                            sp_next = score(0)
                            for kt in range(nkt):
                                qi0 = max(0, kt - 4 * G)
                                sp_, E = sp_next, Er.next()
                                es_ = slice(qi0 * 128, 512)
                                if kt + 1 < nkt:
                                    sp_next = score(kt + 1)
                                S.op("act", lambda e: e.activation(out=E[:, es_], in_=sp_[:, es_], func=AF.Exp), r=[sp_], w=[E])
                                if kt >= 4 * G:
                                    ds_ = slice(qi0 * 128, (qi0 + 1) * 128)
                                    S.op("dve", lambda e: e.tensor_tensor(out=E[:, ds_], in0=E[:, ds_], in1=tri[:], op=ALU.mult), r=[E, tri], w=[E])
                                for qi in range(qi0, 4):
                                    S.op("pe", lambda e, qi=qi: e.matmul(accp[:, qi, 0:129], lhsT=E[:, qi * 128:(qi + 1) * 128], rhs=vh[:, kt, :],
                                                                          start=(kt == 0), stop=(kt == 4 * G + qi)),
                                         r=[E, vh], w=[accb[qi]], same_ok=True)
                            for qi in range(4):
                                qt_ = 4 * G + qi
                                rs = rsr.next()
                                S.op("dve", lambda e: e.reciprocal(out=rs[:, 0:1], in_=accp[:, qi, 128:129]), r=[accb[qi]], w=[rs])
                                if m == 0:
                                    S.op("dve", lambda e: e.tensor_scalar(out=o1b[:, qt_, :], in0=accp[:, qi, 0:128], scalar1=rs[:, 0:1], scalar2=None,
                                                                          op0=ALU.mult), r=[accb[qi], rs], w=[o1b])
                                else:
                                    tmp, o2, jj = tmpr.next(), o2r.next(), jr.next()
                                    S.op("dve", lambda e: e.tensor_scalar(out=tmp[:], in0=accp[:, qi, 0:128], scalar1=rs[:, 0:1], scalar2=None,
                                                                          op0=ALU.mult), r=[accb[qi], rs], w=[tmp])
                                    S.op("dve", lambda e: e.scalar_tensor_tensor(out=o2[:], in0=tmp[:], scalar=neglam, in1=o1b[:, qt_, :],
                                                                                  op0=ALU.mult, op1=ALU.add), r=[tmp, lv, o1b], w=[o2])
                                    S.op("dve", lambda e: e.scalar_tensor_tensor(out=jj[:], in0=o2[:], scalar=1.0, in1=o2[:], op0=ALU.mult, op1=ALU.mult,
                                                                                  accum_out=rs[:, 1:2]), r=[o2], w=[jj, rs])
                                    S.op("dve", lambda e: e.tensor_scalar(out=rs[:, 2:3], in0=rs[:, 1:2], scalar1=1.0 / 128.0, scalar2=EPS,
                                                                          op0=ALU.mult, op1=ALU.add), r=[rs], w=[rs])
                                    S.op("act", lambda e: e.activation(out=rs[:, 2:3], in_=rs[:, 2:3], func=AF.Sqrt), r=[rs], w=[rs])
                                    S.op("dve", lambda e: e.reciprocal(out=rs[:, 3:4], in_=rs[:, 2:3]), r=[rs], w=[rs])
                                    S.op("dve", lambda e: e.scalar_tensor_tensor(out=ot[:, qt_, :], in0=o2[:], scalar=rs[:, 3:4], in1=gsub[:],
                                                                                   op0=ALU.mult, op1=ALU.mult), r=[o2, rs, gsub], w=[ot])
                    S.dma("sp", lambda e: e.dma_start(out=tm(OATT)[:, s * 16:(s + 1) * 16, h * 128:(h + 1) * 128], in_=ot[:]), r=[ot])
                S.barrier()

        steps = []
        phase_init()
        phase_uv_prep()
        cur = x_in
        for l in range(DEPTH):
            last = (l == DEPTH - 1)
            if l < 2:
                steps.append(lambda l=l, cur=cur: (phase_ret_proj(l), phase_ret_core(),
                                                   phase_outproj_ln(Z, 2048, ret_w_out[l], cur, X, l, 0)))
            else:
                steps.append(lambda l=l, cur=cur: (phase_tok_proj_rope(diff_w_q[l - 2], 0.125, QTD), phase_attn(l - 2, l),
                                                   phase_outproj_ln(OATT, 1024, diff_w_out[l - 2], cur, X, l, 0)))
            cur = X
            steps.append(lambda l=l, last=last: (phase_peer_q(l), phase_peer_main(l, X, out if last else X)))
            if l == 1:
                steps.append(lambda: phase_tok_proj_rope(kv_w[:, 0:1024], 1.0, KTS, v_ap=kv_w[:, 1024:2048]))
        for i, f in enumerate(steps):
            if i < n_steps:
                f()
        S.barrier()
        print("instructions issued:", S.n_ins, {k: c.count for k, c in S.ctr.items()})
    return nc


_CACHE = {}


def _in_maps(inputs, nseq, n_cores):
    c = make_consts()
    x = np.asarray(inputs["x"], dtype=np.float32).reshape(-1, D)
    skT = np.ascontiguousarray(np.asarray(inputs["peer_subkeys"], dtype=np.float32).transpose(0, 4, 1, 2, 3).reshape(4, 128, 16, 128))
    shared = {
        "ret_w_in": np.asarray(inputs["ret_w_in"], np.float32), "ret_w_out": np.asarray(inputs["ret_w_out"], np.float32),
        "kv_w": np.asarray(inputs["kv_w"], np.float32), "diff_w_q": np.asarray(inputs["diff_w_q"], np.float32),
        "diff_lambda": np.asarray(inputs["diff_lambda"], np.float32).reshape(2, 256),
        "diff_subln_g": np.asarray(inputs["diff_subln_g"], np.float32), "diff_w_out": np.asarray(inputs["diff_w_out"], np.float32),
        "peer_w_q": np.asarray(inputs["peer_w_q"], np.float32), "peer_skT": skT,
        "peer_u": np.asarray(inputs["peer_u"], np.float32), "peer_v": np.asarray(inputs["peer_v"], np.float32),
        "ln_g": np.asarray(inputs["ln_g"], np.float32), "ln_b": np.asarray(inputs["ln_b"], np.float32),
    }
    for k in CONST_SPECS:
        shared["c_" + k] = c[k]
    maps = []
    tk = nseq * SEQ
    for i in range(n_cores):
        m = dict(shared)
        m["x"] = np.ascontiguousarray(x[i * tk:(i + 1) * tk])
        maps.append(m)
    return maps


def kernel(**inputs):
    nseq = 2
    if "nc" not in _CACHE:
        _CACHE["nc"] = build_program(nseq=nseq)
    nc = _CACHE["nc"]
    maps = _in_maps(inputs, nseq, N_CORES)
    res = run_bass_kernel_spmd(nc, maps, core_ids=list(range(N_CORES)))
    outs = [np.asarray(r["out"], dtype=np.float32) for r in res.results]
    return np.concatenate(outs, axis=0).reshape(16, SEQ, D)
```

```python
import math
from contextlib import ExitStack

import numpy as np
import ml_dtypes
import concourse.bass as bass
import concourse.mybir as mybir
from concourse.bass_utils import run_bass_kernel_spmd

F32 = mybir.dt.float32
BF16 = mybir.dt.bfloat16
I32 = mybir.dt.int32
U32 = mybir.dt.uint32
AF = mybir.ActivationFunctionType
ALU = mybir.AluOpType
AX = mybir.AxisListType

D = 1024
SEQ = 2048
DEPTH = 4
ALPHA = (2 * DEPTH) ** 0.25
EPS = 1e-5
NEXP = 16384
N_CORES = 8


class Ctr:
    def __init__(self, sem, step):
        self.sem = sem
        self.step = step
        self.count = 0


class Buf:
    def __init__(self, name=""):
        self.name = name
        self.w = None
        self.r = {}


class T:
    def __init__(self, t, name):
        self.t = t
        self.b = Buf(name)

    def __getitem__(self, k):
        return self.t[k]


def _b(x):
    return x.b if hasattr(x, "b") else x


class Sched:
    def __init__(self, nc, es, n_dma_sems=(12, 4, 12)):
        self.nc = nc
        self.engs = {"pe": nc.tensor, "dve": nc.vector, "act": nc.scalar, "pool": nc.gpsimd, "sp": nc.sync}
        self.ctr = {}
        for k in ("pe", "dve", "act", "pool"):
            self.ctr[k] = Ctr(es.enter_context(nc.semaphore("c_" + k)), 1)
        self.dq = {}
        for k, n in zip(("sp", "act", "pool"), n_dma_sems):
            self.dq[k] = [Ctr(es.enter_context(nc.semaphore(f"d_{k}{i}")), 16) for i in range(n)]
        self.dq_i = {"sp": 0, "act": 0, "pool": 0}
        self.bar_sem = es.enter_context(nc.semaphore("bar"))
        self.bar_n = 0
        self.waited = {k: {} for k in ("pe", "dve", "act", "pool", "sp")}
        self.n_ins = 0

    def _wait(self, ek, deps):
        best = {}
        for c, v in deps:
            if v > best.get(c, 0):
                best[c] = v
        for c, v in best.items():
            if self.waited[ek].get(c, 0) >= v:
                continue
            self.engs[ek].wait_ge(c.sem, v)
            self.waited[ek][c] = v
            self.n_ins += 1

    def _deps(self, r, w, skip=None):
        deps = []
        for b in r:
            if b.w is not None:
                deps.append(b.w)
        for b in w:
            if b.w is not None:
                deps.append(b.w)
            for c, v in b.r.items():
                deps.append((c, v))
        if skip is not None:
            deps = [(c, v) for c, v in deps if c is not skip]
        return deps

    def _mark(self, r, w, c, v):
        for b in w:
            b.w = (c, v)
            b.r = {}
        for b in r:
            if b.r.get(c, 0) < v:
                b.r[c] = v

    def op(self, ek, fn, r=(), w=(), same_ok=False):
        r = [_b(x) for x in r]
        w = [_b(x) for x in w]
        c = self.ctr[ek]
        self._wait(ek, self._deps(r, w, skip=c if same_ok else None))
        ins = fn(self.engs[ek])
        c.count += 1
        ins.then_inc(c.sem, 1)
        self.n_ins += 1
        self._mark(r, w, c, c.count)
        return ins

    def dma(self, qk, fn, r=(), w=()):
        r = [_b(x) for x in r]
        w = [_b(x) for x in w]
        lst = self.dq[qk]
        i = self.dq_i[qk]
        self.dq_i[qk] = (i + 1) % len(lst)
        c = lst[i]
        deps = self._deps(r, w)
        if c.count > 0:
            deps.append((c, c.count))
        self._wait(qk, deps)
        ins = fn(self.engs[qk])
        c.count += 16
        ins.then_inc(c.sem, 16)
        self.n_ins += 1
        self._mark(r, w, c, c.count)
        return ins

    def barrier(self):
        deps = []
        for k in ("pe", "dve", "act", "pool"):
            c = self.ctr[k]
            if c.count:
                deps.append((c, c.count))
        for k in self.dq:
            for c in self.dq[k]:
                if c.count:
                    deps.append((c, c.count))
        self._wait("sp", deps)
        self.bar_n += 1
        self.engs["sp"].sem_inc(self.bar_sem, 1)
        for k in ("pe", "dve", "act", "pool"):
            self.engs[k].wait_ge(self.bar_sem, self.bar_n)
            for c, v in deps:
                self.waited[k][c] = v
        for c, v in deps:
            self.waited["sp"][c] = v


class Rot:
    def __init__(self, tiles):
        self.tiles = tiles
        self.i = 0

    def next(self):
        t = self.tiles[self.i % len(self.tiles)]
        self.i += 1
        return t


def make_consts():
    c = {}
    pos = np.arange(SEQ, dtype=np.float32)
    ret_freqs = (1.0 / (np.float32(10000.0) ** np.linspace(0.0, 1.0, 128, dtype=np.float32))).astype(np.float32)
    ang = (pos[:, None] * ret_freqs[None, :]).astype(np.float32)
    c["ret_cos"] = np.ascontiguousarray(np.cos(ang).T).astype(np.float32)
    c["ret_sin"] = np.ascontiguousarray(np.sin(ang).T).astype(np.float32)
    log_g = np.log(1.0 - np.exp2(-5.0 - np.arange(4, dtype=np.float32))).astype(np.float32)
    ar = np.arange(128, dtype=np.float32)
    rel = ar[:, None] - ar[None, :]
    dm = np.where(rel[None] >= 0, np.exp(np.maximum(rel, 0.0)[None] * log_g[:, None, None]), 0.0)
    c["maskT"] = np.ascontiguousarray(dm.transpose(2, 0, 1)).astype(np.float32)
    qd = np.exp((ar + 1.0)[None] * log_g[:, None]).astype(np.float32)
    c["qdec"] = np.ascontiguousarray(np.broadcast_to(qd[None], (128, 4, 128))).astype(np.float32)
    kd = np.exp((128 - 1.0 - ar)[None] * log_g[:, None]).astype(np.float32)
    c["kdec"] = np.ascontiguousarray(kd.T).astype(np.float32)
    c["cdec"] = [float(x) for x in np.exp(128 * log_g)]
    dfreq = (np.float32(500000.0) ** (-np.arange(0, 16, 2, dtype=np.float32) / 16)).astype(np.float32)
    dang = (pos[:, None] * dfreq[None, :]).astype(np.float32)
    c["dcos"] = np.ascontiguousarray(np.cos(dang).reshape(16, 128, 8).transpose(1, 0, 2)).astype(np.float32)
    c["dsin"] = np.ascontiguousarray(np.sin(dang).reshape(16, 128, 8).transpose(1, 0, 2)).astype(np.float32)
    c["tri"] = (ar[None, :] >= ar[:, None]).astype(np.float32).astype(ml_dtypes.bfloat16)
    c["ident"] = np.eye(128, dtype=np.float32).astype(ml_dtypes.bfloat16)
    io = np.arange(16, dtype=np.float32)
    c["iota16"] = np.ascontiguousarray(np.broadcast_to(np.stack([io, io * 16.0])[None], (128, 2, 16))).astype(np.float32)
    return c


CONST_SPECS = {
    "ret_cos": ([128, SEQ], F32), "ret_sin": ([128, SEQ], F32), "maskT": ([128, 4, 128], F32),
    "qdec": ([128, 4, 128], F32), "kdec": ([128, 4], F32), "dcos": ([128, 16, 8], F32),
    "dsin": ([128, 16, 8], F32), "tri": ([128, 128], BF16), "ident": ([128, 128], BF16),
    "iota16": ([128, 2, 16], F32),
}


def build_program(nseq=2, n_steps=99, dbg=False):
    Tk = nseq * SEQ
    NT = Tk // 128
    NTB = Tk // 512
    nc = bass.Bass("TRN2", target_bir_lowering=False)
    cdec = make_consts()["cdec"]

    def din(name, shape, dt=F32):
        return nc.dram_tensor(name, shape, dt, kind="ExternalInput").ap()

    def dscr(name, shape, dt):
        return nc.dram_tensor(name, shape, dt, kind="ExternalOutput" if dbg else "Internal").ap()

    x_in = din("x", [Tk, D])
    ret_w_in = din("ret_w_in", [2, D, 6144])
    ret_w_out = din("ret_w_out", [2, 2048, D])
    kv_w = din("kv_w", [D, 2048])
    diff_w_q = din("diff_w_q", [2, D, D])
    diff_lambda = din("diff_lambda", [2, 256])
    diff_subln_g = din("diff_subln_g", [2, 128])
    diff_w_out = din("diff_w_out", [2, D, D])
    peer_w_q = din("peer_w_q", [4, D, 2048])
    peer_skT = din("peer_skT", [4, 128, 16, 128])
    peer_u = din("peer_u", [4, NEXP, D])
    peer_v = din("peer_v", [4, NEXP, D])
    peer_u_flat = peer_u.rearrange("l n d -> (l n) d")
    peer_v_flat = peer_v.rearrange("l n d -> (l n) d")
    ln_g = din("ln_g", [4, 2, D])
    ln_b = din("ln_b", [4, 2, D])
    cst = {k: din("c_" + k, shp, dt) for k, (shp, dt) in CONST_SPECS.items()}
    out = nc.dram_tensor("out", [Tk, D], F32, kind="ExternalOutput").ap()

    X = dscr("X", [Tk, D], F32)
    XT = dscr("XT", [D, Tk], BF16)
    QT = dscr("QT", [D, Tk], BF16)
    QDT = dscr("QDT", [D, Tk], BF16)
    KT = dscr("KT", [D, Tk], BF16)
    KD = dscr("KD", [Tk, D], BF16)
    V = dscr("V", [Tk, 2048], BF16)
    SG = dscr("SG", [Tk, 2048], BF16)
    Z = dscr("Z", [Tk, 2048], BF16)
    QP = dscr("QP", [2048, Tk], BF16)
    KTS = dscr("KTS", [D, Tk], BF16)
    VS = dscr("VS", [Tk, D], BF16)
    QTD = dscr("QTD", [D, Tk], BF16)
    OATT = dscr("OATT", [Tk, D], BF16)
    UVB = nc.dram_tensor("UVB", [4 * NEXP, 2048], BF16, kind="Internal").ap()

    def fm(ap):
        return ap.rearrange("(c p) t -> p c t", p=128)

    def tm(ap):
        return ap.rearrange("(n p) f -> p n f", p=128)

    with ExitStack() as es:
        S = Sched(nc, es)

        uid = {"n": 0}

        def sb(st, name, shape, dt):
            uid["n"] += 1
            name = f"{name}_{uid['n']}"
            return T(st.enter_context(nc.sbuf_tensor(name, shape, dt)), name)

        def ps(st, name, shape, dt):
            uid["n"] += 1
            name = f"{name}_{uid['n']}"
            return T(st.enter_context(nc.psum_tensor(name, shape, dt)), name)

        def rot(st, name, shape, dt, n, psum=False):
            mk = ps if psum else sb
            return Rot([mk(st, f"{name}{i}", shape, dt) for i in range(n)])

        ident = sb(es, "ident", [128, 128], BF16)
        S.dma("sp", lambda e: e.dma_start(out=ident[:], in_=cst["ident"]), w=[ident])

        flip = {"i": 0}

        def copy_any(out_ap, in_ap, r, w, engines=("act", "dve")):
            ek = engines[flip["i"] % len(engines)]
            flip["i"] += 1
            if ek == "act":
                S.op("act", lambda e: e.copy(out=out_ap, in_=in_ap), r=r, w=w)
            else:
                S.op(ek, lambda e: e.tensor_copy(out=out_ap, in_=in_ap), r=r, w=w)

        def load_xT_all(st):
            xT = sb(st, "xT_all", [128, 8, Tk], BF16)
            for c in range(8):
                S.dma("sp", lambda e, c=c: e.dma_start(out=xT[:, c, :], in_=fm(XT)[:, c, :]), w=[xT])
            return xT

        class WLoader:
            def __init__(self, st, kc, wmax, nbuf=2, name="wl"):
                self.kc = kc
                self.wf = rot(st, name + "f", [128, kc, wmax], F32, nbuf)
                self.wb = rot(st, name + "b", [128, kc, wmax], BF16, nbuf)

            def load(self, w_ap, w):
                wf = self.wf.next()
                wb = self.wb.next()
                src = w_ap.rearrange("(c p) n -> p c n", p=128)
                half = self.kc // 2
                S.dma("sp", lambda e: e.dma_start(out=wf[:, 0:half, 0:w], in_=src[:, 0:half, :]), w=[wf])
                S.dma("sp", lambda e: e.dma_start(out=wf[:, half:, 0:w], in_=src[:, half:, :]), w=[wf])
                S.op("pool", lambda e: e.tensor_copy(out=wb[:, :, 0:w], in_=wf[:, :, 0:w]), r=[wf], w=[wb])
                return wb

        class LNEpilogue:
            def __init__(self, st, l, which, ew="pool", npt=2, nbuf=2):
                self.ew = ew
                self.gb = sb(st, "ln_gbc", [128, D], F32)
                self.bb = sb(st, "ln_bbc", [128, D], F32)
                S.dma("sp", lambda e: e.dma_start(out=self.gb[:], in_=ln_g[l, which].partition_broadcast(128)), w=[self.gb])
                S.dma("sp", lambda e: e.dma_start(out=self.bb[:], in_=ln_b[l, which].partition_broadcast(128)), w=[self.bb])
                self.st = rot(st, "ln_st", [128, 2, 6], F32, 2)
                self.mv = rot(st, "ln_mv", [128, 4], F32, 2)
                self.xn = rot(st, "ln_xn", [128, D], F32, nbuf)
                self.xo = rot(st, "ln_xo", [128, D], F32, nbuf)
                self.xb = rot(st, "ln_xb", [128, D], BF16, nbuf)
                self.xT = rot(st, "ln_xT", [128, 8, 128], BF16, nbuf)
                self.pT = rot(st, "ln_pT", [128, 8, 128], BF16, npt, psum=True)

            def run(self, y, tt, x_dst):
                st_, mv, xn, xo, xb, xT, pT = (self.st.next(), self.mv.next(), self.xn.next(), self.xo.next(),
                                               self.xb.next(), self.xT.next(), self.pT.next())
                S.op("dve", lambda e: e.bn_stats(out=st_[:, 0, :], in_=y[:, 0:512]), r=[y], w=[st_])
                S.op("dve", lambda e: e.bn_stats(out=st_[:, 1, :], in_=y[:, 512:1024]), r=[y], w=[st_])
                S.op("dve", lambda e: e.bn_aggr(out=mv[:, 0:2], in_=st_[:].rearrange("p a b -> p (a b)")), r=[st_], w=[mv])
                S.op("dve", lambda e: e.tensor_scalar(out=mv[:, 2:3], in0=mv[:, 1:2], scalar1=1.0, scalar2=EPS,
                                                      op0=ALU.mult, op1=ALU.add), r=[mv], w=[mv])
                S.op("act", lambda e: e.activation(out=mv[:, 2:3], in_=mv[:, 2:3], func=AF.Sqrt), r=[mv], w=[mv])
                S.op("dve", lambda e: e.reciprocal(out=mv[:, 3:4], in_=mv[:, 2:3]), r=[mv], w=[mv])
                S.op("dve", lambda e: e.tensor_scalar(out=xn[:], in0=y[:], scalar1=mv[:, 0:1], scalar2=mv[:, 3:4],
                                                      op0=ALU.subtract, op1=ALU.mult), r=[y, mv], w=[xn])
                S.op(self.ew, lambda e: e.tensor_tensor(out=xn[:], in0=xn[:], in1=self.gb[:], op=ALU.mult), r=[xn, self.gb], w=[xn])
                S.op(self.ew, lambda e: e.tensor_tensor(out=xo[:], in0=xn[:], in1=self.bb[:], op=ALU.add), r=[xn, self.bb], w=[xo])
                sq = "pool" if self.ew == "pool" else "sp"
                S.dma(sq, lambda e: e.dma_start(out=x_dst[tt * 128:(tt + 1) * 128, :], in_=xo[:]), r=[xo])
                S.op("act", lambda e: e.copy(out=xb[:], in_=xo[:]), r=[xo], w=[xb])
                for c in range(8):
                    S.op("pe", lambda e, c=c: e.transpose(pT[:, c, :], xb[:, c * 128:(c + 1) * 128], ident[:]),
                         r=[xb, ident], w=[pT], same_ok=True)
                S.op("act", lambda e: e.copy(out=xT[:], in_=pT[:]), r=[pT], w=[xT])
                S.dma(sq, lambda e: e.dma_start(out=fm(XT)[:, :, tt * 128:(tt + 1) * 128], in_=xT[:]), r=[xT])

        def phase_init():
            with ExitStack() as st:
                xf = rot(st, "i_xf", [128, D], F32, 2)
                xb = rot(st, "i_xb", [128, D], BF16, 2)
                xT = rot(st, "i_xT", [128, 8, 128], BF16, 2)
                pT = rot(st, "i_pT", [128, 8, 128], BF16, 2, psum=True)
                for tt in range(NT):
                    a, b_, c_, p = xf.next(), xb.next(), xT.next(), pT.next()
                    S.dma("sp", lambda e: e.dma_start(out=a[:], in_=x_in[tt * 128:(tt + 1) * 128, :]), w=[a])
                    S.op("dve", lambda e: e.tensor_copy(out=b_[:], in_=a[:]), r=[a], w=[b_])
                    for c in range(8):
                        S.op("pe", lambda e, c=c: e.transpose(p[:, c, :], b_[:, c * 128:(c + 1) * 128], ident[:]),
                             r=[b_, ident], w=[p], same_ok=True)
                    S.op("act", lambda e: e.copy(out=c_[:], in_=p[:]), r=[p], w=[c_])
                    S.dma("sp", lambda e: e.dma_start(out=fm(XT)[:, :, tt * 128:(tt + 1) * 128], in_=c_[:]), r=[c_])
                S.barrier()

        def phase_ret_proj(l):
            with ExitStack() as st:
                xT = load_xT_all(st)
                cosT = sb(st, "r_cos", [128, SEQ], F32)
                sinT = sb(st, "r_sin", [128, SEQ], F32)
                qdec = sb(st, "r_qdec", [128, 4, 128], F32)
                kdec = sb(st, "r_kdec", [128, 4], F32)
                S.dma("sp", lambda e: e.dma_start(out=cosT[:], in_=cst["ret_cos"]), w=[cosT])
                S.dma("sp", lambda e: e.dma_start(out=sinT[:], in_=cst["ret_sin"]), w=[sinT])
                S.dma("sp", lambda e: e.dma_start(out=qdec[:], in_=cst["qdec"]), w=[qdec])
                S.dma("sp", lambda e: e.dma_start(out=kdec[:], in_=cst["kdec"]), w=[kdec])
                wl = WLoader(st, 8, 512, 2, "rw")
                p1r = rot(st, "r_p1", [128, 512], F32, 2, psum=True)
                p2r = rot(st, "r_p2", [128, 512], F32, 2, psum=True)
                pkr = rot(st, "r_pk", [128, 8, 128], BF16, 2, psum=True)
                a1r = rot(st, "r_a1", [128, 512], F32, 2)
                a2r = rot(st, "r_a2", [128, 512], F32, 2)
                tr = [rot(st, f"r_t{i}", [128, 512], F32, 2) for i in range(4)]
                o1r = rot(st, "r_o1", [128, 512], F32, 2)
                o2r = rot(st, "r_o2", [128, 512], F32, 2)
                obr = rot(st, "r_ob", [128, 2, 512], BF16, 3)
                odr = rot(st, "r_od", [128, 2, 512], BF16, 3)
                kdr = rot(st, "r_kd", [128, 4, 256], BF16, 2)
                for kind in range(2):
                    for h in range(4):
                        c0 = kind * 1024 + h * 256
                        wb = wl.load(ret_w_in[l, :, c0:c0 + 256], 256)
                        for tb in range(NTB):
                            p1, p2 = p1r.next(), p2r.next()
                            tsl = slice(tb * 512, (tb + 1) * 512)
                            for kc in range(8):
                                S.op("pe", lambda e, kc=kc: e.matmul(p1[:], lhsT=wb[:, kc, 0:128], rhs=xT[:, kc, tsl],
                                                                      start=(kc == 0), stop=(kc == 7)),
                                     r=[wb, xT], w=[p1], same_ok=True)
                            for kc in range(8):
                                S.op("pe", lambda e, kc=kc: e.matmul(p2[:], lhsT=wb[:, kc, 128:256], rhs=xT[:, kc, tsl],
                                                                      start=(kc == 0), stop=(kc == 7)),
                                     r=[wb, xT], w=[p2], same_ok=True)
                            a1, a2 = a1r.next(), a2r.next()
                            sc = 1.0 if kind == 0 else 1.0 / 16.0
                            S.op("act", lambda e: e.mul(out=a1[:], in_=p1[:], mul=sc), r=[p1], w=[a1])
                            S.op("act", lambda e: e.mul(out=a2[:], in_=p2[:], mul=sc), r=[p2], w=[a2])
                            p0 = (tb * 512) % SEQ
                            cs, sn = cosT[:, p0:p0 + 512], sinT[:, p0:p0 + 512]
                            t1, t2, t3, t4 = [r_.next() for r_ in tr]
                            S.op("dve", lambda e: e.tensor_tensor(out=t1[:], in0=a1[:], in1=cs, op=ALU.mult), r=[a1, cosT], w=[t1])
                            S.op("pool", lambda e: e.tensor_tensor(out=t2[:], in0=a2[:], in1=sn, op=ALU.mult), r=[a2, sinT], w=[t2])
                            S.op("dve", lambda e: e.tensor_tensor(out=t3[:], in0=a1[:], in1=sn, op=ALU.mult), r=[a1, sinT], w=[t3])
                            S.op("pool", lambda e: e.tensor_tensor(out=t4[:], in0=a2[:], in1=cs, op=ALU.mult), r=[a2, cosT], w=[t4])
                            ob = obr.next()
                            if kind == 0:
                                o1, o2, od = o1r.next(), o2r.next(), odr.next()
                                S.op("dve", lambda e: e.tensor_tensor(out=o1[:], in0=t1[:], in1=t2[:], op=ALU.subtract), r=[t1, t2], w=[o1])
                                S.op("pool", lambda e: e.tensor_tensor(out=o2[:], in0=t3[:], in1=t4[:], op=ALU.add), r=[t3, t4], w=[o2])
                                S.op("act", lambda e: e.copy(out=ob[:, 0, :], in_=o1[:]), r=[o1], w=[ob])
                                S.op("act", lambda e: e.copy(out=ob[:, 1, :], in_=o2[:]), r=[o2], w=[ob])
                                qd_b = qdec[:, h, :].unsqueeze(1).to_broadcast([128, 4, 128])
                                S.op("dve", lambda e: e.tensor_tensor(out=od[:, 0, :].rearrange("p (a b) -> p a b", a=4),
                                                                      in0=o1[:].rearrange("p (a b) -> p a b", a=4), in1=qd_b, op=ALU.mult),
                                     r=[o1, qdec], w=[od])
                                S.op("pool", lambda e: e.tensor_tensor(out=od[:, 1, :].rearrange("p (a b) -> p a b", a=4),
                                                                       in0=o2[:].rearrange("p (a b) -> p a b", a=4), in1=qd_b, op=ALU.mult),
                                     r=[o2, qdec], w=[od])
                                S.dma("sp", lambda e: e.dma_start(out=fm(QT)[:, 2 * h:2 * h + 2, tsl], in_=ob[:]), r=[ob])
                                S.dma("sp", lambda e: e.dma_start(out=fm(QDT)[:, 2 * h:2 * h + 2, tsl], in_=od[:]), r=[od])
                            else:
                                S.op("dve", lambda e: e.tensor_tensor(out=ob[:, 0, :], in0=t1[:], in1=t2[:], op=ALU.subtract), r=[t1, t2], w=[ob])
                                S.op("pool", lambda e: e.tensor_tensor(out=ob[:, 1, :], in0=t3[:], in1=t4[:], op=ALU.add), r=[t3, t4], w=[ob])
                                S.dma("sp", lambda e: e.dma_start(out=fm(KT)[:, 2 * h:2 * h + 2, tsl], in_=ob[:]), r=[ob])
                                pk, kd = pkr.next(), kdr.next()
                                for i in range(4):
                                    for j in range(2):
                                        S.op("pe", lambda e, i=i, j=j: e.transpose(pk[:, i * 2 + j, :], ob[:, j, i * 128:(i + 1) * 128], ident[:]),
                                             r=[ob, ident], w=[pk], same_ok=True)
                                S.op("act", lambda e: e.activation(out=kd[:].rearrange("p a b -> p (a b)"),
                                                                   in_=pk[:].rearrange("p a b -> p (a b)"), func=AF.Copy,
                                                                   scale=kdec[:, h:h + 1]), r=[pk, kdec], w=[kd])
                                S.dma("sp", lambda e: e.dma_start(out=tm(KD)[:, tb * 4:(tb + 1) * 4, h * 256:(h + 1) * 256], in_=kd[:]), r=[kd])
                pvr = rot(st, "r_pv", [128, 512], F32, 2, psum=True)
                ovr = rot(st, "r_ov", [128, 512], BF16, 3)
                for kind in range(2):
                    for cg in range(4):
                        c0 = 2048 + kind * 2048 + cg * 512
                        wb = wl.load(ret_w_in[l, :, c0:c0 + 512], 512)
                        dst = V if kind == 0 else SG
                        for tt in range(NT):
                            pv, ov = pvr.next(), ovr.next()
                            for kc in range(8):
                                S.op("pe", lambda e, kc=kc: e.matmul(pv[:], lhsT=xT[:, kc, tt * 128:(tt + 1) * 128], rhs=wb[:, kc, :],
                                                                      start=(kc == 0), stop=(kc == 7)),
                                     r=[wb, xT], w=[pv], same_ok=True)
                            if kind == 0:
                                copy_any(ov[:], pv[:], [pv], [ov])
                            else:
                                S.op("act", lambda e: e.activation(out=ov[:], in_=pv[:], func=AF.Silu), r=[pv], w=[ov])
                            S.dma("sp", lambda e: e.dma_start(out=dst[tt * 128:(tt + 1) * 128, cg * 512:(cg + 1) * 512], in_=ov[:]), r=[ov])
                S.barrier()

        def phase_ret_core():
            with ExitStack() as st:
                maskT = sb(st, "c_mask", [128, 4, 128], F32)
                S.dma("sp", lambda e: e.dma_start(out=maskT[:], in_=cst["maskT"]), w=[maskT])
                st_fs = [sb(st, f"c_stf{i}", [128, 2, 512], F32) for i in range(2)]
                st_bs = [sb(st, f"c_stb{i}", [128, 2, 512], BF16) for i in range(2)]
                qtr = rot(st, "c_qt", [128, 2, 128], BF16, 3)
                qdr = rot(st, "c_qd", [128, 2, 128], BF16, 3)
                ktr = rot(st, "c_kt", [128, 2, 128], BF16, 3)
                kdr = rot(st, "c_kd", [128, 256], BF16, 3)
                vr = rot(st, "c_v", [128, 512], BF16, 3)
                sgr = rot(st, "c_sg", [128, 512], BF16, 3)
                scmr = rot(st, "c_scm", [128, 128], BF16, 2)
                ynr = rot(st, "c_yn", [128, 512], F32, 2)
                zr = rot(st, "c_z", [128, 512], BF16, 2)
                str_ = rot(st, "c_st", [128, 6], F32, 2)
                mvr = rot(st, "c_mv", [128, 4], F32, 2)
                scp = rot(st, "c_scp", [128, 128], F32, 2, psum=True)
                ypr = rot(st, "c_yp", [128, 512], F32, 2, psum=True)
                upr = rot(st, "c_up", [128, 2, 512], F32, 2, psum=True)

                def loads(s, h, c):
                    tt = s * 16 + c
                    tsl = slice(tt * 128, (tt + 1) * 128)
                    qt, qd, kt, kd, v, sg = qtr.next(), qdr.next(), ktr.next(), kdr.next(), vr.next(), sgr.next()
                    S.dma("sp", lambda e: e.dma_start(out=qt[:], in_=fm(QT)[:, 2 * h:2 * h + 2, tsl]), w=[qt])
                    S.dma("sp", lambda e: e.dma_start(out=qd[:], in_=fm(QDT)[:, 2 * h:2 * h + 2, tsl]), w=[qd])
                    S.dma("sp", lambda e: e.dma_start(out=kt[:], in_=fm(KT)[:, 2 * h:2 * h + 2, tsl]), w=[kt])
                    S.dma("sp", lambda e: e.dma_start(out=kd[:], in_=KD[tsl, h * 256:(h + 1) * 256]), w=[kd])
                    S.dma("sp", lambda e: e.dma_start(out=v[:], in_=V[tsl, h * 512:(h + 1) * 512]), w=[v])
                    S.dma("sp", lambda e: e.dma_start(out=sg[:], in_=SG[tsl, h * 512:(h + 1) * 512]), w=[sg])
                    return qt, qd, kt, kd, v, sg

                items = [(s, h, c) for s in range(nseq) for hp in (0, 2) for c in range(16) for h in (hp, hp + 1)]
                nxt = loads(*items[0])
                for idx, (s, h, c) in enumerate(items):
                    st_f, st_b = st_fs[h % 2], st_bs[h % 2]
                    qt, qd, kt, kd, v, sg = nxt
                    if idx + 1 < len(items):
                        nxt = loads(*items[idx + 1])
                    tt = s * 16 + c
                    if c == 0:
                        S.op("pool", lambda e: e.memset(st_f[:], 0.0), w=[st_f])
                        S.op("pool", lambda e: e.memset(st_b[:], 0.0), w=[st_b])
                    sp_, yp, up = scp.next(), ypr.next(), upr.next()
                    for dc in range(2):
                        S.op("pe", lambda e, dc=dc: e.matmul(sp_[:], lhsT=kt[:, dc, :], rhs=qt[:, dc, :], start=(dc == 0), stop=(dc == 1)),
                             r=[kt, qt], w=[sp_], same_ok=True)
                    scm = scmr.next()
                    S.op("dve", lambda e: e.tensor_tensor(out=scm[:], in0=sp_[:], in1=maskT[:, h, :], op=ALU.mult), r=[sp_, maskT], w=[scm])
                    S.op("pe", lambda e: e.matmul(yp[:], lhsT=scm[:], rhs=v[:], start=True, stop=False), r=[scm, v], w=[yp], same_ok=True)
                    for dc in range(2):
                        S.op("pe", lambda e, dc=dc: e.matmul(yp[:], lhsT=qd[:, dc, :], rhs=st_b[:, dc, :], start=False, stop=(dc == 1)),
                             r=[qd, st_b], w=[yp], same_ok=True)
                    for dc in range(2):
                        S.op("pe", lambda e, dc=dc: e.matmul(up[:, dc, :], lhsT=kd[:, dc * 128:(dc + 1) * 128], rhs=v[:], start=True, stop=True),
                             r=[kd, v], w=[up], same_ok=True)
                    if c < 15:
                        S.op("dve", lambda e: e.scalar_tensor_tensor(out=st_f[:].rearrange("p a b -> p (a b)"),
                                                                      in0=st_f[:].rearrange("p a b -> p (a b)"), scalar=cdec[h],
                                                                      in1=up[:].rearrange("p a b -> p (a b)"), op0=ALU.mult, op1=ALU.add),
                             r=[st_f, up], w=[st_f])
                        S.op("act", lambda e: e.copy(out=st_b[:], in_=st_f[:]), r=[st_f], w=[st_b])
                    st_, mv, yn, z = str_.next(), mvr.next(), ynr.next(), zr.next()
                    S.op("dve", lambda e: e.bn_stats(out=st_[:], in_=yp[:]), r=[yp], w=[st_])
                    S.op("dve", lambda e: e.bn_aggr(out=mv[:, 0:2], in_=st_[:]), r=[st_], w=[mv])
                    S.op("dve", lambda e: e.tensor_scalar(out=mv[:, 2:3], in0=mv[:, 1:2], scalar1=1.0, scalar2=EPS,
                                                          op0=ALU.mult, op1=ALU.add), r=[mv], w=[mv])
                    S.op("act", lambda e: e.activation(out=mv[:, 2:3], in_=mv[:, 2:3], func=AF.Sqrt), r=[mv], w=[mv])
                    S.op("dve", lambda e: e.reciprocal(out=mv[:, 3:4], in_=mv[:, 2:3]), r=[mv], w=[mv])
                    S.op("dve", lambda e: e.tensor_scalar(out=yn[:], in0=yp[:], scalar1=mv[:, 0:1], scalar2=mv[:, 3:4],
                                                          op0=ALU.subtract, op1=ALU.mult), r=[yp, mv], w=[yn])
                    S.op("pool", lambda e: e.tensor_tensor(out=z[:], in0=yn[:], in1=sg[:], op=ALU.mult), r=[yn, sg], w=[z])
                    S.dma("pool", lambda e: e.dma_start(out=Z[tt * 128:(tt + 1) * 128, h * 512:(h + 1) * 512], in_=z[:]), r=[z])
                S.barrier()

        def phase_outproj_ln(Zsrc, F, w_ap, x_src, x_dst, l, which):
            FC = F // 128
            with ExitStack() as st:
                wsb = sb(st, "o_w", [128, FC, D], BF16)
                with ExitStack() as st2:
                    wl = WLoader(st2, 8, 512, 2, "ow")
                    for fg in range(FC // 8):
                        for hf in range(2):
                            wb = wl.load(w_ap[fg * 1024:(fg + 1) * 1024, hf * 512:(hf + 1) * 512], 512)
                            S.op("dve", lambda e: e.tensor_copy(out=wsb[:, fg * 8:(fg + 1) * 8, hf * 512:(hf + 1) * 512], in_=wb[:]),
                                 r=[wb], w=[wsb])
                    S.barrier()
                ln = LNEpilogue(st, l, which)
                zr = rot(st, "o_z", [128, F], BF16, 3)
                xr = rot(st, "o_x", [128, D], F32, 3)
                zTr = rot(st, "o_zT", [128, FC, 128], BF16, 2)
                yr = rot(st, "o_y", [128, D], F32, 2)
                pzr = rot(st, "o_pz", [128, FC, 128], BF16, 1, psum=True)
                pmr = rot(st, "o_pm", [128, 2, 512], F32, 2, psum=True)

                def loads(tt):
                    z, x = zr.next(), xr.next()
                    S.dma("sp", lambda e: e.dma_start(out=z[:], in_=Zsrc[tt * 128:(tt + 1) * 128, :]), w=[z])
                    S.dma("sp", lambda e: e.dma_start(out=x[:], in_=x_src[tt * 128:(tt + 1) * 128, :]), w=[x])
                    return z, x
                def front(tt, z, x):
                    pz, zT, pm, y = pzr.next(), zTr.next(), pmr.next(), yr.next()
                    for fc in range(FC):
                        S.op("pe", lambda e, fc=fc: e.transpose(pz[:, fc, :], z[:, fc * 128:(fc + 1) * 128], ident[:]),
                             r=[z, ident], w=[pz], same_ok=True)
                    for g8 in range(FC // 8):
                        copy_any(zT[:, g8 * 8:(g8 + 1) * 8, :], pz[:, g8 * 8:(g8 + 1) * 8, :], [pz], [zT])
                    for hf in range(2):
                        for fc in range(FC):
                            S.op("pe", lambda e, fc=fc, hf=hf: e.matmul(pm[:, hf, :], lhsT=zT[:, fc, :], rhs=wsb[:, fc, hf * 512:(hf + 1) * 512],
                                                                         start=(fc == 0), stop=(fc == FC - 1)),
                                 r=[zT, wsb], w=[pm], same_ok=True)
                    return (x, pm, y)

                def front_b(x, pm, y):
                    for hf in range(2):
                        S.op("dve", lambda e, hf=hf: e.scalar_tensor_tensor(out=y[:, hf * 512:(hf + 1) * 512], in0=x[:, hf * 512:(hf + 1) * 512],
                                                                             scalar=ALPHA, in1=pm[:, hf, :], op0=ALU.mult, op1=ALU.add),
                             r=[x, pm], w=[y])
                    return y

                ld = {0: loads(0)}
                if NT > 1:
                    ld[1] = loads(1)
                y_cur = front_b(*front(0, *ld.pop(0)))
                for tt in range(NT):
                    fa = None
                    if tt + 1 < NT:
                        zx = ld.pop(tt + 1)
                        if tt + 2 < NT:
                            ld[tt + 2] = loads(tt + 2)
                        fa = front(tt + 1, *zx)
                    ln.run(y_cur, tt, x_dst)
                    y_cur = front_b(*fa) if fa is not None else None
                S.barrier()

        def phase_uv_prep():
            R = 4
            NI = 4 * NEXP // (128 * R)
            uview = peer_u_flat.rearrange("(n p r) d -> n p r d", p=128, r=R)
            vview = peer_v_flat.rearrange("(n p r) d -> n p r d", p=128, r=R)
            oview = UVB.rearrange("(n p r) d -> n p r d", p=128, r=R)
            with ExitStack() as st:
                ufr = rot(st, "uv_uf", [128, R, D], F32, 3)
                vfr = rot(st, "uv_vf", [128, R, D], F32, 3)
                obr = rot(st, "uv_ob", [128, R, 2048], BF16, 3)

                def loads(n):
                    uf, vf = ufr.next(), vfr.next()
                    S.dma("sp", lambda e: e.dma_start(out=uf[:], in_=uview[n]), w=[uf])
                    S.dma("sp", lambda e: e.dma_start(out=vf[:], in_=vview[n]), w=[vf])
                    return uf, vf
                nxt = loads(0)
                for n in range(NI):
                    uf, vf = nxt
                    if n + 1 < NI:
                        nxt = loads(n + 1)
                    ob = obr.next()
                    S.op("dve", lambda e: e.tensor_copy(out=ob[:, :, 0:D], in_=uf[:]), r=[uf], w=[ob])
                    if n % 3 == 2:
                        S.op("pool", lambda e: e.tensor_copy(out=ob[:, :, D:2 * D], in_=vf[:]), r=[vf], w=[ob])
                    else:
                        S.op("act", lambda e: e.copy(out=ob[:, :, D:2 * D], in_=vf[:]), r=[vf], w=[ob])
                    S.dma("sp", lambda e: e.dma_start(out=oview[n], in_=ob[:]), r=[ob])
                S.barrier()

        def phase_peer_q(l):
            with ExitStack() as st:
                xT = load_xT_all(st)
                wl = WLoader(st, 8, 512, 2, "pw")
                ppr = rot(st, "p_pp", [128, 512], F32, 3, psum=True)
                obr = rot(st, "p_ob", [128, 512], BF16, 3)
                for g in range(4):
                    wb = wl.load(peer_w_q[l, :, g * 512:(g + 1) * 512], 512)
                    for j in range(4):
                        for tb in range(NTB):
                            pp, ob = ppr.next(), obr.next()
                            tsl = slice(tb * 512, (tb + 1) * 512)
                            for kc in range(8):
                                S.op("pe", lambda e, kc=kc: e.matmul(pp[:], lhsT=wb[:, kc, j * 128:(j + 1) * 128], rhs=xT[:, kc, tsl],
                                                                      start=(kc == 0), stop=(kc == 7)),
                                     r=[wb, xT], w=[pp], same_ok=True)
                            copy_any(ob[:], pp[:], [pp], [ob])
                            S.dma("sp", lambda e: e.dma_start(out=fm(QP)[:, g * 4 + j, tsl], in_=ob[:]), r=[ob])
                S.barrier()

        def phase_peer_main(l, x_src, x_dst):
            with ExitStack() as st:
                skb = sb(st, "m_skb", [128, 16, 128], BF16)
                with ExitStack() as st2:
                    skf = sb(st2, "m_skf", [128, 16, 128], F32)
                    S.dma("sp", lambda e: e.dma_start(out=skf[:], in_=peer_skT[l]), w=[skf])
                    S.op("dve", lambda e: e.tensor_copy(out=skb[:], in_=skf[:]), r=[skf], w=[skb])
                    S.barrier()
                io16 = sb(st, "m_io16", [128, 2, 16], F32)
                S.dma("sp", lambda e: e.dma_start(out=io16[:], in_=cst["iota16"]), w=[io16])
                ln = LNEpilogue(st, l, 1, ew="dve", npt=1, nbuf=1)
                xr = rot(st, "m_x", [128, D], F32, 3)
                qpr = rot(st, "m_qp", [128, 16, 128], BF16, 2)
                scps = ps(st, "m_scp", [128, 2, 512], F32)
                sc = sb(st, "m_sc", [128, 16, 128], F32)
                scr = rot(st, "m_scr", [128, 128], F32, 4)
                s16h = [Buf(f"s16h{i}") for i in range(16)]
                i16h = [Buf(f"i16h{i}") for i in range(16)]
                s16 = sb(st, "m_s16", [128, 16, 16], F32)
                i16u = sb(st, "m_i16u", [128, 16, 16], U32)
                i16f = sb(st, "m_i16f", [128, 16, 16], F32)
                candr = rot(st, "m_cand", [128, 16, 16], F32, 2)
                scr2 = rot(st, "m_scr2", [128, 256], F32, 2)
                tv = sb(st, "m_tv", [128, 8, 16], F32)
                posu = sb(st, "m_posu", [128, 8, 16], U32)
                posf = sb(st, "m_posf", [128, 8, 16], F32)
                posb = sb(st, "m_posb", [128, 8, 16], U32)
                bfm = sb(st, "m_bfm", [128, 8, 16], F32)
                a16 = sb(st, "m_a16", [128, 8, 16], F32)
                eq4 = rot(st, "m_eq4", [128, 8, 16, 16], F32, 1)
                sel1 = sb(st, "m_sel1", [128, 8, 16], F32)
                sel2 = sb(st, "m_sel2", [128, 8, 16], F32)
                idxf = sb(st, "m_idxf", [128, 128], F32)
                idxi = rot(st, "m_idxi", [128, 128], I32, 2)
                ex = sb(st, "m_ex", [128, 8, 16], F32)
                ssum = sb(st, "m_ssum", [128, 8], F32)
                gater = rot(st, "m_gate", [128, 8, 16], F32, 2)
                dots = sb(st, "m_dots", [128, 128], F32)
                wgt = sb(st, "m_wgt", [128, 128], F32)
                g1 = rot(st, "m_g1", [128, 16], F32, 2)
                g2 = rot(st, "m_g2", [128, 16], F32, 2)
                gbr = rot(st, "m_gb", [128, 2048], BF16, 24)
                junkb = rot(st, "m_junkb", [128, D], BF16, 4)
                junka = rot(st, "m_junka", [128, D], BF16, 2)
                dotsb = [Buf("dots_e"), Buf("dots_o")]
                xbr = rot(st, "m_xb", [128, D], BF16, 2)
                dgr = rot(st, "m_dg", [128, 8, 128], BF16, 3)
                accr = rot(st, "m_accp", [128, 2, 512], F32, 2, psum=True)
                g3 = rot(st, "m_g3", [128, 16], F32, 2)
                yr = rot(st, "m_y", [128, D], F32, 1)

                def select(tt, holder):
                    x, qp = xr.next(), qpr.next()
                    S.dma("sp", lambda e: e.dma_start(out=x[:], in_=x_src[tt * 128:(tt + 1) * 128, :]), w=[x])
                    S.dma("sp", lambda e: e.dma_start(out=qp[:], in_=fm(QP)[:, :, tt * 128:(tt + 1) * 128]), w=[qp])
                    for rnd in range(2):
                        for hq in range(8):
                            hp = rnd * 8 + hq
                            S.op("pe", lambda e, hp=hp, hq=hq: e.matmul(scps[:, hq // 4, (hq % 4) * 128:(hq % 4 + 1) * 128], lhsT=qp[:, hp, :], rhs=skb[:, hp, :],
                                                                         start=True, stop=True), r=[qp, skb], w=[scps], same_ok=True)
                        for bk in range(2):
                            S.op("act", lambda e, bk=bk, rnd=rnd: e.copy(out=sc[:, rnd * 8 + bk * 4:rnd * 8 + (bk + 1) * 4, :].rearrange("p a b -> p (a b)"),
                                                                          in_=scps[:, bk, :]), r=[scps], w=[sc])
                    xb = xbr.next()
                    S.op("act", lambda e: e.copy(out=xb[:], in_=x[:]), r=[x], w=[xb])
                    yield
                    for hp0 in range(0, 16, 2):
                        pr = [(hp0, scr.next()), (hp0 + 1, scr.next())]
                        for (hp, sr) in pr:
                            S.op("dve", lambda e: e.max(out=s16[:, hp, 0:8], in_=sc[:, hp, :]), r=[sc], w=[s16h[hp]])
                        for (hp, sr) in pr:
                            S.op("dve", lambda e: e.max_index(out=i16u[:, hp, 0:8], in_max=s16[:, hp, 0:8], in_values=sc[:, hp, :]), r=[sc, s16h[hp]], w=[i16h[hp]])
                        for (hp, sr) in pr:
                            S.op("dve", lambda e: e.match_replace(out=sr[:], in_to_replace=s16[:, hp, 0:8], in_values=sc[:, hp, :], imm_value=-1e30),
                                 r=[sc, s16h[hp]], w=[sr])
                        for (hp, sr) in pr:
                            S.op("dve", lambda e: e.max(out=s16[:, hp, 8:16], in_=sr[:]), r=[sr], w=[s16h[hp]])
                        for (hp, sr) in pr:
                            S.op("dve", lambda e: e.max_index(out=i16u[:, hp, 8:16], in_max=s16[:, hp, 8:16], in_values=sr[:]), r=[sr, s16h[hp]], w=[i16h[hp]])
                        if hp0 % 4 == 2:
                            yield
                    S.op("dve", lambda e: e.tensor_copy(out=i16f[:], in_=i16u[:]), r=i16h, w=[i16f])
                    i4 = i16f[:].rearrange("p (h two) k -> p h two k", two=2)
                    i1v, i2v = i4[:, :, 0, :], i4[:, :, 1, :]
                    S.op("dve", lambda e: e.tensor_scalar(out=i1v, in0=i1v, scalar1=128.0, scalar2=None, op0=ALU.mult), r=[i16f], w=[i16f])
                    for h in range(8):
                        cand, s2 = candr.next(), scr2.next()
                        a_b = lambda t_, hp_: t_[:, hp_, :].unsqueeze(2).to_broadcast([128, 16, 16])
                        b_b = lambda t_, hp_: t_[:, hp_, :].unsqueeze(1).to_broadcast([128, 16, 16])
                        S.op("dve", lambda e: e.tensor_tensor(out=cand[:], in0=a_b(s16, 2 * h), in1=b_b(s16, 2 * h + 1), op=ALU.add), r=[s16h[2 * h], s16h[2 * h + 1]], w=[cand])
                        cf = cand[:].rearrange("p a b -> p (a b)")
                        S.op("dve", lambda e: e.max(out=tv[:, h, 0:8], in_=cf), r=[cand], w=[tv])
                        S.op("dve", lambda e: e.max_index(out=posu[:, h, 0:8], in_max=tv[:, h, 0:8], in_values=cf), r=[cand, tv], w=[posu])
                        S.op("dve", lambda e: e.match_replace(out=s2[:], in_to_replace=tv[:, h, 0:8], in_values=cf, imm_value=-1e30), r=[cand, tv], w=[s2])
                        S.op("dve", lambda e: e.max(out=tv[:, h, 8:16], in_=s2[:]), r=[s2], w=[tv])
                        S.op("dve", lambda e: e.max_index(out=posu[:, h, 8:16], in_max=tv[:, h, 8:16], in_values=s2[:]), r=[s2, tv], w=[posu])
                        if h % 2 == 1:
                            yield
                    S.op("dve", lambda e: e.tensor_copy(out=posf[:], in_=posu[:]), r=[posu], w=[posf])
                    S.op("dve", lambda e: e.tensor_single_scalar(out=posb[:], in_=posu[:], scalar=15, op=ALU.bitwise_and), r=[posu], w=[posb])
                    S.op("dve", lambda e: e.tensor_copy(out=bfm[:], in_=posb[:]), r=[posb], w=[bfm])
                    S.op("dve", lambda e: e.tensor_tensor(out=a16[:], in0=posf[:], in1=bfm[:], op=ALU.subtract), r=[posf, bfm], w=[a16])
                    shp = [128, 8, 16, 16]
                    for (keyt, iot, valv, selo) in ((a16, 1, i1v, sel1), (bfm, 0, i2v, sel2)):
                        e4 = eq4.next()
                        S.op("dve", lambda e: e.tensor_tensor(out=e4[:], in0=keyt[:].unsqueeze(3).to_broadcast(shp),
                                                              in1=io16[:, iot, :].unsqueeze(1).unsqueeze(1).to_broadcast(shp), op=ALU.is_equal),
                             r=[keyt, io16], w=[e4])
                        S.op("dve", lambda e: e.tensor_tensor(out=e4[:], in0=e4[:], in1=valv.unsqueeze(2).to_broadcast(shp), op=ALU.mult),
                             r=[e4, i16f], w=[e4])
                        S.op("dve", lambda e: e.reduce_sum(out=selo[:].rearrange("p a b -> p (a b)"), in_=e4[:].rearrange("p a b c -> p (a b) c"), axis=AX.X),
                             r=[e4], w=[selo])
                        yield
                    S.op("dve", lambda e: e.tensor_tensor(out=idxf[:], in0=sel1[:].rearrange("p a b -> p (a b)"), in1=sel2[:].rearrange("p a b -> p (a b)"), op=ALU.add),
                         r=[sel1, sel2], w=[idxf])
                    gate = gater.next()
                    S.op("dve", lambda e: e.tensor_tensor(out=ex[:], in0=tv[:], in1=tv[:, :, 0:1].to_broadcast([128, 8, 16]), op=ALU.subtract), r=[tv], w=[ex])
                    S.op("act", lambda e: e.activation(out=ex[:], in_=ex[:], func=AF.Exp), r=[ex], w=[ex])
                    S.op("dve", lambda e: e.reduce_sum(out=ssum[:], in_=ex[:], axis=AX.X), r=[ex], w=[ssum])
                    S.op("dve", lambda e: e.reciprocal(out=ssum[:], in_=ssum[:]), r=[ssum], w=[ssum])
                    S.op("dve", lambda e: e.tensor_tensor(out=gate[:], in0=ex[:], in1=ssum[:].unsqueeze(2).to_broadcast([128, 8, 16]), op=ALU.mult),
                         r=[ex, ssum], w=[gate])
                    ii = idxi.next()
                    S.op("dve", lambda e: e.tensor_scalar(out=idxf[:], in0=idxf[:], scalar1=float(NEXP - 1), scalar2=float(l * NEXP), op0=ALU.min, op1=ALU.add),
                         r=[idxf], w=[idxf])
                    S.op("dve", lambda e: e.tensor_copy(out=ii[:], in_=idxf[:]), r=[idxf], w=[ii])
                    holder.update(dict(x=x, xb=xb, ii=ii, gate=gate))

                GS = 8
                NG = 128 // GS
                CG = 1.5957691216057308

                def dcol(sl):
                    return (sl % 2) * 64 + sl // 2

                def dv(g):
                    return dots[:].rearrange("p (two k) -> p two k", two=2)[:, :, g * 4:(g + 1) * 4]

                def sv(t_, g):
                    return t_.rearrange("p (k two) -> p two k", two=2)[:, :, g * 4:(g + 1) * 4]

                def t3(t_):
                    return t_[:, 0:GS].rearrange("p (two k) -> p two k", two=2)

                def stage_a(sel, g, j0, j1):
                    xb, ii = sel["xb"], sel["ii"]
                    gbs = sel["gbs"].setdefault(g, [])
                    for j in range(j0, j1):
                        sl = g * GS + j
                        gb, jb = gbr.next(), junkb.next()
                        gbs.append(gb)
                        S.dma("pool", lambda e, sl=sl: e.indirect_dma_start(out=gb[:], out_offset=None, in_=UVB,
                                                                            in_offset=bass.IndirectOffsetOnAxis(ap=ii[:, sl:sl + 1], axis=0)),
                              r=[ii], w=[gb])
                        if j % 2 == 0:
                            S.op("dve", lambda e, sl=sl: e.scalar_tensor_tensor(out=jb[:], in0=gb[:, 0:D], scalar=1.0, in1=xb[:], op0=ALU.mult, op1=ALU.mult,
                                                                                 accum_out=dots[:, dcol(sl):dcol(sl) + 1]), r=[gb, xb], w=[jb, dotsb[sl % 2]])
                        else:
                            S.op("dve", lambda e: e.tensor_tensor(out=jb[:], in0=gb[:, 0:D], in1=xb[:], op=ALU.mult), r=[gb, xb], w=[jb])
                            ja = junka.next()
                            S.op("act", lambda e, sl=sl: e.activation(out=ja[:], in_=jb[:], func=AF.Copy, accum_out=dots[:, dcol(sl):dcol(sl) + 1]),
                                 r=[jb], w=[ja, dotsb[sl % 2]])

                def stage_b1(sel, g):
                    hs = slice(g * GS, (g + 1) * GS)
                    a, b_ = g1.next(), g2.next()
                    sel["g12"] = (a, b_)
                    S.op("dve", lambda e: e.tensor_tensor(out=t3(a), in0=dv(g), in1=dv(g), op=ALU.mult), r=dotsb, w=[a])
                    S.op("dve", lambda e: e.tensor_scalar(out=a[:, 0:GS], in0=a[:, 0:GS], scalar1=0.044715, scalar2=1.0, op0=ALU.mult, op1=ALU.add), r=[a], w=[a])
                    S.op("dve", lambda e: e.tensor_tensor(out=t3(a), in0=t3(a), in1=dv(g), op=ALU.mult), r=[a] + dotsb, w=[a])
                    S.op("act", lambda e: e.activation(out=b_[:, 0:GS], in_=a[:, 0:GS], func=AF.Exp, scale=-CG), r=[a], w=[b_])

                def stage_b2(sel, g):
                    gate, accp = sel["gate"], sel["accp"]
                    gf = gate[:].rearrange("p a b -> p (a b)")
                    hs = slice(g * GS, (g + 1) * GS)
                    a, b_ = sel["g12"]
                    c_ = g3.next()
                    gbs = sel["gbs"][g]
                    S.op("dve", lambda e: e.tensor_scalar(out=b_[:, 0:GS], in0=b_[:, 0:GS], scalar1=1.0, scalar2=None, op0=ALU.add), r=[b_], w=[b_])
                    S.op("dve", lambda e: e.reciprocal(out=c_[:, 0:GS], in_=b_[:, 0:GS]), r=[b_], w=[c_])
                    S.op("dve", lambda e: e.tensor_tensor(out=t3(c_), in0=t3(c_), in1=dv(g), op=ALU.mult), r=[c_] + dotsb, w=[c_])
                    S.op("dve", lambda e: e.tensor_tensor(out=sv(wgt[:], g), in0=t3(c_), in1=sv(gf, g), op=ALU.mult), r=[c_, gate], w=[wgt])

                def stage_b2b(sel, g):
                    accp = sel["accp"]
                    hs = slice(g * GS, (g + 1) * GS)
                    gbs = sel["gbs"][g]
                    dg = dgr.next()
                    for j in range(GS):
                        sl = g * GS + j
                        S.op("act", lambda e, j=j, sl=sl: e.activation(out=dg[:, j, :], in_=ident[:], func=AF.Copy, scale=wgt[:, sl:sl + 1]),
                             r=[ident, wgt], w=[dg])
                    for j in range(GS):
                        sl = g * GS + j
                        for hf in range(2):
                            S.op("pe", lambda e, j=j, hf=hf: e.matmul(accp[:, hf, :], lhsT=dg[:, j, :], rhs=gbs[j][:, D + hf * 512:D + (hf + 1) * 512],
                                                                       start=(sl == 0), stop=(sl == 127)),
                                 r=[dg, gbs[j]], w=[accp], same_ok=True)
                    del sel["gbs"][g]

                def finish(sel, tt):
                    x, accp = sel["x"], sel["accp"]
                    y = yr.next()
                    for hf in range(2):
                        S.op("dve", lambda e, hf=hf: e.scalar_tensor_tensor(out=y[:, hf * 512:(hf + 1) * 512], in0=x[:, hf * 512:(hf + 1) * 512],
                                                                             scalar=ALPHA, in1=accp[:, hf, :], op0=ALU.mult, op1=ALU.add),
                             r=[x, accp], w=[y])
                    ln.run(y, tt, x_dst)

                cur = {}
                for _ in select(0, cur):
                    pass
                prev = None
                for tt in range(NT):
                    nxt = {}
                    gen = select(tt + 1, nxt) if tt + 1 < NT else None
                    cur["accp"] = accr.next()
                    cur["gbs"] = {}
                    pend = None
                    for g in range(NG):
                        stage_a(cur, g, 0, 2)
                        if pend is not None:
                            stage_b1(cur, pend)
                        stage_a(cur, g, 2, 5)
                        if pend is not None:
                            stage_b2(cur, pend)
                        stage_a(cur, g, 5, GS)
                        if pend is not None:
                            stage_b2b(cur, pend)
                        pend = g
                        if g == 1 and prev is not None:
                            finish(*prev)
                            prev = None
                        if gen is not None:
                            if next(gen, "done") == "done":
                                gen = None
                    stage_b1(cur, pend)
                    stage_b2(cur, pend)
                    stage_b2b(cur, pend)
                    if gen is not None:
                        for _ in gen:
                            pass
                    prev = (cur, tt)
                    cur = nxt
                finish(*prev)
                S.barrier()

        def phase_tok_proj_rope(w_ap, scale, dstT, v_ap=None):
            with ExitStack() as st:
                xT = load_xT_all(st)
                dcos = sb(st, "t_cos", [128, 16, 8], F32)
                dsin = sb(st, "t_sin", [128, 16, 8], F32)
                S.dma("sp", lambda e: e.dma_start(out=dcos[:], in_=cst["dcos"]), w=[dcos])
                S.dma("sp", lambda e: e.dma_start(out=dsin[:], in_=cst["dsin"]), w=[dsin])
                nW = 2 if v_ap is not None else 1
                wsb = sb(st, "t_w", [128, 8, 1024 * nW], BF16)
                with ExitStack() as st2:
                    wl = WLoader(st2, 8, 512, 2, "tw")
                    for wi, wa in enumerate([w_ap, v_ap][:nW]):
                        for hf in range(2):
                            wb = wl.load(wa[:, hf * 512:(hf + 1) * 512], 512)
                            S.op("dve", lambda e: e.tensor_copy(out=wsb[:, :, wi * 1024 + hf * 512:wi * 1024 + (hf + 1) * 512], in_=wb[:]),
                                 r=[wb], w=[wsb])
                    S.barrier()
                ppr = rot(st, "t_pp", [128, 2, 512], F32, 2, psum=True)
                pTr = rot(st, "t_pT", [128, 8, 128], BF16, 2, psum=True)
                kfr = rot(st, "t_kf", [128, 16, 64], F32, 2)
                tmr = [rot(st, f"t_tm{i}", [128, 16, 8], F32, 2) for i in range(4)]
                kbr = rot(st, "t_kb", [128, D], BF16, 2)
                kTr = rot(st, "t_kT", [128, 8, 128], BF16, 2)
                vbr = rot(st, "t_vb", [128, D], BF16, 2)
                def front(tt):
                    pp, kf, kb = ppr.next(), kfr.next(), kbr.next()
                    for hf in range(2):
                        for kc in range(8):
                            S.op("pe", lambda e, kc=kc, hf=hf: e.matmul(pp[:, hf, :], lhsT=xT[:, kc, tt * 128:(tt + 1) * 128],
                                                                         rhs=wsb[:, kc, hf * 512:(hf + 1) * 512], start=(kc == 0), stop=(kc == 7)),
                                 r=[xT, wsb], w=[pp], same_ok=True)
                    kff = kf[:].rearrange("p a b -> p (a b)")
                    for hf in range(2):
                        S.op("act", lambda e, hf=hf: e.mul(out=kff[:, hf * 512:(hf + 1) * 512], in_=pp[:, hf, :], mul=scale), r=[pp], w=[kf])
                    pt = tt % 16
                    cs = dcos[:, pt, :].unsqueeze(1).to_broadcast([128, 16, 8])
                    sn = dsin[:, pt, :].unsqueeze(1).to_broadcast([128, 16, 8])
                    t1, t2, t3, t4 = [r_.next() for r_ in tmr]
                    x1, x2 = kf[:, :, 0:8], kf[:, :, 8:16]
                    S.op("dve", lambda e: e.tensor_tensor(out=t1[:], in0=x1, in1=cs, op=ALU.mult), r=[kf, dcos], w=[t1])
                    S.op("dve", lambda e: e.tensor_tensor(out=t2[:], in0=x2, in1=sn, op=ALU.mult), r=[kf, dsin], w=[t2])
                    S.op("dve", lambda e: e.tensor_tensor(out=t3[:], in0=x1, in1=sn, op=ALU.mult), r=[kf, dsin], w=[t3])
                    S.op("dve", lambda e: e.tensor_tensor(out=t4[:], in0=x2, in1=cs, op=ALU.mult), r=[kf, dcos], w=[t4])
                    S.op("dve", lambda e: e.tensor_tensor(out=x1, in0=t1[:], in1=t2[:], op=ALU.subtract), r=[t1, t2], w=[kf])
                    S.op("dve", lambda e: e.tensor_tensor(out=x2, in0=t3[:], in1=t4[:], op=ALU.add), r=[t3, t4], w=[kf])
                    S.op("act", lambda e: e.copy(out=kb[:], in_=kff), r=[kf], w=[kb])
                    if v_ap is not None:
                        pv, vb = ppr.next(), vbr.next()
                        for hf in range(2):
                            for kc in range(8):
                                S.op("pe", lambda e, kc=kc, hf=hf: e.matmul(pv[:, hf, :], lhsT=xT[:, kc, tt * 128:(tt + 1) * 128],
                                                                             rhs=wsb[:, kc, 1024 + hf * 512:1024 + (hf + 1) * 512],
                                                                             start=(kc == 0), stop=(kc == 7)),
                                     r=[xT, wsb], w=[pv], same_ok=True)
                        for hf in range(2):
                            copy_any(vb[:, hf * 512:(hf + 1) * 512], pv[:, hf, :], [pv], [vb])
                        S.dma("sp", lambda e: e.dma_start(out=VS[tt * 128:(tt + 1) * 128, :], in_=vb[:]), r=[vb])

                    return kb

                def back(tt, kb):
                    pT, kT = pTr.next(), kTr.next()
                    for c in range(8):
                        S.op("pe", lambda e, c=c: e.transpose(pT[:, c, :], kb[:, c * 128:(c + 1) * 128], ident[:]), r=[kb, ident], w=[pT], same_ok=True)
                    S.op("dve", lambda e: e.tensor_copy(out=kT[:], in_=pT[:]), r=[pT], w=[kT])
                    S.dma("sp", lambda e: e.dma_start(out=fm(dstT)[:, :, tt * 128:(tt + 1) * 128], in_=kT[:]), r=[kT])

                kb_cur = front(0)
                for tt in range(NT):
                    kb_next = front(tt + 1) if tt + 1 < NT else None
                    back(tt, kb_cur)
                    kb_cur = kb_next
                S.barrier()

        def phase_attn(j, layer_idx):
            lam_init = 0.8 - 0.6 * math.exp(-0.3 * layer_idx)
            with ExitStack() as st:
                lamt = sb(st, "a_lamt", [128, 256], F32)
                lj = sb(st, "a_lj", [128, 64], F32)
                lv = sb(st, "a_lv", [128, 8], F32)
                gsub = sb(st, "a_gsub", [128, 128], F32)
                tri = sb(st, "a_tri", [128, 128], BF16)
                S.dma("sp", lambda e: e.dma_start(out=lamt[:], in_=diff_lambda[j].partition_broadcast(128)), w=[lamt])
                S.dma("sp", lambda e: e.dma_start(out=gsub[:], in_=diff_subln_g[j].partition_broadcast(128)), w=[gsub])
                S.dma("sp", lambda e: e.dma_start(out=tri[:], in_=cst["tri"]), w=[tri])
                S.op("dve", lambda e: e.scalar_tensor_tensor(out=lj[:], in0=lamt[:, 0:64], scalar=1.0, in1=lamt[:, 64:128], op0=ALU.mult, op1=ALU.mult,
                                                              accum_out=lv[:, 0:1]), r=[lamt], w=[lj, lv])
                S.op("dve", lambda e: e.scalar_tensor_tensor(out=lj[:], in0=lamt[:, 128:192], scalar=1.0, in1=lamt[:, 192:256], op0=ALU.mult, op1=ALU.mult,
                                                              accum_out=lv[:, 1:2]), r=[lamt], w=[lj, lv])
                S.op("act", lambda e: e.activation(out=lv[:, 2:4], in_=lv[:, 0:2], func=AF.Exp), r=[lv], w=[lv])
                S.op("dve", lambda e: e.tensor_tensor(out=lv[:, 4:5], in0=lv[:, 3:4], in1=lv[:, 2:3], op=ALU.subtract), r=[lv], w=[lv])
                S.op("dve", lambda e: e.tensor_scalar(out=lv[:, 5:6], in0=lv[:, 4:5], scalar1=-lam_init, scalar2=None, op0=ALU.add), r=[lv], w=[lv])
                S.op("dve", lambda e: e.tensor_scalar(out=gsub[:], in0=gsub[:], scalar1=(1.0 - lam_init), scalar2=None, op0=ALU.mult), r=[gsub], w=[gsub])
                neglam = lv[:, 5:6]

                kTr = rot(st, "a_kT", [128, SEQ], BF16, 2)
                qTr = rot(st, "a_qT", [128, SEQ], BF16, 2)
                vhr = rot(st, "a_vh", [128, 16, 129], BF16, 2)
                for t_ in vhr.tiles:
                    S.op("pool", lambda e, t_=t_: e.memset(t_[:, :, 128:129], 1.0), w=[t_])
                o1b = sb(st, "a_o1", [128, 16, 128], F32)
                oat = rot(st, "a_oat", [128, 16, 128], BF16, 2)
                Er = rot(st, "a_E", [128, 512], BF16, 3)
                tmpr = rot(st, "a_tmp", [128, 128], F32, 2)
                o2r = rot(st, "a_o2", [128, 128], F32, 2)
                jr = rot(st, "a_j", [128, 128], F32, 2)
                rsr = rot(st, "a_rs", [128, 4], F32, 4)
                spr = rot(st, "a_sp", [128, 512], F32, 3, psum=True)
                accp = ps(st, "a_acc", [128, 4, 512], F32)
                accb = [Buf(f"a_accb{i}") for i in range(4)]

                def loads(s, h):
                    kT, qT, vh = kTr.next(), qTr.next(), vhr.next()
                    S.dma("sp", lambda e: e.dma_start(out=kT[:], in_=fm(KTS)[:, h, s * SEQ:(s + 1) * SEQ]), w=[kT])
                    S.dma("sp", lambda e: e.dma_start(out=qT[:], in_=fm(QTD)[:, h, s * SEQ:(s + 1) * SEQ]), w=[qT])
                    S.dma("sp", lambda e: e.dma_start(out=vh[:, :, 0:128], in_=tm(VS)[:, s * 16:(s + 1) * 16, h * 128:(h + 1) * 128]), w=[vh])
                    return kT, qT, vh
                items = [(s, h) for s in range(nseq) for h in range(8)]
                nxt = loads(*items[0])
                for idx, (s, h) in enumerate(items):
                    kT, qT, vh = nxt
                    if idx + 1 < len(items):
                        nxt = loads(*items[idx + 1])
                    ot = oat.next()
                    for m in range(2):
                        msl = slice(m * 64, (m + 1) * 64)
                        for G in range(4):
                            nkt = 4 * G + 4

                            def score(kt):
                                qi0 = max(0, kt - 4 * G)
                                sp_ = spr.next()
                                qs = slice(G * 512 + qi0 * 128, (G + 1) * 512)
                                es_ = slice(qi0 * 128, 512)
                                S.op("pe", lambda e: e.matmul(sp_[:, es_], lhsT=kT[msl, kt * 128:(kt + 1) * 128], rhs=qT[msl, qs], start=True, stop=True),
                                     r=[kT, qT], w=[sp_], same_ok=True)
                                return sp_
                            sp_next = score(0)
                            for kt in range(nkt):
                                qi0 = max(0, kt - 4 * G)
                                sp_, E = sp_next, Er.next()
                                es_ = slice(qi0 * 128, 512)
                                if kt + 1 < nkt:
                                    sp_next = score(kt + 1)
                                S.op("act", lambda e: e.activation(out=E[:, es_], in_=sp_[:, es_], func=AF.Exp), r=[sp_], w=[E])
                                if kt >= 4 * G:
                                    ds_ = slice(qi0 * 128, (qi0 + 1) * 128)
                                    S.op("dve", lambda e: e.tensor_tensor(out=E[:, ds_], in0=E[:, ds_], in1=tri[:], op=ALU.mult), r=[E, tri], w=[E])
                                for qi in range(qi0, 4):
                                    S.op("pe", lambda e, qi=qi: e.matmul(accp[:, qi, 0:129], lhsT=E[:, qi * 128:(qi + 1) * 128], rhs=vh[:, kt, :],
                                                                          start=(kt == 0), stop=(kt == 4 * G + qi)),
                                         r=[E, vh], w=[accb[qi]], same_ok=True)
                            for qi in range(4):
                                qt_ = 4 * G + qi
                                rs = rsr.next()
                                S.op("dve", lambda e: e.reciprocal(out=rs[:, 0:1], in_=accp[:, qi, 128:129]), r=[accb[qi]], w=[rs])
                                if m == 0:
                                    S.op("dve", lambda e: e.tensor_scalar(out=o1b[:, qt_, :], in0=accp[:, qi, 0:128], scalar1=rs[:, 0:1], scalar2=None,
                                                                          op0=ALU.mult), r=[accb[qi], rs], w=[o1b])
                                else:
                                    tmp, o2, jj = tmpr.next(), o2r.next(), jr.next()
                                    S.op("dve", lambda e: e.tensor_scalar(out=tmp[:], in0=accp[:, qi, 0:128], scalar1=rs[:, 0:1], scalar2=None,
                                                                          op0=ALU.mult), r=[accb[qi], rs], w=[tmp])
                                    S.op("dve", lambda e: e.scalar_tensor_tensor(out=o2[:], in0=tmp[:], scalar=neglam, in1=o1b[:, qt_, :],
                                                                                  op0=ALU.mult, op1=ALU.add), r=[tmp, lv, o1b], w=[o2])
                                    S.op("dve", lambda e: e.scalar_tensor_tensor(out=jj[:], in0=o2[:], scalar=1.0, in1=o2[:], op0=ALU.mult, op1=ALU.mult,
                                                                                  accum_out=rs[:, 1:2]), r=[o2], w=[jj, rs])
                                    S.op("dve", lambda e: e.tensor_scalar(out=rs[:, 2:3], in0=rs[:, 1:2], scalar1=1.0 / 128.0, scalar2=EPS,
                                                                          op0=ALU.mult, op1=ALU.add), r=[rs], w=[rs])
                                    S.op("act", lambda e: e.activation(out=rs[:, 2:3], in_=rs[:, 2:3], func=AF.Sqrt), r=[rs], w=[rs])
                                    S.op("dve", lambda e: e.reciprocal(out=rs[:, 3:4], in_=rs[:, 2:3]), r=[rs], w=[rs])
                                    S.op("dve", lambda e: e.scalar_tensor_tensor(out=ot[:, qt_, :], in0=o2[:], scalar=rs[:, 3:4], in1=gsub[:],
                                                                                   op0=ALU.mult, op1=ALU.mult), r=[o2, rs, gsub], w=[ot])
                    S.dma("sp", lambda e: e.dma_start(out=tm(OATT)[:, s * 16:(s + 1) * 16, h * 128:(h + 1) * 128], in_=ot[:]), r=[ot])
                S.barrier()

        steps = []
        phase_init()
        phase_uv_prep()
        cur = x_in
        for l in range(DEPTH):
            last = (l == DEPTH - 1)
            if l < 2:
                steps.append(lambda l=l, cur=cur: (phase_ret_proj(l), phase_ret_core(),
                                                   phase_outproj_ln(Z, 2048, ret_w_out[l], cur, X, l, 0)))
            else:
                steps.append(lambda l=l, cur=cur: (phase_tok_proj_rope(diff_w_q[l - 2], 0.125, QTD), phase_attn(l - 2, l),
                                                   phase_outproj_ln(OATT, 1024, diff_w_out[l - 2], cur, X, l, 0)))
            cur = X
            steps.append(lambda l=l, last=last: (phase_peer_q(l), phase_peer_main(l, X, out if last else X)))
            if l == 1:
                steps.append(lambda: phase_tok_proj_rope(kv_w[:, 0:1024], 1.0, KTS, v_ap=kv_w[:, 1024:2048]))
        for i, f in enumerate(steps):
            if i < n_steps:
                f()
        S.barrier()
        print("instructions issued:", S.n_ins, {k: c.count for k, c in S.ctr.items()})
    return nc


_CACHE = {}


def _in_maps(inputs, nseq, n_cores):
    c = make_consts()
    x = np.asarray(inputs["x"], dtype=np.float32).reshape(-1, D)
    skT = np.ascontiguousarray(np.asarray(inputs["peer_subkeys"], dtype=np.float32).transpose(0, 4, 1, 2, 3).reshape(4, 128, 16, 128))
    shared = {
        "ret_w_in": np.asarray(inputs["ret_w_in"], np.float32), "ret_w_out": np.asarray(inputs["ret_w_out"], np.float32),
        "kv_w": np.asarray(inputs["kv_w"], np.float32), "diff_w_q": np.asarray(inputs["diff_w_q"], np.float32),
        "diff_lambda": np.asarray(inputs["diff_lambda"], np.float32).reshape(2, 256),
        "diff_subln_g": np.asarray(inputs["diff_subln_g"], np.float32), "diff_w_out": np.asarray(inputs["diff_w_out"], np.float32),
        "peer_w_q": np.asarray(inputs["peer_w_q"], np.float32), "peer_skT": skT,
        "peer_u": np.asarray(inputs["peer_u"], np.float32), "peer_v": np.asarray(inputs["peer_v"], np.float32),
        "ln_g": np.asarray(inputs["ln_g"], np.float32), "ln_b": np.asarray(inputs["ln_b"], np.float32),
    }
    for k in CONST_SPECS:
        shared["c_" + k] = c[k]
    maps = []
    tk = nseq * SEQ
    for i in range(n_cores):
        m = dict(shared)
        m["x"] = np.ascontiguousarray(x[i * tk:(i + 1) * tk])
        maps.append(m)
    return maps


def kernel(**inputs):
    nseq = 2
    if "nc" not in _CACHE:
        _CACHE["nc"] = build_program(nseq=nseq)
    nc = _CACHE["nc"]
    maps = _in_maps(inputs, nseq, N_CORES)
    res = run_bass_kernel_spmd(nc, maps, core_ids=list(range(N_CORES)))
    outs = [np.asarray(r["out"], dtype=np.float32) for r in res.results]
    return np.concatenate(outs, axis=0).reshape(16, SEQ, D)
```
